# Optimizing a Trainium2 kernel written in Bass

```python
import math
import jax, jax.numpy as jnp
from jax import lax
import numpy as np

D_MODEL = 1024
BATCH = 8
SEQ = 2048
DEPTH = 2

HEAD_DIM = 64
SB_HEADS = 8
SWA_Q_HEADS = 8
SWA_KV_HEADS = 2
SWA_GROUP = SWA_Q_HEADS // SWA_KV_HEADS
WINDOW = 128
BLOCK = 128
N_BUCKETS = 32
MAX_DISTANCE = 128
D_FF = 2816
EPS = 1e-6
NEG_INF = -1e30

SB_WIDTH = SB_HEADS * HEAD_DIM
SWA_WIDTH = SWA_Q_HEADS * HEAD_DIM
KV_WIDTH = SWA_KV_HEADS * HEAD_DIM
MIX_WIDTH = SB_WIDTH + SWA_WIDTH
IN_WIDTH = 3 * SB_WIDTH + SWA_WIDTH + 2 * KV_WIDTH
SPLITS = (SB_WIDTH, 2 * SB_WIDTH, 3 * SB_WIDTH,
          3 * SB_WIDTH + SWA_WIDTH, 3 * SB_WIDTH + SWA_WIDTH + KV_WIDTH)

kernel_name = "hymba_stickbreak_swa_sink_macaron"


def rms_norm(x, g):
    xf = x.astype(jnp.float32)
    y = xf * lax.rsqrt(jnp.mean(xf * xf, axis=-1, keepdims=True) + EPS)
    return (y * g.astype(jnp.float32)).astype(x.dtype)


def swiglu(x, w_gu, w_down):
    gate, up = jnp.split(x @ w_gu, 2, axis=-1)
    return (jax.nn.silu(gate) * up) @ w_down


def t5_causal_bucket(dist):
    max_exact = N_BUCKETS // 2
    d = jnp.maximum(dist, 1).astype(jnp.float32)
    large = max_exact + (jnp.log(d / max_exact) / math.log(MAX_DISTANCE / max_exact)
                         * (N_BUCKETS - max_exact)).astype(jnp.int32)
    large = jnp.minimum(large, N_BUCKETS - 1)
    return jnp.where(dist < max_exact, dist, large)


def stick_breaking_attention(q, k, v):
    B, S = q.shape[:2]
    nblk = S // BLOCK
    outs = []
    for i in range(nblk):
        L = (i + 1) * BLOCK
        qi = q[:, i * BLOCK:L]
        z = jnp.einsum('bqhd,bkhd->bhqk', qi, k[:, :L],
                       preferred_element_type=jnp.float32) * (HEAD_DIM ** -0.5)
        t_pos = i * BLOCK + jnp.arange(BLOCK)[:, None]
        s_pos = jnp.arange(L)[None, :]
        causal = s_pos < t_pos
        neg_log_keep = jnp.where(causal, jax.nn.softplus(z), 0.0)
        suffix = lax.cumsum(neg_log_keep, axis=3, reverse=True) - neg_log_keep
        log_w = jax.nn.log_sigmoid(z) - suffix
        w = jnp.where(causal, jnp.exp(log_w), 0.0).astype(v.dtype)
        outs.append(jnp.einsum('bhqk,bkhd->bqhd', w, v[:, :L]))
    return jnp.concatenate(outs, axis=1).reshape(B, S, SB_WIDTH)


def sliding_window_sink_attention(q, k, v, sinks, rel_bias):
    B, S = q.shape[:2]
    nblk = S // BLOCK
    qb = q.reshape(B, nblk, BLOCK, SWA_KV_HEADS, SWA_GROUP, HEAD_DIM)

    def band(t):
        tp = jnp.pad(t, ((0, 0), (BLOCK, 0), (0, 0), (0, 0)))
        prev = tp[:, :S].reshape(B, nblk, BLOCK, SWA_KV_HEADS, HEAD_DIM)
        cur = t.reshape(B, nblk, BLOCK, SWA_KV_HEADS, HEAD_DIM)
        return jnp.concatenate([prev, cur], axis=2)

    kb, vb = band(k), band(v)
    scores = jnp.einsum('bnqhgd,bnkhd->bnhgqk', qb, kb,
                        preferred_element_type=jnp.float32) * (HEAD_DIM ** -0.5)
    a = jnp.arange(BLOCK)[:, None]
    c = jnp.arange(2 * BLOCK)[None, :]
    dist = BLOCK + a - c
    bias = rel_bias.astype(jnp.float32)[t5_causal_bucket(jnp.maximum(dist, 0))]
    bias = bias.transpose(2, 0, 1).reshape(SWA_KV_HEADS, SWA_GROUP, BLOCK, 2 * BLOCK)
    in_band = (dist >= 0) & (dist < WINDOW)
    key_exists = (jnp.arange(nblk)[:, None] > 0) | (c >= BLOCK)
    mask = in_band[None] & key_exists[:, None, :]
    scores = jnp.where(mask[None, :, None, None], scores + bias, NEG_INF)
    sink = sinks.astype(jnp.float32).reshape(SWA_KV_HEADS, SWA_GROUP)[None, None, :, :, None, None]
    m = jnp.maximum(jnp.max(scores, axis=-1, keepdims=True), sink)
    p = jnp.exp(scores - m)
    p = (p / (jnp.sum(p, axis=-1, keepdims=True) + jnp.exp(sink - m))).astype(v.dtype)
    out = jnp.einsum('bnhgqk,bnkhd->bnqhgd', p, vb)
    return out.reshape(B, S, SWA_WIDTH)


def setup_inputs(seed: int = 0) -> dict:
    key = jax.random.key(seed)
    ks = jax.random.split(key, 16)
    f32 = jnp.float32

    def w(k, shape, fan_in):
        return jax.random.normal(k, shape, f32) * (fan_in ** -0.5)

    def gain(k, shape):
        return 1.0 + 0.02 * jax.random.normal(k, shape, f32)

    return {
        "x": jax.random.normal(ks[0], (BATCH, SEQ, D_MODEL), f32),
        "norm_ffn1": gain(ks[1], (DEPTH, D_MODEL)),
        "w_ffn1_gu": w(ks[2], (DEPTH, D_MODEL, 2 * D_FF), D_MODEL),
        "w_ffn1_down": w(ks[3], (DEPTH, D_FF, D_MODEL), D_FF),
        "norm_mix": gain(ks[4], (DEPTH, D_MODEL)),
        "w_in": w(ks[5], (DEPTH, D_MODEL, IN_WIDTH), D_MODEL),
        "sinks": 0.5 * jax.random.normal(ks[6], (DEPTH, SWA_Q_HEADS), f32),
        "norm_out_sb": gain(ks[7], (DEPTH, SB_WIDTH)),
        "norm_out_swa": gain(ks[8], (DEPTH, SWA_WIDTH)),
        "w_out": w(ks[9], (DEPTH, MIX_WIDTH, D_MODEL), MIX_WIDTH),
        "norm_ffn2": gain(ks[10], (DEPTH, D_MODEL)),
        "w_ffn2_gu": w(ks[11], (DEPTH, D_MODEL, 2 * D_FF), D_MODEL),
        "w_ffn2_down": w(ks[12], (DEPTH, D_FF, D_MODEL), D_FF),
        "rel_bias": 0.5 * jax.random.normal(ks[13], (N_BUCKETS, SWA_Q_HEADS), f32),
        "norm_final": gain(ks[14], (D_MODEL,)),
    }


def reference(x, norm_ffn1, w_ffn1_gu, w_ffn1_down, norm_mix, w_in, sinks,
              norm_out_sb, norm_out_swa, w_out, norm_ffn2, w_ffn2_gu, w_ffn2_down,
              rel_bias, norm_final):
    B, S, _ = x.shape
    h = x
    for l in range(DEPTH):
        h = h + 0.5 * swiglu(rms_norm(h, norm_ffn1[l]), w_ffn1_gu[l], w_ffn1_down[l])
        n = rms_norm(h, norm_mix[l])
        proj = n @ w_in[l]
        q_sb, k_sb, v_sb, q_sw, k_sw, v_sw = jnp.split(proj, SPLITS, axis=-1)
        o_sb = stick_breaking_attention(
            q_sb.reshape(B, S, SB_HEADS, HEAD_DIM),
            k_sb.reshape(B, S, SB_HEADS, HEAD_DIM),
            v_sb.reshape(B, S, SB_HEADS, HEAD_DIM))
        o_sw = sliding_window_sink_attention(
            q_sw.reshape(B, S, SWA_Q_HEADS, HEAD_DIM),
            k_sw.reshape(B, S, SWA_KV_HEADS, HEAD_DIM),
            v_sw.reshape(B, S, SWA_KV_HEADS, HEAD_DIM),
            sinks[l], rel_bias)
        mixed = jnp.concatenate([rms_norm(o_sb, norm_out_sb[l]),
                                 rms_norm(o_sw, norm_out_swa[l])], axis=-1)
        h = h + mixed @ w_out[l]
        h = h + 0.5 * swiglu(rms_norm(h, norm_ffn2[l]), w_ffn2_gu[l], w_ffn2_down[l])
    return rms_norm(h, norm_final)
```

```python
import numpy as np
from contextlib import ExitStack

import concourse.bass as bass
import concourse.mybir as mybir
from concourse.bass_utils import run_bass_kernel_spmd

F32 = mybir.dt.float32
BF16 = mybir.dt.bfloat16
AF = mybir.ActivationFunctionType
ALU = mybir.AluOpType

D = 1024
S = 2048
DFF = 2816
NFC = DFF // 128
NKC = D // 128
NTG = S // 512
NL = 2
EPS = 1e-6
NEG = -30000.0
FGROUPS = [(0, 4), (4, 4), (8, 4), (12, 4), (16, 4), (20, 2)]

G_PER_L = 36
G_FFN1, G_MIX, G_OSB, G_OSW, G_FFN2, G_SINK = 0, 8, 16, 20, 24, 32
G_FINAL = NL * G_PER_L
NGC = G_FINAL + 8

C_ONES, C_IDENT, C_NEGTRI, C_NEGMASK, C_SPMASK, C_NEGONES, C_ZEROS, C_ONESPAD, C_SPMASK2 = 0, 128, 256, 384, 512, 640, 768, 896, 1152
NCST = 1408

WIN_SB = 8 * 384
WIN_SW = 8 * 448
WIN_TOT = 4 * WIN_SB + 2 * WIN_SW


class Sched:
    ENGS = ["pe", "act", "dve", "pool", "sp"]

    def __init__(self):
        self.ops = []
        self.last_w = {}
        self.readers = {}
        self.dma_count = {}
        self.bar = set()
        self.last_eng = {}
        self.last_tag = {}

    def add(self, eng, fn, reads=(), writes=(), tag=None):
        i = len(self.ops)
        deps = set(self.bar)
        for r in reads:
            if r in self.last_w:
                deps.add(self.last_w[r])
        for w in writes:
            if w in self.last_w:
                deps.add(self.last_w[w])
            deps.update(self.readers.get(w, ()))
        op = dict(eng=eng, fn=fn, deps=deps, tag=tag, cnt=None)
        if tag is not None:
            self.dma_count[tag] = self.dma_count.get(tag, 0) + 1
            op["dma_n"] = self.dma_count[tag]
            self.last_tag[tag] = i
        else:
            self.last_eng[eng] = i
        self.ops.append(op)
        for r in reads:
            self.readers.setdefault(r, []).append(i)
        for w in writes:
            self.last_w[w] = i
            self.readers[w] = []
        return i

    def barrier(self):
        self.bar = set(self.last_eng.values()) | set(self.last_tag.values())

    def emit(self, nc):
        ops = self.ops
        for op in ops:
            op["deps"] = {
                d for d in op["deps"]
                if not (ops[d]["eng"] == "pe" and op["eng"] == "pe"
                        and ops[d]["tag"] is None and op["tag"] is None)
            }
        needed = set()
        for op in ops:
            for d in op["deps"]:
                if ops[d]["tag"] is None:
                    needed.add(d)
        cnt = {e: 0 for e in self.ENGS}
        for i, op in enumerate(ops):
            if op["tag"] is None and i in needed:
                cnt[op["eng"]] += 1
                op["cnt"] = cnt[op["eng"]]
        tags = sorted(self.dma_count.keys())
        with ExitStack() as es:
            esem = {e: es.enter_context(nc.semaphore("s_" + e)) for e in self.ENGS}
            tsem = {t: es.enter_context(nc.semaphore("d_" + str(t))) for t in tags}
            block = es.enter_context(nc.Block())

            def run(eng_name, eng):
                seen = {}
                for op in ops:
                    if op["eng"] != eng_name:
                        continue
                    waits = {}
                    for d in op["deps"]:
                        dop = ops[d]
                        if dop["tag"] is not None:
                            key, val, sem = ("t", dop["tag"]), 16 * dop["dma_n"], tsem[dop["tag"]]
                        else:
                            key, val, sem = ("e", dop["eng"]), dop["cnt"], esem[dop["eng"]]
                        if seen.get(key, 0) < val and waits.get(key, (0, None))[0] < val:
                            waits[key] = (val, sem)
                    for key, (val, sem) in waits.items():
                        eng.wait_ge(sem, val)
                        seen[key] = val
                    ins = op["fn"](eng)
                    if op["tag"] is not None:
                        ins.then_inc(tsem[op["tag"]], 16)
                    elif op["cnt"] is not None:
                        ins.then_inc(esem[eng_name], 1)
                last = {}
                for op in ops:
                    if op["eng"] == eng_name and op["tag"] is not None:
                        last[op["tag"]] = max(last.get(op["tag"], 0), 16 * op["dma_n"])
                for t, v in last.items():
                    if seen.get(("t", t), 0) < v:
                        eng.wait_ge(tsem[t], v)

            @block.tensor
            def _(e):
                run("pe", e)

            @block.scalar
            def _(e):
                run("act", e)

            @block.vector
            def _(e):
                run("dve", e)

            @block.gpsimd
            def _(e):
                run("pool", e)

            @block.sync
            def _(e):
                run("sp", e)


def _interleave(*gens):
    gens = list(gens)
    while gens:
        for g in list(gens):
            try:
                next(g)
            except StopIteration:
                gens.remove(g)


def _pipeline(tiles, stages, lags):
    n = len(tiles)
    mx = max(lags)
    for s in range(n + mx):
        for st, lag in zip(stages, lags):
            i = s - lag
            if 0 <= i < n:
                st(tiles[i])
        yield


def build_nc(cfg):
    nlayers = cfg.get("layers", NL)
    stages = cfg.get("stages", ("ffn1", "mix", "ffn2"))
    final_norm = cfg.get("final_norm", True)

    nc = bass.Bass("TRN2", target_bir_lowering=False)
    xT_d = nc.dram_tensor("xT", [128, NKC, S], F32, kind="ExternalInput").ap()
    outT_d = nc.dram_tensor("outT", [128, NKC, S], F32, kind="ExternalOutput").ap()
    gains_d = nc.dram_tensor("gains", [128, NGC], F32, kind="ExternalInput").ap()
    cst_d = nc.dram_tensor("cst", [128, NCST], F32, kind="ExternalInput").ap()
    bm_d = nc.dram_tensor("bm", [128, 8 * 256], F32, kind="ExternalInput").ap()
    wgu_d, wd_d, win_d, wout_d = {}, {}, {}, {}
    for l in range(NL):
        for f in (1, 2):
            wgu_d[(l, f)] = nc.dram_tensor("wgu%d_%d" % (f, l), [128, NFC * 2048], F32, kind="ExternalInput").ap()
            wd_d[(l, f)] = nc.dram_tensor("wd%d_%d" % (f, l), [128, NFC * 1024], F32, kind="ExternalInput").ap()
        win_d[l] = nc.dram_tensor("win_%d" % l, [128, WIN_TOT], F32, kind="ExternalInput").ap()
        wout_d[l] = nc.dram_tensor("wout_%d" % l, [128, 8 * 1024], F32, kind="ExternalInput").ap()

    es = ExitStack()
    with es:
        hT = es.enter_context(nc.sbuf_tensor("hT", [128, NKC, S], F32))
        nT = es.enter_context(nc.sbuf_tensor("nT", [128, NKC, S], BF16))
        WA = es.enter_context(nc.sbuf_tensor("WA", [128, 8192], BF16))
        cst = es.enter_context(nc.sbuf_tensor("cst_sb", [128, NCST], BF16))
        bm = es.enter_context(nc.sbuf_tensor("bm_sb", [128, 8, 256], BF16))
        gains = es.enter_context(nc.sbuf_tensor("gains_sb", [128, NGC], F32))
        sinkexp = es.enter_context(nc.sbuf_tensor("sinkexp", [128, NL * 4], F32))
        SCR = es.enter_context(nc.sbuf_tensor("SCR", [128, 45056], BF16))
        PSALL = es.enter_context(nc.psum_tensor("psall", [128, 8, 512], F32))
        PS = [PSALL[:, i, :] for i in range(8)]

        def scr(off, n):
            return SCR[:, off:off + n]

        oT = scr(0, 16384).rearrange("p (c t) -> p c t", c=8)
        qbuf = scr(16384, 4096).rearrange("p (c t) -> p c t", c=2)
        kz = scr(20480, 4096).rearrange("p (c t) -> p c t", c=2)
        vpad = scr(24576, 4096).rearrange("p (t v c) -> p t v c", t=16, v=2)
        spr = scr(28672, 3072).rearrange("p (s h t) -> p s h t", s=3, h=2)
        ssum = scr(31744, 4096).rearrange("p (s h t) -> p s h t", s=4, h=2)
        ebuf = scr(35840, 2048).rearrange("p (s h t) -> p s h t", s=2, h=2)
        wbuf = scr(37888, 3072).rearrange("p (s h t) -> p s h t", s=3, h=2)
        pbuf = scr(28672, 5120).rearrange("p (u h b t) -> p u h b t", u=2, h=2, b=5)
        lnden = scr(33792, 1024).bitcast(F32)
        rden = scr(34816, 1024).bitcast(F32)
        actb = scr(0, 16384).rearrange("p (g f t) -> p g f t", g=2, f=4)
        sgb = scr(16384, 4096).bitcast(F32).rearrange("p (s t) -> p s t", s=4)
        WB = scr(20480, 4096).rearrange("p (s t) -> p s t", s=4)
        sqb = scr(40960, 2048).rearrange("p (s t) -> p s t", s=4)
        lnv = scr(43008, 2048).bitcast(F32).rearrange("p (s t) -> p s t", s=2)

        ones = cst[:, C_ONES:C_ONES + 128]
        ident = cst[:, C_IDENT:C_IDENT + 128]
        negtri = cst[:, C_NEGTRI:C_NEGTRI + 128]
        negmask = cst[:, C_NEGMASK:C_NEGMASK + 128]
        spmask = cst[:, C_SPMASK:C_SPMASK + 128]
        onespad = [cst[:, C_ONESPAD + 128 * i:C_ONESPAD + 128 * (i + 1)] for i in range(2)]

        negones = cst[:, C_NEGONES:C_NEGONES + 128]
        spmask2 = cst[:, C_SPMASK2:C_SPMASK2 + 256].rearrange("p (h t) -> p h t", h=2)
        zeros = cst[:, C_ZEROS:C_ZEROS + 128]

        Sx = Sched()
        add = Sx.add
        st = dict(sq=0, rs=0, pb=0)

        def PSK(b):
            return ("ps", b)

        add("sp", lambda e: e.dma_start(out=gains[:, :], in_=gains_d[:, :]), writes=["gains"], tag="gains")
        add("pool", lambda e: e.dma_start(out=cst[:, :], in_=cst_d[:, :], max_dma_last_dim=4096), writes=["cst"], tag="cst")
        add("pool", lambda e: e.dma_start(out=bm[:, :, :], in_=bm_d.rearrange("p (h t) -> p h t", h=8), max_dma_last_dim=4096),
            writes=["bm"], tag="bm")
        for kc in range(NKC):
            add("sp", (lambda kc: lambda e: e.dma_start(out=hT[:, kc, :], in_=xT_d[:, kc, :]))(kc),
                writes=[("h", kc, tg) for tg in range(NTG)], tag="h%d" % kc)
        for l in range(NL):
            add("act", (lambda l: lambda e: e.activation(
                out=sinkexp[:, 4 * l:4 * l + 4], in_=gains[:, l * G_PER_L + G_SINK:l * G_PER_L + G_SINK + 4], func=AF.Exp))(l),
                reads=["gains"], writes=["sinkexp%d" % l])

        def rmsnorm(src, nk, gcol, dn, dst):
            for tg in range(NTG):
                bank = 4 + tg
                for kc in range(nk):
                    sl = st["sq"] % 4
                    st["sq"] += 1
                    sap, skey = src(kc, tg)
                    if kc % 2 == 0:
                        add("act", (lambda sap, sl: lambda e: e.activation(out=sqb[:, sl, :], in_=sap, func=AF.Square))(sap, sl),
                            reads=[skey], writes=[("sq", sl)])
                    else:
                        add("pool", (lambda sap, sl: lambda e: e.tensor_tensor(out=sqb[:, sl, :], in0=sap, in1=sap,
                                                                              op=ALU.mult))(sap, sl),
                            reads=[skey], writes=[("sq", sl)])
                    add("pe", (lambda sl, kc, bank: lambda e: e.matmul(PS[bank][:, :], lhsT=ones, rhs=sqb[:, sl, :],
                                                                         start=(kc == 0), stop=(kc == nk - 1)))(sl, kc, bank),
                        reads=[("sq", sl), "cst"], writes=[PSK(bank)])
                add("act", (lambda bank, tg: lambda e: e.activation(out=lnv[:, tg % 2, :], in_=PS[bank][:, :], func=AF.Ln,
                                                                    scale=1.0 / dn, bias=EPS))(bank, tg),
                    writes=[PSK(bank), ("lnv", tg % 2)])
                add("act", (lambda bank, tg: lambda e: e.activation(out=PS[bank][:, :], in_=lnv[:, tg % 2, :], func=AF.Exp,
                                                                    scale=-0.5))(bank, tg),
                    reads=[("lnv", tg % 2)], writes=[PSK(bank)])
            for kc in range(nk):
                for tg in range(NTG):
                    sap, skey = src(kc, tg)
                    dap, dkey = dst(kc, tg)
                    add("dve", (lambda sap, dap, kc, tg: lambda e: e.scalar_tensor_tensor(
                        out=dap, in0=sap, scalar=gains[:, gcol + kc:gcol + kc + 1], in1=PS[4 + tg][:, :],
                        op0=ALU.mult, op1=ALU.mult))(sap, dap, kc, tg),
                        reads=[skey, "gains"], writes=[PSK(4 + tg), dkey])

        def h_src(kc, tg):
            return hT[:, kc, tg * 512:(tg + 1) * 512], ("h", kc, tg)

        def n_dst(kc, tg):
            return nT[:, kc, tg * 512:(tg + 1) * 512], ("n", kc, tg)

        def ffn(l, f, gcol):
            wgu = wgu_d[(l, f)]
            wd = wd_d[(l, f)]
            Sx.barrier()
            rmsnorm(h_src, NKC, gcol, D, n_dst)

            def wa_view(slot):
                return WA[:, slot * 2048:(slot + 1) * 2048].rearrange("p (k h c) -> p k h c", k=8, h=2)

            def dma_wgu(fc):
                slot = fc % 4
                add("pool", lambda e: e.dma_start(out=WA[:, slot * 2048:(slot + 1) * 2048],
                                                  in_=wgu[:, fc * 2048:(fc + 1) * 2048], max_dma_last_dim=4096),
                    writes=[("WA", slot)], tag="WA%d" % slot)

            def dma_wd(fc):
                slot = fc % 4
                add("pool", lambda e: e.dma_start(out=WB[:, slot, :], in_=wd[:, fc * 1024:(fc + 1) * 1024],
                                                  max_dma_last_dim=4096),
                    writes=[("WB", slot)], tag="WB%d" % slot)

            def gu_chunk(g, fi, fc):
                slot = fc % 4
                wv = wa_view(slot)
                for half in range(2):
                    for kc in range(NKC):
                        for tg in range(NTG):
                            bank = half * 4 + tg
                            add("pe", (lambda kc, tg, bank, half: lambda e: e.matmul(
                                PS[bank][:, :], lhsT=wv[:, kc, half, :], rhs=nT[:, kc, tg * 512:(tg + 1) * 512],
                                start=(kc == 0), stop=(kc == NKC - 1)))(kc, tg, bank, half),
                                reads=[("WA", slot), ("n", kc, tg)], writes=[PSK(bank)])
                    if half == 0:
                        for tg in range(NTG):
                            add("act", (lambda tg: lambda e: e.activation(out=sgb[:, tg, :], in_=PS[tg][:, :], func=AF.Silu))(tg),
                                writes=[PSK(tg), ("sg", tg)])
                    else:
                        for tg in range(NTG):
                            add("dve", (lambda tg: lambda e: e.tensor_tensor(
                                out=actb[:, g % 2, fi, tg * 512:(tg + 1) * 512], in0=PS[4 + tg][:, :], in1=sgb[:, tg, :],
                                op=ALU.mult))(tg),
                                reads=[("sg", tg)], writes=[PSK(4 + tg), ("act", g % 2, fi, tg)])

            def down(g, nf, f0):
                for dc in range(NKC):
                    for fi in range(nf):
                        for tg in range(NTG):
                            bank = (dc % 2) * 4 + tg
                            add("pe", (lambda dc, fi, tg, bank: lambda e: e.matmul(
                                PS[bank][:, :], lhsT=WB[:, (f0 + fi) % 4, dc * 128:(dc + 1) * 128],
                                rhs=actb[:, g % 2, fi, tg * 512:(tg + 1) * 512],
                                start=(fi == 0), stop=(fi == nf - 1)))(dc, fi, tg, bank),
                                reads=[("WB", (f0 + fi) % 4), ("act", g % 2, fi, tg)], writes=[PSK(bank)])
                    for tg in range(NTG):
                        bank = (dc % 2) * 4 + tg
                        add("dve", (lambda dc, tg, bank: lambda e: e.scalar_tensor_tensor(
                            out=hT[:, dc, tg * 512:(tg + 1) * 512], in0=PS[bank][:, :], scalar=0.5,
                            in1=hT[:, dc, tg * 512:(tg + 1) * 512], op0=ALU.mult, op1=ALU.add))(dc, tg, bank),
                            reads=[], writes=[PSK(bank), ("h", dc, tg)])

            for fc in range(4):
                dma_wgu(fc)
            for fc in range(4):
                dma_wd(fc)
            for g, (f0, nf) in enumerate(FGROUPS):
                for fi in range(nf):
                    fc = f0 + fi
                    gu_chunk(g, fi, fc)
                    if fc + 4 < NFC:
                        dma_wgu(fc + 4)
                    if fi == 0 and g > 0:
                        pf0, pnf = FGROUPS[g - 1]
                        down(g - 1, pnf, pf0)
                        for k in range(nf):
                            dma_wd(f0 + k)
            lf0, lnf = FGROUPS[-1]
            down(len(FGROUPS) - 1, lnf, lf0)
            Sx.barrier()

        def mixer(l):
            gb = l * G_PER_L
            win = win_d[l]
            Sx.barrier()
            rmsnorm(h_src, NKC, gb + G_MIX, D, n_dst)
            add("dve", lambda e: e.memset(kz[:, :, :], 0.0), writes=["kz", "kzz"])
            add("dve", lambda e: e.memset(vpad[:, :, :, :], 0.0), writes=["vpad", "vpz"])

            def nextbank():
                b = st["pb"] % 8
                st["pb"] += 1
                return b

            def dma_win(step):
                slot = step % 2
                if step < 4:
                    off, n = step * WIN_SB, WIN_SB
                else:
                    off, n = 4 * WIN_SB + (step - 4) * WIN_SW, WIN_SW
                add("pool", lambda e: e.dma_start(out=WA[:, slot * 4096:slot * 4096 + n], in_=win[:, off:off + n],
                                                  max_dma_last_dim=4096),
                    writes=[("WA", 2 * slot), ("WA", 2 * slot + 1)], tag="WA%d" % (2 * slot))

            def wstep(step):
                slot = step % 2
                ncol = 384 if step < 4 else 448
                return WA[:, slot * 4096:slot * 4096 + 8 * ncol].rearrange("p (k c) -> p k c", k=8), \
                    [("WA", 2 * slot), ("WA", 2 * slot + 1)]

            def proj_fm(wv, wkeys, c0, evac):
                base = 4 * (st["pb"] % 2)
                st["pb"] += 1
                for kc in range(NKC):
                    for tg in range(NTG):
                        bank = base + tg
                        add("pe", (lambda kc, tg, bank: lambda e: e.matmul(
                            PS[bank][:, :], lhsT=wv[:, kc, c0:c0 + 128], rhs=nT[:, kc, tg * 512:(tg + 1) * 512],
                            start=(kc == 0), stop=(kc == NKC - 1)))(kc, tg, bank),
                            reads=wkeys + [("n", kc, tg)], writes=[PSK(bank)])
                for tg in range(NTG):
                    evac(tg, base + tg)

            def evac_q(ci):
                def f(tg, bank):
                    add("dve", lambda e: e.tensor_copy(out=qbuf[:, ci, tg * 512:(tg + 1) * 512], in_=PS[bank][:, :]),
                        writes=[PSK(bank), ("q", ci, tg)])
                return f

            def evac_k(tg, bank):
                add("dve", lambda e: e.tensor_scalar(out=kz[0:64, 0, tg * 512:(tg + 1) * 512], in0=PS[bank][0:64, :],
                                                     scalar1=0.125, scalar2=None, op0=ALU.mult),
                    reads=["kzz"], writes=[PSK(bank), ("kz", 0, tg)])
                add("act", lambda e: e.activation(out=kz[64:128, 1, tg * 512:(tg + 1) * 512], in_=PS[bank][64:128, :],
                                                  func=AF.Copy, scale=0.125),
                    reads=["kzz"], writes=[PSK(bank), ("kz", 1, tg)])

            def proj_v(wv, wkeys, c0, ncols):
                for t4 in range(4):
                    bank = nextbank()
                    for ti in range(4):
                        tt = t4 * 4 + ti
                        for kc in range(NKC):
                            add("pe", (lambda kc, tt, ti, bank: lambda e: e.matmul(
                                PS[bank][:, ti * 128:ti * 128 + ncols], lhsT=nT[:, kc, tt * 128:(tt + 1) * 128],
                                rhs=wv[:, kc, c0:c0 + ncols], start=(kc == 0), stop=(kc == NKC - 1)))(kc, tt, ti, bank),
                                reads=wkeys + [("n", kc, tt // 4)], writes=[PSK(bank)])
                    psv = PS[bank][:, :].rearrange("p (t c) -> p t c", t=4)
                    src1 = psv[:, :, 64:128] if ncols == 128 else psv[:, :, 0:64]
                    add("dve", (lambda t4, psv: lambda e: e.tensor_copy(
                        out=vpad[:, t4 * 4:(t4 + 1) * 4, 0, 0:64], in_=psv[:, :, 0:64]))(t4, psv),
                        reads=["vpz"], writes=[PSK(bank), ("vp", 0, t4)])
                    add("dve", (lambda t4, src1: lambda e: e.tensor_copy(
                        out=vpad[:, t4 * 4:(t4 + 1) * 4, 1, 64:128], in_=src1))(t4, src1),
                        reads=["vpz"], writes=[PSK(bank), ("vp", 1, t4)])

            def sb_step(c):
                wv, wkeys = wstep(c)
                proj_fm(wv, wkeys, 0, evac_q(0))
                proj_fm(wv, wkeys, 128, evac_k)
                proj_v(wv, wkeys, 256, 128)
                if c + 2 < 6:
                    dma_win(c + 2)
                tiles = []
                zslots = [2, 4, 6]
                for g in range(4):
                    nt = 4 * g + 4
                    for j, b in enumerate(range(nt - 1, -1, -1)):
                        k = b - 4 * g
                        i = len(tiles)
                        tiles.append(dict(b=b, g=g, cs=max(k, 0) * 128, diag=(k >= 0), i=i, j=j,
                                          first=(j == 0), last=(j == nt - 1), sset=(g % 2) * 2,
                                          zb=zslots[i % 3], es=i % 2, sps=i % 3, ws=i % 3, ob=g % 2))

                def s_z(t):
                    b, cs, zb, g = t["b"], t["cs"], t["zb"], t["g"]
                    q0 = g * 512
                    if t["first"]:
                        ss0, ob = t["sset"], t["ob"]
                        add("dve", lambda e: e.memset(ssum[:, ss0:ss0 + 2, :, :], 0.0),
                            writes=[("ss", ss0), ("ss", ss0 + 1)])
                        add("pe", lambda e: e.matmul(PSALL[:, ob, :], lhsT=zeros, rhs=qbuf[:, 0, q0:q0 + 512],
                                                     start=True, stop=False),
                            reads=["cst", ("q", 0, g)], writes=[PSK(ob)])
                    for par in range(2):
                        add("pe", (lambda par: lambda e: e.matmul(
                            PSALL[:, zb + par, cs:512], lhsT=kz[:, par, b * 128:(b + 1) * 128],
                            rhs=qbuf[:, 0, q0 + cs:q0 + 512], start=True, stop=False))(par),
                            reads=[("kz", par, b // 4), ("q", 0, g)], writes=[PSK(zb + par)])

                def s_exp(t):
                    cs, zb, es_ = t["cs"], t["zb"], t["es"]
                    add("act", lambda e: e.activation(out=ebuf[:, es_, :, cs:512], in_=PSALL[:, zb:zb + 2, cs:512], func=AF.Exp),
                        writes=[PSK(zb), PSK(zb + 1), ("e", es_)])

                def s_ln(t):
                    cs, es_, sps = t["cs"], t["es"], t["sps"]
                    add("act", lambda e: e.activation(out=spr[:, sps, :, cs:512], in_=ebuf[:, es_, :, cs:512], func=AF.Ln, bias=1.0),
                        reads=[("e", es_)], writes=[("spr", sps)])
                    if t["diag"]:
                        add("dve", lambda e: e.tensor_tensor(out=spr[:, sps, :, cs:cs + 128], in0=spr[:, sps, :, cs:cs + 128],
                                                             in1=spmask2, op=ALU.mult),
                            reads=["cst"], writes=[("spr", sps)])

                def s_cum(t):
                    cs, zb, sps, ws, j = t["cs"], t["zb"], t["sps"], t["ws"], t["j"]
                    scur = t["sset"] + (j % 2)
                    snxt = t["sset"] + ((j + 1) % 2)
                    for par in range(2):
                        add("pe", (lambda par: lambda e: e.matmul(
                            PSALL[:, zb + par, cs:512], lhsT=negtri, rhs=spr[:, sps, par, cs:512],
                            start=False, stop=False))(par),
                            reads=[("spr", sps), "cst"], writes=[PSK(zb + par)])
                        if not t["first"]:
                            add("pe", (lambda par: lambda e: e.matmul(
                                PSALL[:, zb + par, cs:512], lhsT=negones, rhs=ssum[:, scur, par, cs:512],
                                start=False, stop=(not t["diag"])))(par),
                                reads=[("ss", scur), "cst"], writes=[PSK(zb + par)])
                        if t["diag"]:
                            add("pe", (lambda par: lambda e: e.matmul(
                                PSALL[:, zb + par, cs:cs + 128], lhsT=ident, rhs=negmask, start=False, stop=True))(par),
                                reads=["cst"], writes=[PSK(zb + par)])
                    add("act", lambda e: e.activation(out=wbuf[:, ws, :, cs:512], in_=PSALL[:, zb:zb + 2, cs:512], func=AF.Exp),
                        writes=[PSK(zb), PSK(zb + 1), ("w", ws)])
                    if not t["last"]:
                        add("dve", lambda e: e.tensor_tensor(out=ssum[:, snxt, :, cs:512], in0=ssum[:, scur, :, cs:512],
                                                             in1=spr[:, sps, :, cs:512], op=ALU.add),
                            reads=[("ss", scur), ("spr", sps)], writes=[("ss", snxt)])

                def s_av(t):
                    b, cs, ws, ob, g = t["b"], t["cs"], t["ws"], t["ob"], t["g"]
                    for par in range(2):
                        add("pe", (lambda par: lambda e: e.matmul(
                            PSALL[:, ob, cs:512], lhsT=vpad[:, b, par, :], rhs=wbuf[:, ws, par, cs:512],
                            start=False, stop=(t["last"] and par == 1)))(par),
                            reads=[("w", ws), ("vp", par, b // 4)], writes=[PSK(ob)])
                    if t["last"]:
                        add("dve", lambda e: e.tensor_copy(out=oT[:, c, g * 512:(g + 1) * 512], in_=PSALL[:, ob, :]),
                            writes=[PSK(ob), ("o", c, g)])

                for _ in _pipeline(tiles, [s_z, s_cum, s_ln, s_av, s_exp], [0, 2, 1, 3, 0]):
                    pass

            def sw_step(j):
                step = 4 + j
                wv, wkeys = wstep(step)
                proj_fm(wv, wkeys, 0, evac_q(0))
                proj_fm(wv, wkeys, 128, evac_q(1))
                proj_fm(wv, wkeys, 256, evac_k)
                proj_v(wv, wkeys, 384, 64)
                if step + 2 < 6:
                    dma_win(step + 2)
                else:
                    if step == 5:
                        for hh in range(2):
                            add("pool", (lambda hh: lambda e: e.dma_start(
                                out=WA[:, hh * 4096:(hh + 1) * 4096], in_=wout_d[l][:, hh * 4096:(hh + 1) * 4096],
                                max_dma_last_dim=4096))(hh),
                                writes=[("WA", 2 * hh), ("WA", 2 * hh + 1)], tag="WA%d" % (2 * hh))

                def gen_score(ci, quad, u):
                    n0 = quad * 4
                    for par in range(2):
                        h = 2 * (2 * j + ci) + par
                        for bi, b in enumerate(range(n0 - 1, n0 + 4)):
                            if b < 0:
                                continue
                            if b == n0 - 1:
                                qlo, ncol, bmo = n0 * 128, 128, 128
                            elif b == n0 + 3:
                                qlo, ncol, bmo = b * 128, 128, 0
                            else:
                                qlo, ncol, bmo = b * 128, 256, 0
                            sbk = 3 + ((par * 5 + bi) % 3)
                            add("pe", (lambda b, qlo, ncol, sbk, par: lambda e: e.matmul(
                                PS[sbk][:, 0:ncol], lhsT=kz[:, par, b * 128:(b + 1) * 128],
                                rhs=qbuf[:, ci, qlo:qlo + ncol], start=True, stop=False))(b, qlo, ncol, sbk, par),
                                reads=[("kz", par, b // 4), ("q", ci, qlo // 512), ("q", ci, (qlo + ncol - 1) // 512)],
                                writes=[PSK(sbk)])
                            add("pe", (lambda ncol, sbk, bmo, h: lambda e: e.matmul(
                                PS[sbk][:, 0:ncol], lhsT=ident, rhs=bm[:, h, bmo:bmo + ncol],
                                start=False, stop=True))(ncol, sbk, bmo, h),
                                reads=["cst", "bm"], writes=[PSK(sbk)])
                            add("act", (lambda ncol, sbk, bmo, par, bi: lambda e: e.activation(
                                out=pbuf[:, u % 2, par, bi, bmo:bmo + ncol], in_=PS[sbk][:, 0:ncol], func=AF.Exp))(ncol, sbk, bmo, par, bi),
                                writes=[PSK(sbk), ("p", u % 2, par, bi)])
                            yield

                def gen_pv(ci, quad, u):
                    n0 = quad * 4
                    cidx = 2 * j + ci
                    ob = 1 + (u % 2)
                    db = 6 + (u % 2)
                    for which in range(2):
                        bank = ob if which == 0 else db
                        for qi in range(4):
                            n = n0 + qi
                            mms = []
                            for par in range(2):
                                if n >= 1:
                                    mms.append((par, n - 1, qi, 128))
                                mms.append((par, n, qi + 1, 0))
                            for mi, (par, kb, bi, po) in enumerate(mms):
                                lhs = vpad[:, kb, par, :] if which == 0 else onespad[par]
                                add("pe", (lambda lhs, par, bi, po, qi, mi, bank, nm: lambda e: e.matmul(
                                    PS[bank][:, qi * 128:(qi + 1) * 128], lhsT=lhs, rhs=pbuf[:, u % 2, par, bi, po:po + 128],
                                    start=(mi == 0), stop=(mi == nm - 1)))(lhs, par, bi, po, qi, mi, bank, len(mms)),
                                    reads=[("p", u % 2, par, bi), ("vp", par, kb // 4), "cst"], writes=[PSK(bank)])
                            yield
                    col = 4 * l + cidx
                    add("act", lambda e: e.activation(out=lnden[:, :], in_=PS[db][:, :], func=AF.Ln,
                                                      bias=sinkexp[:, col:col + 1]),
                        reads=["sinkexp%d" % l], writes=[PSK(db), "lnden"])
                    add("act", lambda e: e.activation(out=rden[:, :], in_=lnden[:, :], func=AF.Exp, scale=-1.0),
                        reads=["lnden"], writes=["rden"])
                    add("dve", lambda e: e.tensor_tensor(out=oT[:, 4 + cidx, n0 * 128:n0 * 128 + 512], in0=PS[ob][:, :],
                                                         in1=rden[:, :], op=ALU.mult),
                        reads=["rden"], writes=[PSK(ob), ("o", 4 + cidx, quad)])
                    yield

                units = [(ci, quad) for ci in range(2) for quad in range(4)]
                prev = None
                for ui, (ci, quad) in enumerate(units):
                    u = j * 8 + ui
                    gs = gen_score(ci, quad, u)
                    if prev is None:
                        _interleave(gs)
                    else:
                        _interleave(prev, gs)
                    prev = gen_pv(ci, quad, u)
                _interleave(prev)

            dma_win(0)
            dma_win(1)
            for c in range(4):
                sb_step(c)
            def o_src(c0):
                return lambda kc, tg: (oT[:, c0 + kc, tg * 512:(tg + 1) * 512], ("o", c0 + kc, tg))

            Sx.barrier()
            rmsnorm(o_src(0), 4, gb + G_OSB, 512, o_src(0))
            for j in range(2):
                sw_step(j)
            Sx.barrier()
            rmsnorm(o_src(4), 4, gb + G_OSW, 512, o_src(4))
            wo = WA[:, :].rearrange("p (k c) -> p k c", k=8)
            for dc in range(NKC):
                for tg in range(NTG):
                    bank = nextbank()
                    for kc in range(NKC):
                        add("pe", (lambda dc, tg, kc, bank: lambda e: e.matmul(
                            PS[bank][:, :], lhsT=wo[:, kc, dc * 128:(dc + 1) * 128], rhs=oT[:, kc, tg * 512:(tg + 1) * 512],
                            start=(kc == 0), stop=(kc == NKC - 1)))(dc, tg, kc, bank),
                            reads=[("WA", kc // 2), ("o", kc, tg)], writes=[PSK(bank)])
                    add("dve", (lambda dc, tg, bank: lambda e: e.tensor_tensor(
                        out=hT[:, dc, tg * 512:(tg + 1) * 512], in0=PS[bank][:, :], in1=hT[:, dc, tg * 512:(tg + 1) * 512],
                        op=ALU.add))(dc, tg, bank),
                        writes=[PSK(bank), ("h", dc, tg)])
            Sx.barrier()

        for l in range(nlayers):
            gb = l * G_PER_L
            if "ffn1" in stages:
                ffn(l, 1, gb + G_FFN1)
            if "mix" in stages:
                mixer(l)
            if "ffn2" in stages:
                ffn(l, 2, gb + G_FFN2)
        Sx.barrier()
        if final_norm:
            rmsnorm(h_src, NKC, G_FINAL, D, h_src)
        for kc in range(NKC):
            add("sp", (lambda kc: lambda e: e.dma_start(out=outT_d[:, kc, :], in_=hT[:, kc, :]))(kc),
                reads=[("h", kc, tg) for tg in range(NTG)], tag="out%d" % kc)
        Sx.emit(nc)
    return nc


def _t5_bucket(dist):
    max_exact = 16
    d = np.maximum(dist, 1).astype(np.float32)
    large = max_exact + (np.log(d / np.float32(max_exact)) / np.float32(np.log(128 / max_exact))
                         * np.float32(32 - max_exact)).astype(np.int32)
    large = np.minimum(large, 31)
    return np.where(dist < max_exact, dist, large)


def _consts():
    c = np.zeros((128, NCST), np.float32)
    p = np.arange(128)[:, None]
    f = np.arange(128)[None, :]
    c[:, C_ONES:C_ONES + 128] = 1.0
    c[:, C_IDENT:C_IDENT + 128] = (p == f)
    c[:, C_NEGTRI:C_NEGTRI + 128] = -1.0 * (p >= f)
    c[:, C_NEGMASK:C_NEGMASK + 128] = NEG * (p >= f)
    c[:, C_SPMASK:C_SPMASK + 128] = 1.0 * (p < f)
    c[:, C_NEGONES:C_NEGONES + 128] = -1.0
    c[:, C_SPMASK2:C_SPMASK2 + 128] = 1.0 * (p < f)
    c[:, C_SPMASK2 + 128:C_SPMASK2 + 256] = 1.0 * (p < f)
    c[:, C_ONESPAD:C_ONESPAD + 64] = 1.0
    c[:, C_ONESPAD + 128 + 64:C_ONESPAD + 256] = 1.0
    return c


def _fm(v):
    return np.ascontiguousarray(v.reshape(-1, 128).T)


def _prep_shared(inp):
    sh = {}
    g = np.zeros((128, NGC), np.float32)
    for l in range(NL):
        b = l * G_PER_L
        g[:, b + G_FFN1:b + G_FFN1 + 8] = _fm(inp["norm_ffn1"][l])
        g[:, b + G_MIX:b + G_MIX + 8] = _fm(inp["norm_mix"][l])
        g[:, b + G_OSB:b + G_OSB + 4] = _fm(inp["norm_out_sb"][l])
        g[:, b + G_OSW:b + G_OSW + 4] = _fm(inp["norm_out_swa"][l])
        g[:, b + G_FFN2:b + G_FFN2 + 8] = _fm(inp["norm_ffn2"][l])
        g[:, b + G_SINK:b + G_SINK + 4] = np.repeat(inp["sinks"][l].reshape(4, 2).T, 64, axis=0)
    g[:, G_FINAL:G_FINAL + 8] = _fm(inp["norm_final"])
    sh["gains"] = g
    sh["cst"] = _consts()
    s_idx = np.arange(128)[:, None]
    a_idx = np.arange(128)[None, :]
    d_cur = a_idx - s_idx
    d_prev = 128 + a_idx - s_idx
    rb = inp["rel_bias"].astype(np.float32)
    bmm = np.zeros((128, 8, 256), np.float32)
    for h in range(8):
        cur = np.where(d_cur >= 0, rb[_t5_bucket(np.maximum(d_cur, 0)), h], np.float32(NEG))
        prev = np.where(d_prev < 128, rb[_t5_bucket(np.minimum(d_prev, 127)), h], np.float32(NEG))
        bmm[:, h, 0:128] = cur
        bmm[:, h, 128:256] = prev
    sh["bm"] = bmm.reshape(128, 8 * 256)
    for l in range(NL):
        for f, (kgu, kd) in ((1, ("w_ffn1_gu", "w_ffn1_down")), (2, ("w_ffn2_gu", "w_ffn2_down"))):
            W = inp[kgu][l]
            gu = np.stack([W[:, :DFF], W[:, DFF:]], axis=1)
            gu = gu.reshape(8, 128, 2, NFC, 128).transpose(1, 3, 0, 2, 4)
            sh["wgu%d_%d" % (f, l)] = np.ascontiguousarray(gu).reshape(128, NFC * 2048)
            Wd = inp[kd][l]
            sh["wd%d_%d" % (f, l)] = np.ascontiguousarray(Wd.reshape(NFC, 128, 1024).transpose(1, 0, 2)).reshape(128, NFC * 1024)
        Wi = inp["w_in"][l]
        parts = []
        for c in range(4):
            cols = np.concatenate([np.arange(c * 128, c * 128 + 128), 512 + np.arange(c * 128, c * 128 + 128),
                                   1024 + np.arange(c * 128, c * 128 + 128)])
            parts.append(Wi[:, cols].reshape(8, 128, 384).transpose(1, 0, 2).reshape(128, WIN_SB))
        for j in range(2):
            kc_ = 2048 + j * 64 + np.arange(64)
            cols = np.concatenate([1536 + j * 256 + np.arange(256), kc_, kc_, 2176 + j * 64 + np.arange(64)])
            parts.append(Wi[:, cols].reshape(8, 128, 448).transpose(1, 0, 2).reshape(128, WIN_SW))
        sh["win_%d" % l] = np.ascontiguousarray(np.concatenate(parts, axis=1))
        Wo = inp["w_out"][l]
        sh["wout_%d" % l] = np.ascontiguousarray(Wo.reshape(8, 128, 1024).transpose(1, 0, 2)).reshape(128, 8 * 1024)
    return sh


_CFG = dict(layers=NL, stages=("ffn1", "mix", "ffn2"), final_norm=True)


def run_cfg(inputs, cfg):
    inp = {k: np.asarray(v, dtype=np.float32) for k, v in inputs.items()}
    sh = _prep_shared(inp)
    x = inp["x"]
    B = x.shape[0]
    in_maps = []
    for b in range(B):
        m = dict(sh)
        m["xT"] = np.ascontiguousarray(x[b].T.reshape(NKC, 128, S).transpose(1, 0, 2))
        in_maps.append(m)
    nc = build_nc(cfg)
    res = run_bass_kernel_spmd(nc, in_maps, core_ids=list(range(B)))
    outs = []
    for b in range(B):
        oT_ = np.asarray(res.results[b]["outT"])
        outs.append(oT_.transpose(1, 0, 2).reshape(D, S).T)
    return np.ascontiguousarray(np.stack(outs, axis=0)).astype(np.float32)


def kernel(**inputs):
    return run_cfg(inputs, _CFG)
```

```python
import numpy as np
from contextlib import ExitStack

import concourse.bass as bass
import concourse.mybir as mybir
from concourse.bass_utils import run_bass_kernel_spmd

F32 = mybir.dt.float32
BF16 = mybir.dt.bfloat16
AF = mybir.ActivationFunctionType
ALU = mybir.AluOpType

D = 1024
S = 2048
DFF = 2816
NFC = DFF // 128
NKC = D // 128
NTG = S // 512
NL = 2
EPS = 1e-6
NEG = -30000.0
FGROUPS = [(0, 4), (4, 4), (8, 4), (12, 4), (16, 4), (20, 2)]

G_PER_L = 36
G_FFN1, G_MIX, G_OSB, G_OSW, G_FFN2, G_SINK = 0, 8, 16, 20, 24, 32
G_FINAL = NL * G_PER_L
NGC = G_FINAL + 8

C_ONES, C_IDENT, C_NEGTRI, C_NEGMASK, C_SPMASK, C_NEGONES, C_ZEROS, C_ONESPAD = 0, 128, 256, 384, 512, 640, 768, 896
NCST = 1152

WIN_SB = 8 * 384
WIN_SW = 8 * 448
WIN_TOT = 4 * WIN_SB + 2 * WIN_SW


class Sched:
    ENGS = ["pe", "act", "dve", "pool", "sp"]

    def __init__(self):
        self.ops = []
        self.last_w = {}
        self.readers = {}
        self.dma_count = {}
        self.bar = set()
        self.last_eng = {}
        self.last_tag = {}

    def add(self, eng, fn, reads=(), writes=(), tag=None):
        i = len(self.ops)
        deps = set(self.bar)
        for r in reads:
            if r in self.last_w:
                deps.add(self.last_w[r])
        for w in writes:
            if w in self.last_w:
                deps.add(self.last_w[w])
            deps.update(self.readers.get(w, ()))
        op = dict(eng=eng, fn=fn, deps=deps, tag=tag, cnt=None)
        if tag is not None:
            self.dma_count[tag] = self.dma_count.get(tag, 0) + 1
            op["dma_n"] = self.dma_count[tag]
            self.last_tag[tag] = i
        else:
            self.last_eng[eng] = i
        self.ops.append(op)
        for r in reads:
            self.readers.setdefault(r, []).append(i)
        for w in writes:
            self.last_w[w] = i
            self.readers[w] = []
        return i

    def barrier(self):
        self.bar = set(self.last_eng.values()) | set(self.last_tag.values())

    def emit(self, nc):
        ops = self.ops
        for op in ops:
            op["deps"] = {
                d for d in op["deps"]
                if not (ops[d]["eng"] == "pe" and op["eng"] == "pe"
                        and ops[d]["tag"] is None and op["tag"] is None)
            }
        needed = set()
        for op in ops:
            for d in op["deps"]:
                if ops[d]["tag"] is None:
                    needed.add(d)
        cnt = {e: 0 for e in self.ENGS}
        for i, op in enumerate(ops):
            if op["tag"] is None and i in needed:
                cnt[op["eng"]] += 1
                op["cnt"] = cnt[op["eng"]]
        tags = sorted(self.dma_count.keys())
        with ExitStack() as es:
            esem = {e: es.enter_context(nc.semaphore("s_" + e)) for e in self.ENGS}
            tsem = {t: es.enter_context(nc.semaphore("d_" + str(t))) for t in tags}
            block = es.enter_context(nc.Block())

            def run(eng_name, eng):
                seen = {}
                for op in ops:
                    if op["eng"] != eng_name:
                        continue
                    waits = {}
                    for d in op["deps"]:
                        dop = ops[d]
                        if dop["tag"] is not None:
                            key, val, sem = ("t", dop["tag"]), 16 * dop["dma_n"], tsem[dop["tag"]]
                        else:
                            key, val, sem = ("e", dop["eng"]), dop["cnt"], esem[dop["eng"]]
                        if seen.get(key, 0) < val and waits.get(key, (0, None))[0] < val:
                            waits[key] = (val, sem)
                    for key, (val, sem) in waits.items():
                        eng.wait_ge(sem, val)
                        seen[key] = val
                    ins = op["fn"](eng)
                    if op["tag"] is not None:
                        ins.then_inc(tsem[op["tag"]], 16)
                    elif op["cnt"] is not None:
                        ins.then_inc(esem[eng_name], 1)
                last = {}
                for op in ops:
                    if op["eng"] == eng_name and op["tag"] is not None:
                        last[op["tag"]] = max(last.get(op["tag"], 0), 16 * op["dma_n"])
                for t, v in last.items():
                    if seen.get(("t", t), 0) < v:
                        eng.wait_ge(tsem[t], v)

            @block.tensor
            def _(e):
                run("pe", e)

            @block.scalar
            def _(e):
                run("act", e)

            @block.vector
            def _(e):
                run("dve", e)

            @block.gpsimd
            def _(e):
                run("pool", e)

            @block.sync
            def _(e):
                run("sp", e)


def _interleave(*gens):
    gens = list(gens)
    while gens:
        for g in list(gens):
            try:
                next(g)
            except StopIteration:
                gens.remove(g)


def _pipeline(tiles, stages, lags):
    n = len(tiles)
    mx = max(lags)
    for s in range(n + mx):
        for st, lag in zip(stages, lags):
            i = s - lag
            if 0 <= i < n:
                st(tiles[i])
        yield


def build_nc(cfg):
    nlayers = cfg.get("layers", NL)
    stages = cfg.get("stages", ("ffn1", "mix", "ffn2"))
    final_norm = cfg.get("final_norm", True)

    nc = bass.Bass("TRN2", target_bir_lowering=False)
    xT_d = nc.dram_tensor("xT", [128, NKC, S], F32, kind="ExternalInput").ap()
    outT_d = nc.dram_tensor("outT", [128, NKC, S], F32, kind="ExternalOutput").ap()
    gains_d = nc.dram_tensor("gains", [128, NGC], F32, kind="ExternalInput").ap()
    cst_d = nc.dram_tensor("cst", [128, NCST], F32, kind="ExternalInput").ap()
    bm_d = nc.dram_tensor("bm", [128, 8 * 256], F32, kind="ExternalInput").ap()
    wgu_d, wd_d, win_d, wout_d = {}, {}, {}, {}
    for l in range(NL):
        for f in (1, 2):
            wgu_d[(l, f)] = nc.dram_tensor("wgu%d_%d" % (f, l), [128, NFC * 2048], F32, kind="ExternalInput").ap()
            wd_d[(l, f)] = nc.dram_tensor("wd%d_%d" % (f, l), [128, NFC * 1024], F32, kind="ExternalInput").ap()
        win_d[l] = nc.dram_tensor("win_%d" % l, [128, WIN_TOT], F32, kind="ExternalInput").ap()
        wout_d[l] = nc.dram_tensor("wout_%d" % l, [128, 8 * 1024], F32, kind="ExternalInput").ap()

    es = ExitStack()
    with es:
        hT = es.enter_context(nc.sbuf_tensor("hT", [128, NKC, S], F32))
        nT = es.enter_context(nc.sbuf_tensor("nT", [128, NKC, S], BF16))
        WA = es.enter_context(nc.sbuf_tensor("WA", [128, 8192], BF16))
        cst = es.enter_context(nc.sbuf_tensor("cst_sb", [128, NCST], BF16))
        bm = es.enter_context(nc.sbuf_tensor("bm_sb", [128, 8, 256], BF16))
        gains = es.enter_context(nc.sbuf_tensor("gains_sb", [128, NGC], F32))
        sinkexp = es.enter_context(nc.sbuf_tensor("sinkexp", [128, NL * 4], F32))
        SCR = es.enter_context(nc.sbuf_tensor("SCR", [128, 45568], BF16))
        PS = [es.enter_context(nc.psum_tensor("ps%d" % i, [128, 512], F32)) for i in range(8)]

        def scr(off, n):
            return SCR[:, off:off + n]

        oT = scr(0, 16384).rearrange("p (c t) -> p c t", c=8)
        qbuf = scr(16384, 4096).rearrange("p (c t) -> p c t", c=2)
        kz = scr(20480, 4096).rearrange("p (c t) -> p c t", c=2)
        vpadA = scr(24576, 4096).rearrange("p (t v c) -> p t v c", t=16, v=2)
        vpadB = scr(41472, 4096).rearrange("p (t v c) -> p t v c", t=16, v=2)
        vpads = [vpadA, vpadB]
        vpad = vpadA
        spr = scr(28672, 2048).rearrange("p (s t) -> p s t", s=4)
        ssum = scr(30720, 3072).rearrange("p (s t) -> p s t", s=6)
        ebuf = scr(33792, 1536).rearrange("p (s t) -> p s t", s=3)
        wbuf = scr(35328, 2048).rearrange("p (s t) -> p s t", s=4)
        pbuf = scr(28672, 5120).rearrange("p (u h b t) -> p u h b t", u=2, h=2, b=5)
        lnden = scr(33792, 1024).bitcast(F32)
        rden = scr(34816, 1024).bitcast(F32)
        actb = scr(0, 16384).rearrange("p (g f t) -> p g f t", g=2, f=4)
        sgb = scr(16384, 4096).bitcast(F32).rearrange("p (s t) -> p s t", s=4)
        WB = scr(28672, 4096).rearrange("p (s t) -> p s t", s=4)
        sqb = scr(37376, 2048).rearrange("p (s t) -> p s t", s=4)
        lnv = scr(39424, 2048).bitcast(F32).rearrange("p (s t) -> p s t", s=2)

        ones = cst[:, C_ONES:C_ONES + 128]
        ident = cst[:, C_IDENT:C_IDENT + 128]
        negtri = cst[:, C_NEGTRI:C_NEGTRI + 128]
        negmask = cst[:, C_NEGMASK:C_NEGMASK + 128]
        spmask = cst[:, C_SPMASK:C_SPMASK + 128]
        onespad = [cst[:, C_ONESPAD + 128 * i:C_ONESPAD + 128 * (i + 1)] for i in range(2)]

        negones = cst[:, C_NEGONES:C_NEGONES + 128]
        zeros = cst[:, C_ZEROS:C_ZEROS + 128]

        Sx = Sched()
        add = Sx.add
        st = dict(sq=0, rs=0, pb=0)

        def PSK(b):
            return ("ps", b)

        add("sp", lambda e: e.dma_start(out=gains[:, :], in_=gains_d[:, :]), writes=["gains"], tag="gains")
        add("pool", lambda e: e.dma_start(out=cst[:, :], in_=cst_d[:, :], max_dma_last_dim=4096), writes=["cst"], tag="cst")
        add("pool", lambda e: e.dma_start(out=bm[:, :, :], in_=bm_d.rearrange("p (h t) -> p h t", h=8), max_dma_last_dim=4096),
            writes=["bm"], tag="bm")
        for kc in range(NKC):
            add("sp", (lambda kc: lambda e: e.dma_start(out=hT[:, kc, :], in_=xT_d[:, kc, :]))(kc),
                writes=[("h", kc, tg) for tg in range(NTG)], tag="h%d" % kc)
        for l in range(NL):
            add("act", (lambda l: lambda e: e.activation(
                out=sinkexp[:, 4 * l:4 * l + 4], in_=gains[:, l * G_PER_L + G_SINK:l * G_PER_L + G_SINK + 4], func=AF.Exp))(l),
                reads=["gains"], writes=["sinkexp%d" % l])

        add("pool", lambda e: e.memset(kz[:, :, :], 0.0), writes=["kzz"])
        add("pool", lambda e: e.memset(vpadA[:, :, :, :], 0.0), writes=["vpz"])
        add("pool", lambda e: e.memset(vpadB[:, :, :, :], 0.0), writes=["vpz2"])

        def rmsnorm(src, nk, gcol, dn, dst):
            for tg in range(NTG):
                bank = 4 + tg
                for kc in range(nk):
                    sl = st["sq"] % 4
                    st["sq"] += 1
                    sap, skey = src(kc, tg)
                    if kc % 2 == 0:
                        add("act", (lambda sap, sl: lambda e: e.activation(out=sqb[:, sl, :], in_=sap, func=AF.Square))(sap, sl),
                            reads=[skey], writes=[("sq", sl)])
                    else:
                        add("pool", (lambda sap, sl: lambda e: e.tensor_tensor(out=sqb[:, sl, :], in0=sap, in1=sap,
                                                                              op=ALU.mult))(sap, sl),
                            reads=[skey], writes=[("sq", sl)])
                    add("pe", (lambda sl, kc, bank: lambda e: e.matmul(PS[bank][:, :], lhsT=ones, rhs=sqb[:, sl, :],
                                                                         start=(kc == 0), stop=(kc == nk - 1)))(sl, kc, bank),
                        reads=[("sq", sl), "cst"], writes=[PSK(bank)])
                add("act", (lambda bank, tg: lambda e: e.activation(out=lnv[:, tg % 2, :], in_=PS[bank][:, :], func=AF.Ln,
                                                                    scale=1.0 / dn, bias=EPS))(bank, tg),
                    writes=[PSK(bank), ("lnv", tg % 2)])
                add("act", (lambda bank, tg: lambda e: e.activation(out=PS[bank][:, :], in_=lnv[:, tg % 2, :], func=AF.Exp,
                                                                    scale=-0.5))(bank, tg),
                    reads=[("lnv", tg % 2)], writes=[PSK(bank)])
            for kc in range(nk):
                for tg in range(NTG):
                    sap, skey = src(kc, tg)
                    dap, dkey = dst(kc, tg)
                    add("dve", (lambda sap, dap, kc, tg: lambda e: e.scalar_tensor_tensor(
                        out=dap, in0=sap, scalar=gains[:, gcol + kc:gcol + kc + 1], in1=PS[4 + tg][:, :],
                        op0=ALU.mult, op1=ALU.mult))(sap, dap, kc, tg),
                        reads=[skey, "gains"], writes=[PSK(4 + tg), dkey])

        def h_src(kc, tg):
            return hT[:, kc, tg * 512:(tg + 1) * 512], ("h", kc, tg)

        def n_dst(kc, tg):
            return nT[:, kc, tg * 512:(tg + 1) * 512], ("n", kc, tg)

        def ffn(l, f, gcol):
            wgu = wgu_d[(l, f)]
            wd = wd_d[(l, f)]
            Sx.barrier()
            rmsnorm(h_src, NKC, gcol, D, n_dst)

            def wa_view(slot):
                return WA[:, slot * 2048:(slot + 1) * 2048].rearrange("p (k h c) -> p k h c", k=8, h=2)

            def dma_wgu(fc):
                slot = fc % 4
                add("pool", lambda e: e.dma_start(out=WA[:, slot * 2048:(slot + 1) * 2048],
                                                  in_=wgu[:, fc * 2048:(fc + 1) * 2048], max_dma_last_dim=4096),
                    writes=[("WA", slot)], tag="WA%d" % slot)

            def dma_wd(fc):
                slot = fc % 4
                add("pool", lambda e: e.dma_start(out=WB[:, slot, :], in_=wd[:, fc * 1024:(fc + 1) * 1024],
                                                  max_dma_last_dim=4096),
                    writes=[("WB", slot)], tag="WB%d" % slot)

            def gu_chunk(g, fi, fc):
                slot = fc % 4
                wv = wa_view(slot)
                for half in range(2):
                    for kc in range(NKC):
                        for tg in range(NTG):
                            bank = half * 4 + tg
                            add("pe", (lambda kc, tg, bank, half: lambda e: e.matmul(
                                PS[bank][:, :], lhsT=wv[:, kc, half, :], rhs=nT[:, kc, tg * 512:(tg + 1) * 512],
                                start=(kc == 0), stop=(kc == NKC - 1)))(kc, tg, bank, half),
                                reads=[("WA", slot), ("n", kc, tg)], writes=[PSK(bank)])
                    if half == 0:
                        for tg in range(NTG):
                            add("act", (lambda tg: lambda e: e.activation(out=sgb[:, tg, :], in_=PS[tg][:, :], func=AF.Silu))(tg),
                                writes=[PSK(tg), ("sg", tg)])
                    else:
                        for tg in range(NTG):
                            add("dve", (lambda tg: lambda e: e.tensor_tensor(
                                out=actb[:, g % 2, fi, tg * 512:(tg + 1) * 512], in0=PS[4 + tg][:, :], in1=sgb[:, tg, :],
                                op=ALU.mult))(tg),
                                reads=[("sg", tg)], writes=[PSK(4 + tg), ("act", g % 2, fi, tg)])

            def down(g, nf, f0):
                for dc in range(NKC):
                    for fi in range(nf):
                        for tg in range(NTG):
                            bank = (dc % 2) * 4 + tg
                            add("pe", (lambda dc, fi, tg, bank: lambda e: e.matmul(
                                PS[bank][:, :], lhsT=WB[:, (f0 + fi) % 4, dc * 128:(dc + 1) * 128],
                                rhs=actb[:, g % 2, fi, tg * 512:(tg + 1) * 512],
                                start=(fi == 0), stop=(fi == nf - 1)))(dc, fi, tg, bank),
                                reads=[("WB", (f0 + fi) % 4), ("act", g % 2, fi, tg)], writes=[PSK(bank)])
                    for tg in range(NTG):
                        bank = (dc % 2) * 4 + tg
                        add("dve", (lambda dc, tg, bank: lambda e: e.scalar_tensor_tensor(
                            out=hT[:, dc, tg * 512:(tg + 1) * 512], in0=PS[bank][:, :], scalar=0.5,
                            in1=hT[:, dc, tg * 512:(tg + 1) * 512], op0=ALU.mult, op1=ALU.add))(dc, tg, bank),
                            reads=[], writes=[PSK(bank), ("h", dc, tg)])

            for fc in range(4):
                dma_wgu(fc)
            for fc in range(4):
                dma_wd(fc)
            for g, (f0, nf) in enumerate(FGROUPS):
                for fi in range(nf):
                    fc = f0 + fi
                    gu_chunk(g, fi, fc)
                    if fc + 4 < NFC:
                        dma_wgu(fc + 4)
                    if fi == 0 and g > 0:
                        pf0, pnf = FGROUPS[g - 1]
                        down(g - 1, pnf, pf0)
                        for k in range(nf):
                            dma_wd(f0 + k)
            lf0, lnf = FGROUPS[-1]
            down(len(FGROUPS) - 1, lnf, lf0)
            Sx.barrier()

        def mixer(l):
            gb = l * G_PER_L
            win = win_d[l]
            Sx.barrier()
            rmsnorm(h_src, NKC, gb + G_MIX, D, n_dst)

            def nextbank():
                b = st["pb"] % 8
                st["pb"] += 1
                return b

            def dma_win(step):
                slot = step % 2
                if step < 4:
                    off, n = step * WIN_SB, WIN_SB
                else:
                    off, n = 4 * WIN_SB + (step - 4) * WIN_SW, WIN_SW
                add("pool", lambda e: e.dma_start(out=WA[:, slot * 4096:slot * 4096 + n], in_=win[:, off:off + n],
                                                  max_dma_last_dim=4096),
                    writes=[("WA", 2 * slot), ("WA", 2 * slot + 1)], tag="WA%d" % (2 * slot))

            def wstep(step):
                slot = step % 2
                ncol = 384 if step < 4 else 448
                return WA[:, slot * 4096:slot * 4096 + 8 * ncol].rearrange("p (k c) -> p k c", k=8), \
                    [("WA", 2 * slot), ("WA", 2 * slot + 1)]

            def proj_fm(wv, wkeys, c0, evac):
                base = 4 * (st["pb"] % 2)
                st["pb"] += 1
                for kc in range(NKC):
                    for tg in range(NTG):
                        bank = base + tg
                        add("pe", (lambda kc, tg, bank: lambda e: e.matmul(
                            PS[bank][:, :], lhsT=wv[:, kc, c0:c0 + 128], rhs=nT[:, kc, tg * 512:(tg + 1) * 512],
                            start=(kc == 0), stop=(kc == NKC - 1)))(kc, tg, bank),
                            reads=wkeys + [("n", kc, tg)], writes=[PSK(bank)])
                for tg in range(NTG):
                    evac(tg, base + tg)

            def evac_q(ci):
                def f(tg, bank):
                    add("dve", lambda e: e.tensor_copy(out=qbuf[:, ci, tg * 512:(tg + 1) * 512], in_=PS[bank][:, :]),
                        writes=[PSK(bank), ("q", ci, tg)])
                return f

            def evac_k(tg, bank):
                add("dve", lambda e: e.tensor_scalar(out=kz[0:64, 0, tg * 512:(tg + 1) * 512], in0=PS[bank][0:64, :],
                                                     scalar1=0.125, scalar2=None, op0=ALU.mult),
                    reads=["kzz"], writes=[PSK(bank), ("kz", 0, tg)])
                add("act", lambda e: e.activation(out=kz[64:128, 1, tg * 512:(tg + 1) * 512], in_=PS[bank][64:128, :],
                                                  func=AF.Copy, scale=0.125),
                    reads=["kzz"], writes=[PSK(bank), ("kz", 1, tg)])

            def gen_proj_v(wv, wkeys, c0, ncols, vb, banks):
                vdst = vpads[vb]
                for t4 in range(4):
                    bank = banks[t4 % len(banks)]
                    for ti in range(4):
                        tt = t4 * 4 + ti
                        for kc in range(NKC):
                            add("pe", (lambda kc, tt, ti, bank: lambda e: e.matmul(
                                PS[bank][:, ti * 128:ti * 128 + ncols], lhsT=nT[:, kc, tt * 128:(tt + 1) * 128],
                                rhs=wv[:, kc, c0:c0 + ncols], start=(kc == 0), stop=(kc == NKC - 1)))(kc, tt, ti, bank),
                                reads=wkeys + [("n", kc, tt // 4)], writes=[PSK(bank)])
                            if kc % 4 == 3:
                                yield
                    psv = PS[bank][:, :].rearrange("p (t c) -> p t c", t=4)
                    src1 = psv[:, :, 64:128] if ncols == 128 else psv[:, :, 0:64]
                    add("dve", (lambda t4, psv: lambda e: e.tensor_copy(
                        out=vdst[:, t4 * 4:(t4 + 1) * 4, 0, 0:64], in_=psv[:, :, 0:64]))(t4, psv),
                        reads=["vpz", "vpz2"], writes=[PSK(bank), ("vp", vb, 0, t4)])
                    add("dve", (lambda t4, src1: lambda e: e.tensor_copy(
                        out=vdst[:, t4 * 4:(t4 + 1) * 4, 1, 64:128], in_=src1))(t4, src1),
                        reads=["vpz", "vpz2"], writes=[PSK(bank), ("vp", vb, 1, t4)])
                    yield

            def proj_v(wv, wkeys, c0, ncols, vb=0):
                banks = [nextbank(), nextbank()]
                for _ in gen_proj_v(wv, wkeys, c0, ncols, vb, banks):
                    pass

            def gen_proj_q(wv, wkeys, c0, qi, banks):
                for tg in range(NTG):
                    bank = banks[tg % len(banks)]
                    for kc in range(NKC):
                        add("pe", (lambda kc, tg, bank: lambda e: e.matmul(
                            PS[bank][:, :], lhsT=wv[:, kc, c0:c0 + 128], rhs=nT[:, kc, tg * 512:(tg + 1) * 512],
                            start=(kc == 0), stop=(kc == NKC - 1)))(kc, tg, bank),
                            reads=wkeys + [("n", kc, tg)], writes=[PSK(bank)])
                        if kc % 2 == 1:
                            yield
                    add("dve", (lambda tg, bank: lambda e: e.tensor_copy(
                        out=qbuf[:, qi, tg * 512:(tg + 1) * 512], in_=PS[bank][:, :]))(tg, bank),
                        writes=[PSK(bank), ("q", qi, tg)])
                    yield

            def sb_step(c):
                wv, wkeys = wstep(c)
                qi = c % 2
                vb = c % 2
                if c == 0:
                    for _ in gen_proj_q(wv, wkeys, 0, qi, [6, 7]):
                        pass
                    for _ in gen_proj_v(wv, wkeys, 256, 128, vb, [6, 7]):
                        pass
                proj_fm(wv, wkeys, 128, evac_k)
                if c + 2 < 6:
                    dma_win(c + 2)
                tiles = []
                zring = [2, 3, 4, 5]
                for g in range(4):
                    for par in range(2):
                        u = len([1 for t in tiles if t["first"]])
                        nt = 4 * g + 4
                        for j, b in enumerate(range(nt - 1, -1, -1)):
                            k = b - 4 * g
                            i = len(tiles)
                            tiles.append(dict(b=b, g=g, par=par, cs=max(k, 0) * 128, diag=(k >= 0), i=i, j=j,
                                              first=(j == 0), last=(j == nt - 1), sset=(u % 2) * 3,
                                              zb=zring[i % 4], es=i % 3, sps=i % 4, ws=i % 4, ob=g % 2))

                def s0(t):
                    b, cs, par, zb, es_ = t["b"], t["cs"], t["par"], t["zb"], t["es"]
                    q0 = t["g"] * 512
                    add("pe", lambda e: e.matmul(PS[zb][:, cs:512], lhsT=kz[:, par, b * 128:(b + 1) * 128],
                                                 rhs=qbuf[:, qi, q0 + cs:q0 + 512], start=True, stop=False),
                        reads=[("kz", par, b // 4), ("q", qi, t["g"])], writes=[PSK(zb)])
                    add("act", lambda e: e.activation(out=ebuf[:, es_, cs:512], in_=PS[zb][:, cs:512], func=AF.Exp),
                        writes=[PSK(zb), ("e", es_)])
                    if t["first"]:
                        ss0 = t["sset"]
                        add("dve", lambda e: e.memset(ssum[:, ss0:ss0 + 3, :], 0.0),
                            writes=[("ss", ss0), ("ss", ss0 + 1), ("ss", ss0 + 2)])
                        if par == 0:
                            ob = t["ob"]
                            add("pe", lambda e: e.matmul(PS[ob][:, :], lhsT=zeros, rhs=qbuf[:, qi, q0:q0 + 512],
                                                         start=True, stop=False),
                                reads=["cst", ("q", qi, t["g"])], writes=[PSK(ob)])

                def s1(t):
                    cs, es_, sps = t["cs"], t["es"], t["sps"]
                    add("act", lambda e: e.activation(out=spr[:, sps, cs:512], in_=ebuf[:, es_, cs:512], func=AF.Ln, bias=1.0),
                        reads=[("e", es_)], writes=[("spr", sps)])
                    if t["diag"]:
                        add("dve", lambda e: e.tensor_tensor(out=spr[:, sps, cs:cs + 128], in0=spr[:, sps, cs:cs + 128],
                                                             in1=spmask, op=ALU.mult),
                            reads=["cst"], writes=[("spr", sps)])

                def s2(t):
                    cs, zb, sps, ws, j = t["cs"], t["zb"], t["sps"], t["ws"], t["j"]
                    scur = t["sset"] + (j % 3)
                    snxt = t["sset"] + ((j + 1) % 3)
                    add("pe", lambda e: e.matmul(PS[zb][:, cs:512], lhsT=negtri, rhs=spr[:, sps, cs:512],
                                                 start=False, stop=(t["first"] and not t["diag"])),
                        reads=[("spr", sps), "cst"], writes=[PSK(zb)])
                    if not t["first"]:
                        add("pe", lambda e: e.matmul(PS[zb][:, cs:512], lhsT=negones, rhs=ssum[:, scur, cs:512],
                                                     start=False, stop=(not t["diag"])),
                            reads=[("ss", scur), "cst"], writes=[PSK(zb)])
                    if t["diag"]:
                        add("pe", lambda e: e.matmul(PS[zb][:, cs:cs + 128], lhsT=ident, rhs=negmask,
                                                     start=False, stop=True),
                            reads=["cst"], writes=[PSK(zb)])
                    add("act", lambda e: e.activation(out=wbuf[:, ws, cs:512], in_=PS[zb][:, cs:512], func=AF.Exp),
                        writes=[PSK(zb), ("w", ws)])
                    if not t["last"]:
                        add("dve", lambda e: e.tensor_tensor(out=ssum[:, snxt, cs:512], in0=ssum[:, scur, cs:512],
                                                             in1=spr[:, sps, cs:512], op=ALU.add),
                            reads=[("ss", scur), ("spr", sps)], writes=[("ss", snxt)])

                def s3(t):
                    b, cs, par, ws, ob, g = t["b"], t["cs"], t["par"], t["ws"], t["ob"], t["g"]
                    fin = (par == 1 and t["last"])
                    add("pe", lambda e: e.matmul(PS[ob][:, cs:512], lhsT=vpads[vb][:, b, par, :], rhs=wbuf[:, ws, cs:512],
                                                 start=False, stop=fin),
                        reads=[("w", ws), ("vp", vb, par, b // 4)], writes=[PSK(ob)])
                    if fin:
                        add("dve", lambda e: e.tensor_copy(out=oT[:, c, g * 512:(g + 1) * 512], in_=PS[ob][:, :]),
                            writes=[PSK(ob), ("o", c, g)])

                pipe = _pipeline(tiles, [s0, s1, s2, s3], [0, 1, 2, 4])
                if c + 1 < 4:
                    wv2, wkeys2 = wstep(c + 1)

                    def nxt():
                        yield from gen_proj_q(wv2, wkeys2, 0, (c + 1) % 2, [6, 7])
                        yield from gen_proj_v(wv2, wkeys2, 256, 128, (c + 1) % 2, [6, 7])

                    _interleave(pipe, nxt())
                else:
                    _interleave(pipe)

            def sw_step(j):
                step = 4 + j
                wv, wkeys = wstep(step)
                proj_fm(wv, wkeys, 0, evac_q(0))
                proj_fm(wv, wkeys, 128, evac_q(1))
                proj_fm(wv, wkeys, 256, evac_k)
                proj_v(wv, wkeys, 384, 64)
                if step + 2 < 6:
                    dma_win(step + 2)
                else:
                    if step == 5:
                        for hh in range(2):
                            add("pool", (lambda hh: lambda e: e.dma_start(
                                out=WA[:, hh * 4096:(hh + 1) * 4096], in_=wout_d[l][:, hh * 4096:(hh + 1) * 4096],
                                max_dma_last_dim=4096))(hh),
                                writes=[("WA", 2 * hh), ("WA", 2 * hh + 1)], tag="WA%d" % (2 * hh))

                def gen_score(ci, quad, u):
                    n0 = quad * 4
                    for par in range(2):
                        h = 2 * (2 * j + ci) + par
                        for bi, b in enumerate(range(n0 - 1, n0 + 4)):
                            if b < 0:
                                continue
                            if b == n0 - 1:
                                qlo, ncol, bmo = n0 * 128, 128, 128
                            elif b == n0 + 3:
                                qlo, ncol, bmo = b * 128, 128, 0
                            else:
                                qlo, ncol, bmo = b * 128, 256, 0
                            sbk = 3 + ((par * 5 + bi) % 3)
                            add("pe", (lambda b, qlo, ncol, sbk, par: lambda e: e.matmul(
                                PS[sbk][:, 0:ncol], lhsT=kz[:, par, b * 128:(b + 1) * 128],
                                rhs=qbuf[:, ci, qlo:qlo + ncol], start=True, stop=False))(b, qlo, ncol, sbk, par),
                                reads=[("kz", par, b // 4), ("q", ci, qlo // 512), ("q", ci, (qlo + ncol - 1) // 512)],
                                writes=[PSK(sbk)])
                            add("pe", (lambda ncol, sbk, bmo, h: lambda e: e.matmul(
                                PS[sbk][:, 0:ncol], lhsT=ident, rhs=bm[:, h, bmo:bmo + ncol],
                                start=False, stop=True))(ncol, sbk, bmo, h),
                                reads=["cst", "bm"], writes=[PSK(sbk)])
                            add("act", (lambda ncol, sbk, bmo, par, bi: lambda e: e.activation(
                                out=pbuf[:, u % 2, par, bi, bmo:bmo + ncol], in_=PS[sbk][:, 0:ncol], func=AF.Exp))(ncol, sbk, bmo, par, bi),
                                writes=[PSK(sbk), ("p", u % 2, par, bi)])
                            yield

                def gen_pv(ci, quad, u):
                    n0 = quad * 4
                    cidx = 2 * j + ci
                    ob = 1 + (u % 2)
                    db = 6 + (u % 2)
                    for which in range(2):
                        bank = ob if which == 0 else db
                        for qi in range(4):
                            n = n0 + qi
                            mms = []
                            for par in range(2):
                                if n >= 1:
                                    mms.append((par, n - 1, qi, 128))
                                mms.append((par, n, qi + 1, 0))
                            for mi, (par, kb, bi, po) in enumerate(mms):
                                lhs = vpad[:, kb, par, :] if which == 0 else onespad[par]
                                add("pe", (lambda lhs, par, bi, po, qi, mi, bank, nm: lambda e: e.matmul(
                                    PS[bank][:, qi * 128:(qi + 1) * 128], lhsT=lhs, rhs=pbuf[:, u % 2, par, bi, po:po + 128],
                                    start=(mi == 0), stop=(mi == nm - 1)))(lhs, par, bi, po, qi, mi, bank, len(mms)),
                                    reads=[("p", u % 2, par, bi), ("vp", 0, par, kb // 4), "cst"], writes=[PSK(bank)])
                            yield
                    col = 4 * l + cidx
                    add("act", lambda e: e.activation(out=lnden[:, :], in_=PS[db][:, :], func=AF.Ln,
                                                      bias=sinkexp[:, col:col + 1]),
                        reads=["sinkexp%d" % l], writes=[PSK(db), "lnden"])
                    add("act", lambda e: e.activation(out=rden[:, :], in_=lnden[:, :], func=AF.Exp, scale=-1.0),
                        reads=["lnden"], writes=["rden"])
                    add("dve", lambda e: e.tensor_tensor(out=oT[:, 4 + cidx, n0 * 128:n0 * 128 + 512], in0=PS[ob][:, :],
                                                         in1=rden[:, :], op=ALU.mult),
                        reads=["rden"], writes=[PSK(ob), ("o", 4 + cidx, quad)])
                    yield

                units = [(ci, quad) for ci in range(2) for quad in range(4)]
                prev = None
                for ui, (ci, quad) in enumerate(units):
                    u = j * 8 + ui
                    gs = gen_score(ci, quad, u)
                    if prev is None:
                        _interleave(gs)
                    else:
                        _interleave(prev, gs)
                    prev = gen_pv(ci, quad, u)
                _interleave(prev)

            dma_win(0)
            dma_win(1)
            for c in range(4):
                sb_step(c)
            def o_src(c0):
                return lambda kc, tg: (oT[:, c0 + kc, tg * 512:(tg + 1) * 512], ("o", c0 + kc, tg))

            Sx.barrier()
            rmsnorm(o_src(0), 4, gb + G_OSB, 512, o_src(0))
            for j in range(2):
                sw_step(j)
            Sx.barrier()
            rmsnorm(o_src(4), 4, gb + G_OSW, 512, o_src(4))
            wo = WA[:, :].rearrange("p (k c) -> p k c", k=8)
            for dc in range(NKC):
                for tg in range(NTG):
                    bank = nextbank()
                    for kc in range(NKC):
                        add("pe", (lambda dc, tg, kc, bank: lambda e: e.matmul(
                            PS[bank][:, :], lhsT=wo[:, kc, dc * 128:(dc + 1) * 128], rhs=oT[:, kc, tg * 512:(tg + 1) * 512],
                            start=(kc == 0), stop=(kc == NKC - 1)))(dc, tg, kc, bank),
                            reads=[("WA", kc // 2), ("o", kc, tg)], writes=[PSK(bank)])
                    add("dve", (lambda dc, tg, bank: lambda e: e.tensor_tensor(
                        out=hT[:, dc, tg * 512:(tg + 1) * 512], in0=PS[bank][:, :], in1=hT[:, dc, tg * 512:(tg + 1) * 512],
                        op=ALU.add))(dc, tg, bank),
                        writes=[PSK(bank), ("h", dc, tg)])
            Sx.barrier()

        for l in range(nlayers):
            gb = l * G_PER_L
            if "ffn1" in stages:
                ffn(l, 1, gb + G_FFN1)
            if "mix" in stages:
                mixer(l)
            if "ffn2" in stages:
                ffn(l, 2, gb + G_FFN2)
        Sx.barrier()
        if final_norm:
            rmsnorm(h_src, NKC, G_FINAL, D, h_src)
        for kc in range(NKC):
            add("sp", (lambda kc: lambda e: e.dma_start(out=outT_d[:, kc, :], in_=hT[:, kc, :]))(kc),
                reads=[("h", kc, tg) for tg in range(NTG)], tag="out%d" % kc)
        Sx.emit(nc)
    return nc


def _t5_bucket(dist):
    max_exact = 16
    d = np.maximum(dist, 1).astype(np.float32)
    large = max_exact + (np.log(d / np.float32(max_exact)) / np.float32(np.log(128 / max_exact))
                         * np.float32(32 - max_exact)).astype(np.int32)
    large = np.minimum(large, 31)
    return np.where(dist < max_exact, dist, large)


def _consts():
    c = np.zeros((128, NCST), np.float32)
    p = np.arange(128)[:, None]
    f = np.arange(128)[None, :]
    c[:, C_ONES:C_ONES + 128] = 1.0
    c[:, C_IDENT:C_IDENT + 128] = (p == f)
    c[:, C_NEGTRI:C_NEGTRI + 128] = -1.0 * (p >= f)
    c[:, C_NEGMASK:C_NEGMASK + 128] = NEG * (p >= f)
    c[:, C_SPMASK:C_SPMASK + 128] = 1.0 * (p < f)
    c[:, C_NEGONES:C_NEGONES + 128] = -1.0
    c[:, C_ONESPAD:C_ONESPAD + 64] = 1.0
    c[:, C_ONESPAD + 128 + 64:C_ONESPAD + 256] = 1.0
    return c


def _fm(v):
    return np.ascontiguousarray(v.reshape(-1, 128).T)


def _prep_shared(inp):
    sh = {}
    g = np.zeros((128, NGC), np.float32)
    for l in range(NL):
        b = l * G_PER_L
        g[:, b + G_FFN1:b + G_FFN1 + 8] = _fm(inp["norm_ffn1"][l])
        g[:, b + G_MIX:b + G_MIX + 8] = _fm(inp["norm_mix"][l])
        g[:, b + G_OSB:b + G_OSB + 4] = _fm(inp["norm_out_sb"][l])
        g[:, b + G_OSW:b + G_OSW + 4] = _fm(inp["norm_out_swa"][l])
        g[:, b + G_FFN2:b + G_FFN2 + 8] = _fm(inp["norm_ffn2"][l])
        g[:, b + G_SINK:b + G_SINK + 4] = np.repeat(inp["sinks"][l].reshape(4, 2).T, 64, axis=0)
    g[:, G_FINAL:G_FINAL + 8] = _fm(inp["norm_final"])
    sh["gains"] = g
    sh["cst"] = _consts()
    s_idx = np.arange(128)[:, None]
    a_idx = np.arange(128)[None, :]
    d_cur = a_idx - s_idx
    d_prev = 128 + a_idx - s_idx
    rb = inp["rel_bias"].astype(np.float32)
    bmm = np.zeros((128, 8, 256), np.float32)
    for h in range(8):
        cur = np.where(d_cur >= 0, rb[_t5_bucket(np.maximum(d_cur, 0)), h], np.float32(NEG))
        prev = np.where(d_prev < 128, rb[_t5_bucket(np.minimum(d_prev, 127)), h], np.float32(NEG))
        bmm[:, h, 0:128] = cur
        bmm[:, h, 128:256] = prev
    sh["bm"] = bmm.reshape(128, 8 * 256)
    for l in range(NL):
        for f, (kgu, kd) in ((1, ("w_ffn1_gu", "w_ffn1_down")), (2, ("w_ffn2_gu", "w_ffn2_down"))):
            W = inp[kgu][l]
            gu = np.stack([W[:, :DFF], W[:, DFF:]], axis=1)
            gu = gu.reshape(8, 128, 2, NFC, 128).transpose(1, 3, 0, 2, 4)
            sh["wgu%d_%d" % (f, l)] = np.ascontiguousarray(gu).reshape(128, NFC * 2048)
            Wd = inp[kd][l]
            sh["wd%d_%d" % (f, l)] = np.ascontiguousarray(Wd.reshape(NFC, 128, 1024).transpose(1, 0, 2)).reshape(128, NFC * 1024)
        Wi = inp["w_in"][l]
        parts = []
        for c in range(4):
            cols = np.concatenate([np.arange(c * 128, c * 128 + 128), 512 + np.arange(c * 128, c * 128 + 128),
                                   1024 + np.arange(c * 128, c * 128 + 128)])
            parts.append(Wi[:, cols].reshape(8, 128, 384).transpose(1, 0, 2).reshape(128, WIN_SB))
        for j in range(2):
            kc_ = 2048 + j * 64 + np.arange(64)
            cols = np.concatenate([1536 + j * 256 + np.arange(256), kc_, kc_, 2176 + j * 64 + np.arange(64)])
            parts.append(Wi[:, cols].reshape(8, 128, 448).transpose(1, 0, 2).reshape(128, WIN_SW))
        sh["win_%d" % l] = np.ascontiguousarray(np.concatenate(parts, axis=1))
        Wo = inp["w_out"][l]
        sh["wout_%d" % l] = np.ascontiguousarray(Wo.reshape(8, 128, 1024).transpose(1, 0, 2)).reshape(128, 8 * 1024)
    return sh


_CFG = dict(layers=NL, stages=("ffn1", "mix", "ffn2"), final_norm=True)


def run_cfg(inputs, cfg):
    inp = {k: np.asarray(v, dtype=np.float32) for k, v in inputs.items()}
    sh = _prep_shared(inp)
    x = inp["x"]
    B = x.shape[0]
    in_maps = []
    for b in range(B):
        m = dict(sh)
        m["xT"] = np.ascontiguousarray(x[b].T.reshape(NKC, 128, S).transpose(1, 0, 2))
        in_maps.append(m)
    nc = build_nc(cfg)
    res = run_bass_kernel_spmd(nc, in_maps, core_ids=list(range(B)))
    outs = []
    for b in range(B):
        oT_ = np.asarray(res.results[b]["outT"])
        outs.append(oT_.transpose(1, 0, 2).reshape(D, S).T)
    return np.ascontiguousarray(np.stack(outs, axis=0)).astype(np.float32)


def kernel(**inputs):
    return run_cfg(inputs, _CFG)
```

```python
import numpy as np
from contextlib import ExitStack

import concourse.bass as bass
import concourse.mybir as mybir
from concourse.bass_utils import run_bass_kernel_spmd

F32 = mybir.dt.float32
BF16 = mybir.dt.bfloat16
AF = mybir.ActivationFunctionType
ALU = mybir.AluOpType

D = 1024
S = 2048
DFF = 2816
NFC = DFF // 128
NKC = D // 128
NTG = S // 512
NL = 2
EPS = 1e-6
NEG = -30000.0
FGROUPS = [(0, 4), (4, 4), (8, 4), (12, 4), (16, 3), (19, 3)]

G_PER_L = 36
G_FFN1, G_MIX, G_OSB, G_OSW, G_FFN2, G_SINK = 0, 8, 16, 20, 24, 32
G_FINAL = NL * G_PER_L
NGC = G_FINAL + 8

C_ONES, C_IDENT, C_NEGTRI, C_NEGMASK, C_SPMASK, C_NEGONES, C_ZEROS, C_ONESPAD = 0, 128, 256, 384, 512, 640, 768, 896
NCST = 1152

WIN_SB = 8 * 384
WIN_SW = 8 * 448
WIN_TOT = 4 * WIN_SB + 2 * WIN_SW


class Sched:
    ENGS = ["pe", "act", "dve", "pool", "sp"]

    def __init__(self):
        self.ops = []
        self.last_w = {}
        self.readers = {}
        self.dma_count = {}
        self.bar = set()
        self.last_eng = {}
        self.last_tag = {}

    def add(self, eng, fn, reads=(), writes=(), tag=None):
        i = len(self.ops)
        deps = set(self.bar)
        for r in reads:
            if r in self.last_w:
                deps.add(self.last_w[r])
        for w in writes:
            if w in self.last_w:
                deps.add(self.last_w[w])
            deps.update(self.readers.get(w, ()))
        op = dict(eng=eng, fn=fn, deps=deps, tag=tag, cnt=None)
        if tag is not None:
            self.dma_count[tag] = self.dma_count.get(tag, 0) + 1
            op["dma_n"] = self.dma_count[tag]
            self.last_tag[tag] = i
        else:
            self.last_eng[eng] = i
        self.ops.append(op)
        for r in reads:
            self.readers.setdefault(r, []).append(i)
        for w in writes:
            self.last_w[w] = i
            self.readers[w] = []
        return i

    def barrier(self):
        self.bar = set(self.last_eng.values()) | set(self.last_tag.values())

    def emit(self, nc):
        ops = self.ops
        for op in ops:
            op["deps"] = {
                d for d in op["deps"]
                if not (ops[d]["eng"] == "pe" and op["eng"] == "pe"
                        and ops[d]["tag"] is None and op["tag"] is None)
            }
        needed = set()
        for op in ops:
            for d in op["deps"]:
                if ops[d]["tag"] is None:
                    needed.add(d)
        cnt = {e: 0 for e in self.ENGS}
        for i, op in enumerate(ops):
            if op["tag"] is None and i in needed:
                cnt[op["eng"]] += 1
                op["cnt"] = cnt[op["eng"]]
        tags = sorted(self.dma_count.keys())
        with ExitStack() as es:
            esem = {e: es.enter_context(nc.semaphore("s_" + e)) for e in self.ENGS}
            tsem = {t: es.enter_context(nc.semaphore("d_" + str(t))) for t in tags}
            block = es.enter_context(nc.Block())

            def run(eng_name, eng):
                seen = {}
                for op in ops:
                    if op["eng"] != eng_name:
                        continue
                    waits = {}
                    for d in op["deps"]:
                        dop = ops[d]
                        if dop["tag"] is not None:
                            key, val, sem = ("t", dop["tag"]), 16 * dop["dma_n"], tsem[dop["tag"]]
                        else:
                            key, val, sem = ("e", dop["eng"]), dop["cnt"], esem[dop["eng"]]
                        if seen.get(key, 0) < val and waits.get(key, (0, None))[0] < val:
                            waits[key] = (val, sem)
                    for key, (val, sem) in waits.items():
                        eng.wait_ge(sem, val)
                        seen[key] = val
                    ins = op["fn"](eng)
                    if op["tag"] is not None:
                        ins.then_inc(tsem[op["tag"]], 16)
                    elif op["cnt"] is not None:
                        ins.then_inc(esem[eng_name], 1)
                last = {}
                for op in ops:
                    if op["eng"] == eng_name and op["tag"] is not None:
                        last[op["tag"]] = max(last.get(op["tag"], 0), 16 * op["dma_n"])
                for t, v in last.items():
                    if seen.get(("t", t), 0) < v:
                        eng.wait_ge(tsem[t], v)

            @block.tensor
            def _(e):
                run("pe", e)

            @block.scalar
            def _(e):
                run("act", e)

            @block.vector
            def _(e):
                run("dve", e)

            @block.gpsimd
            def _(e):
                run("pool", e)

            @block.sync
            def _(e):
                run("sp", e)


def _interleave(*gens):
    gens = list(gens)
    while gens:
        for g in list(gens):
            try:
                next(g)
            except StopIteration:
                gens.remove(g)


def _pipeline(tiles, stages, lags):
    n = len(tiles)
    mx = max(lags)
    for s in range(n + mx):
        for st, lag in zip(stages, lags):
            i = s - lag
            if 0 <= i < n:
                st(tiles[i])
        yield


def build_nc(cfg):
    nlayers = cfg.get("layers", NL)
    stages = cfg.get("stages", ("ffn1", "mix", "ffn2"))
    final_norm = cfg.get("final_norm", True)

    nc = bass.Bass("TRN2", target_bir_lowering=False)
    xT_d = nc.dram_tensor("xT", [128, NKC, S], F32, kind="ExternalInput").ap()
    outT_d = nc.dram_tensor("outT", [128, NKC, S], F32, kind="ExternalOutput").ap()
    gains_d = nc.dram_tensor("gains", [128, NGC], F32, kind="ExternalInput").ap()
    cst_d = nc.dram_tensor("cst", [128, NCST], F32, kind="ExternalInput").ap()
    bm_d = nc.dram_tensor("bm", [128, 8 * 256], F32, kind="ExternalInput").ap()
    wgu_d, wd_d, win_d, wout_d = {}, {}, {}, {}
    for l in range(NL):
        for f in (1, 2):
            wgu_d[(l, f)] = nc.dram_tensor("wgu%d_%d" % (f, l), [128, NFC * 2048], F32, kind="ExternalInput").ap()
            wd_d[(l, f)] = nc.dram_tensor("wd%d_%d" % (f, l), [128, NFC * 1024], F32, kind="ExternalInput").ap()
        win_d[l] = nc.dram_tensor("win_%d" % l, [128, WIN_TOT], F32, kind="ExternalInput").ap()
        wout_d[l] = nc.dram_tensor("wout_%d" % l, [128, 8 * 1024], F32, kind="ExternalInput").ap()

    es = ExitStack()
    with es:
        hT = es.enter_context(nc.sbuf_tensor("hT", [128, NKC, S], F32))
        nT = es.enter_context(nc.sbuf_tensor("nT", [128, NKC, S], BF16))
        WA = es.enter_context(nc.sbuf_tensor("WA", [128, 8192], BF16))
        cst = es.enter_context(nc.sbuf_tensor("cst_sb", [128, NCST], BF16))
        bm = es.enter_context(nc.sbuf_tensor("bm_sb", [128, 8, 256], BF16))
        gains = es.enter_context(nc.sbuf_tensor("gains_sb", [128, NGC], F32))
        sinkexp = es.enter_context(nc.sbuf_tensor("sinkexp", [128, NL * 4], F32))
        SCR = es.enter_context(nc.sbuf_tensor("SCR", [128, 45568], BF16))
        PS = [es.enter_context(nc.psum_tensor("ps%d" % i, [128, 512], F32)) for i in range(8)]

        def scr(off, n):
            return SCR[:, off:off + n]

        oT = scr(0, 16384).rearrange("p (c t) -> p c t", c=8)
        qbuf = scr(16384, 4096).rearrange("p (c t) -> p c t", c=2)
        kz = scr(20480, 4096).rearrange("p (c t) -> p c t", c=2)
        vpadA = scr(24576, 4096).rearrange("p (t v c) -> p t v c", t=16, v=2)
        vpadB = scr(41472, 4096).rearrange("p (t v c) -> p t v c", t=16, v=2)
        vpads = [vpadA, vpadB]
        vpad = vpadA
        spr = scr(28672, 2048).rearrange("p (s t) -> p s t", s=4)
        ssum = scr(30720, 3072).rearrange("p (s t) -> p s t", s=6)
        ebuf = scr(33792, 1536).rearrange("p (s t) -> p s t", s=3)
        wbuf = scr(35328, 2048).rearrange("p (s t) -> p s t", s=4)
        pbuf = scr(28672, 5120).rearrange("p (u h b t) -> p u h b t", u=2, h=2, b=5)
        lnden = scr(33792, 1024).bitcast(F32)
        rden = scr(34816, 1024).bitcast(F32)
        actb = scr(0, 16384).rearrange("p (g f t) -> p g f t", g=2, f=4)
        sgb = scr(16384, 4096).bitcast(F32).rearrange("p (s t) -> p s t", s=4)
        WB = scr(28672, 4096).rearrange("p (s t) -> p s t", s=4)
        sqb = scr(37376, 2048).rearrange("p (s t) -> p s t", s=4)
        lnv = scr(39424, 2048).bitcast(F32).rearrange("p (s t) -> p s t", s=2)

        ones = cst[:, C_ONES:C_ONES + 128]
        ident = cst[:, C_IDENT:C_IDENT + 128]
        negtri = cst[:, C_NEGTRI:C_NEGTRI + 128]
        negmask = cst[:, C_NEGMASK:C_NEGMASK + 128]
        spmask = cst[:, C_SPMASK:C_SPMASK + 128]
        onespad = [cst[:, C_ONESPAD + 128 * i:C_ONESPAD + 128 * (i + 1)] for i in range(2)]

        negones = cst[:, C_NEGONES:C_NEGONES + 128]
        zeros = cst[:, C_ZEROS:C_ZEROS + 128]

        Sx = Sched()
        add = Sx.add
        st = dict(sq=0, rs=0, pb=0)

        def PSK(b):
            return ("ps", b)

        add("sp", lambda e: e.dma_start(out=gains[:, :], in_=gains_d[:, :]), writes=["gains"], tag="gains")
        add("pool", lambda e: e.dma_start(out=cst[:, :], in_=cst_d[:, :], max_dma_last_dim=4096), writes=["cst"], tag="cst")
        add("pool", lambda e: e.dma_start(out=bm[:, :, :], in_=bm_d.rearrange("p (h t) -> p h t", h=8), max_dma_last_dim=4096),
            writes=["bm"], tag="bm")
        for kc in range(NKC):
            add("sp", (lambda kc: lambda e: e.dma_start(out=hT[:, kc, :], in_=xT_d[:, kc, :]))(kc),
                writes=[("h", kc, tg) for tg in range(NTG)], tag="h%d" % kc)
        for l in range(NL):
            add("act", (lambda l: lambda e: e.activation(
                out=sinkexp[:, 4 * l:4 * l + 4], in_=gains[:, l * G_PER_L + G_SINK:l * G_PER_L + G_SINK + 4], func=AF.Exp))(l),
                reads=["gains"], writes=["sinkexp%d" % l])

        add("pool", lambda e: e.memset(kz[:, :, :], 0.0), writes=["kzz"])
        add("pool", lambda e: e.memset(vpadA[:, :, :, :], 0.0), writes=["vpz"])
        add("pool", lambda e: e.memset(vpadB[:, :, :, :], 0.0), writes=["vpz2"])

        def rmsnorm(src, nk, gcol, dn, dst, order="kc"):
            for tg in range(NTG):
                bank = 4 + tg
                for kc in range(nk):
                    sl = st["sq"] % 4
                    st["sq"] += 1
                    sap, skey = src(kc, tg)
                    if kc % 2 == 0:
                        add("act", (lambda sap, sl: lambda e: e.activation(out=sqb[:, sl, :], in_=sap, func=AF.Square))(sap, sl),
                            reads=[skey], writes=[("sq", sl)])
                    else:
                        add("pool", (lambda sap, sl: lambda e: e.tensor_tensor(out=sqb[:, sl, :], in0=sap, in1=sap,
                                                                              op=ALU.mult))(sap, sl),
                            reads=[skey], writes=[("sq", sl)])
                    add("pe", (lambda sl, kc, bank: lambda e: e.matmul(PS[bank][:, :], lhsT=ones, rhs=sqb[:, sl, :],
                                                                         start=(kc == 0), stop=(kc == nk - 1)))(sl, kc, bank),
                        reads=[("sq", sl), "cst"], writes=[PSK(bank)])
                add("act", (lambda bank, tg: lambda e: e.activation(out=lnv[:, tg % 2, :], in_=PS[bank][:, :], func=AF.Ln,
                                                                    scale=1.0 / dn, bias=EPS))(bank, tg),
                    writes=[PSK(bank), ("lnv", tg % 2)])
                add("act", (lambda bank, tg: lambda e: e.activation(out=PS[bank][:, :], in_=lnv[:, tg % 2, :], func=AF.Exp,
                                                                    scale=-0.5))(bank, tg),
                    reads=[("lnv", tg % 2)], writes=[PSK(bank)])
            pairs = [(kc, tg) for kc in range(nk) for tg in range(NTG)] if order == "kc" else \
                [(kc, tg) for tg in range(NTG) for kc in range(nk)]
            for kc, tg in pairs:
                if True:
                    sap, skey = src(kc, tg)
                    dap, dkey = dst(kc, tg)
                    add("dve", (lambda sap, dap, kc, tg: lambda e: e.scalar_tensor_tensor(
                        out=dap, in0=sap, scalar=gains[:, gcol + kc:gcol + kc + 1], in1=PS[4 + tg][:, :],
                        op0=ALU.mult, op1=ALU.mult))(sap, dap, kc, tg),
                        reads=[skey, "gains"], writes=[PSK(4 + tg), dkey])

        def h_src(kc, tg):
            return hT[:, kc, tg * 512:(tg + 1) * 512], ("h", kc, tg)

        def n_dst(kc, tg):
            return nT[:, kc, tg * 512:(tg + 1) * 512], ("n", kc, tg)

        def ffn(l, f, gcol):
            wgu = wgu_d[(l, f)]
            wd = wd_d[(l, f)]
            Sx.barrier()
            rmsnorm(h_src, NKC, gcol, D, n_dst)

            def wa_view(slot):
                return WA[:, slot * 2048:(slot + 1) * 2048].rearrange("p (k h c) -> p k h c", k=8, h=2)

            def dma_wgu(fc):
                slot = fc % 4
                add("pool", lambda e: e.dma_start(out=WA[:, slot * 2048:(slot + 1) * 2048],
                                                  in_=wgu[:, fc * 2048:(fc + 1) * 2048], max_dma_last_dim=4096),
                    writes=[("WA", slot)], tag="WA%d" % slot)

            def dma_wd(fc):
                slot = fc % 4
                add("pool", lambda e: e.dma_start(out=WB[:, slot, :], in_=wd[:, fc * 1024:(fc + 1) * 1024],
                                                  max_dma_last_dim=4096),
                    writes=[("WB", slot)], tag="WB%d" % slot)

            def gu_chunk(g, fi, fc):
                slot = fc % 4
                wv = wa_view(slot)
                for half in range(2):
                    for kc in range(NKC):
                        for tg in range(NTG):
                            bank = half * 4 + tg
                            add("pe", (lambda kc, tg, bank, half: lambda e: e.matmul(
                                PS[bank][:, :], lhsT=wv[:, kc, half, :], rhs=nT[:, kc, tg * 512:(tg + 1) * 512],
                                start=(kc == 0), stop=(kc == NKC - 1)))(kc, tg, bank, half),
                                reads=[("WA", slot), ("n", kc, tg)], writes=[PSK(bank)])
                    if half == 0:
                        for tg in range(NTG):
                            add("act", (lambda tg: lambda e: e.activation(out=sgb[:, tg, :], in_=PS[tg][:, :], func=AF.Silu))(tg),
                                writes=[PSK(tg), ("sg", tg)])
                    else:
                        for tg in range(NTG):
                            add("dve", (lambda tg: lambda e: e.tensor_tensor(
                                out=actb[:, g % 2, fi, tg * 512:(tg + 1) * 512], in0=PS[4 + tg][:, :], in1=sgb[:, tg, :],
                                op=ALU.mult))(tg),
                                reads=[("sg", tg)], writes=[PSK(4 + tg), ("act", g % 2, fi, tg)])

            def down(g, nf, f0):
                for dc in range(NKC):
                    for fi in range(nf):
                        for tg in range(NTG):
                            bank = (dc % 2) * 4 + tg
                            add("pe", (lambda dc, fi, tg, bank: lambda e: e.matmul(
                                PS[bank][:, :], lhsT=WB[:, (f0 + fi) % 4, dc * 128:(dc + 1) * 128],
                                rhs=actb[:, g % 2, fi, tg * 512:(tg + 1) * 512],
                                start=(fi == 0), stop=(fi == nf - 1)))(dc, fi, tg, bank),
                                reads=[("WB", (f0 + fi) % 4), ("act", g % 2, fi, tg)], writes=[PSK(bank)])
                    for tg in range(NTG):
                        bank = (dc % 2) * 4 + tg
                        add("dve", (lambda dc, tg, bank: lambda e: e.scalar_tensor_tensor(
                            out=hT[:, dc, tg * 512:(tg + 1) * 512], in0=PS[bank][:, :], scalar=0.5,
                            in1=hT[:, dc, tg * 512:(tg + 1) * 512], op0=ALU.mult, op1=ALU.add))(dc, tg, bank),
                            reads=[], writes=[PSK(bank), ("h", dc, tg)])

            for fc in range(4):
                dma_wgu(fc)
            for fc in range(4):
                dma_wd(fc)
            for g, (f0, nf) in enumerate(FGROUPS):
                for fi in range(nf):
                    fc = f0 + fi
                    gu_chunk(g, fi, fc)
                    if fc + 4 < NFC:
                        dma_wgu(fc + 4)
                    if fi == 0 and g > 0:
                        pf0, pnf = FGROUPS[g - 1]
                        down(g - 1, pnf, pf0)
                        for k in range(nf):
                            dma_wd(f0 + k)
            lf0, lnf = FGROUPS[-1]
            down(len(FGROUPS) - 1, lnf, lf0)
            Sx.barrier()

        def mixer(l):
            gb = l * G_PER_L
            win = win_d[l]
            Sx.barrier()
            rmsnorm(h_src, NKC, gb + G_MIX, D, n_dst, order="tg")

            def nextbank():
                b = st["pb"] % 8
                st["pb"] += 1
                return b

            def dma_win(step):
                slot = step % 2
                if step < 4:
                    off, n = step * WIN_SB, WIN_SB
                else:
                    off, n = 4 * WIN_SB + (step - 4) * WIN_SW, WIN_SW
                add("pool", lambda e: e.dma_start(out=WA[:, slot * 4096:slot * 4096 + n], in_=win[:, off:off + n],
                                                  max_dma_last_dim=4096),
                    writes=[("WA", 2 * slot), ("WA", 2 * slot + 1)], tag="WA%d" % (2 * slot))

            def wstep(step):
                slot = step % 2
                ncol = 384 if step < 4 else 448
                return WA[:, slot * 4096:slot * 4096 + 8 * ncol].rearrange("p (k c) -> p k c", k=8), \
                    [("WA", 2 * slot), ("WA", 2 * slot + 1)]

            def proj_fm(wv, wkeys, c0, evac):
                base = 4 * (st["pb"] % 2)
                st["pb"] += 1
                for kc in range(NKC):
                    for tg in range(NTG):
                        bank = base + tg
                        add("pe", (lambda kc, tg, bank: lambda e: e.matmul(
                            PS[bank][:, :], lhsT=wv[:, kc, c0:c0 + 128], rhs=nT[:, kc, tg * 512:(tg + 1) * 512],
                            start=(kc == 0), stop=(kc == NKC - 1)))(kc, tg, bank),
                            reads=wkeys + [("n", kc, tg)], writes=[PSK(bank)])
                for tg in range(NTG):
                    evac(tg, base + tg)

            def evac_q(ci):
                def f(tg, bank):
                    add("dve", lambda e: e.tensor_copy(out=qbuf[:, ci, tg * 512:(tg + 1) * 512], in_=PS[bank][:, :]),
                        writes=[PSK(bank), ("q", ci, tg)])
                return f

            def evac_k(tg, bank):
                add("dve", lambda e: e.tensor_scalar(out=kz[0:64, 0, tg * 512:(tg + 1) * 512], in0=PS[bank][0:64, :],
                                                     scalar1=0.125, scalar2=None, op0=ALU.mult),
                    reads=["kzz"], writes=[PSK(bank), ("kz", 0, tg)])
                add("act", lambda e: e.activation(out=kz[64:128, 1, tg * 512:(tg + 1) * 512], in_=PS[bank][64:128, :],
                                                  func=AF.Copy, scale=0.125),
                    reads=["kzz"], writes=[PSK(bank), ("kz", 1, tg)])

            def gen_proj_v(wv, wkeys, c0, ncols, vb, banks):
                vdst = vpads[vb]
                for t4 in range(4):
                    bank = banks[t4 % len(banks)]
                    for ti in range(4):
                        tt = t4 * 4 + ti
                        for kc in range(NKC):
                            add("pe", (lambda kc, tt, ti, bank: lambda e: e.matmul(
                                PS[bank][:, ti * 128:ti * 128 + ncols], lhsT=nT[:, kc, tt * 128:(tt + 1) * 128],
                                rhs=wv[:, kc, c0:c0 + ncols], start=(kc == 0), stop=(kc == NKC - 1)))(kc, tt, ti, bank),
                                reads=wkeys + [("n", kc, tt // 4)], writes=[PSK(bank)])
                            if kc % 4 == 3:
                                yield
                    psv = PS[bank][:, :].rearrange("p (t c) -> p t c", t=4)
                    src1 = psv[:, :, 64:128] if ncols == 128 else psv[:, :, 0:64]
                    add("dve", (lambda t4, psv: lambda e: e.tensor_copy(
                        out=vdst[:, t4 * 4:(t4 + 1) * 4, 0, 0:64], in_=psv[:, :, 0:64]))(t4, psv),
                        reads=["vpz", "vpz2"], writes=[PSK(bank), ("vp", vb, 0, t4)])
                    add("dve", (lambda t4, src1: lambda e: e.tensor_copy(
                        out=vdst[:, t4 * 4:(t4 + 1) * 4, 1, 64:128], in_=src1))(t4, src1),
                        reads=["vpz", "vpz2"], writes=[PSK(bank), ("vp", vb, 1, t4)])
                    yield

            def proj_v(wv, wkeys, c0, ncols, vb=0):
                banks = [nextbank(), nextbank()]
                for _ in gen_proj_v(wv, wkeys, c0, ncols, vb, banks):
                    pass

            def proj_first(wv, wkeys, qi, vb):
                vdst = vpads[vb]
                cnt = [0]

                def nb():
                    b_ = cnt[0] % 4
                    cnt[0] += 1
                    return b_

                for tg in range(NTG):
                    for c0, ev in ((0, evac_q(qi)), (128, evac_k)):
                        bank = nb()
                        for kc in range(NKC):
                            add("pe", (lambda kc, tg, bank, c0: lambda e: e.matmul(
                                PS[bank][:, :], lhsT=wv[:, kc, c0:c0 + 128], rhs=nT[:, kc, tg * 512:(tg + 1) * 512],
                                start=(kc == 0), stop=(kc == NKC - 1)))(kc, tg, bank, c0),
                                reads=wkeys + [("n", kc, tg)], writes=[PSK(bank)])
                        ev(tg, bank)
                    bank = nb()
                    for ti in range(4):
                        tt = tg * 4 + ti
                        for kc in range(NKC):
                            add("pe", (lambda kc, tt, ti, bank: lambda e: e.matmul(
                                PS[bank][:, ti * 128:ti * 128 + 128], lhsT=nT[:, kc, tt * 128:(tt + 1) * 128],
                                rhs=wv[:, kc, 256:384], start=(kc == 0), stop=(kc == NKC - 1)))(kc, tt, ti, bank),
                                reads=wkeys + [("n", kc, tg)], writes=[PSK(bank)])
                    psv = PS[bank][:, :].rearrange("p (t c) -> p t c", t=4)
                    add("dve", (lambda tg, psv: lambda e: e.tensor_copy(
                        out=vdst[:, tg * 4:(tg + 1) * 4, 0, 0:64], in_=psv[:, :, 0:64]))(tg, psv),
                        reads=["vpz", "vpz2"], writes=[PSK(bank), ("vp", vb, 0, tg)])
                    add("dve", (lambda tg, psv: lambda e: e.tensor_copy(
                        out=vdst[:, tg * 4:(tg + 1) * 4, 1, 64:128], in_=psv[:, :, 64:128]))(tg, psv),
                        reads=["vpz", "vpz2"], writes=[PSK(bank), ("vp", vb, 1, tg)])

            def gen_proj_q(wv, wkeys, c0, qi, banks):
                for tg in range(NTG):
                    bank = banks[tg % len(banks)]
                    for kc in range(NKC):
                        add("pe", (lambda kc, tg, bank: lambda e: e.matmul(
                            PS[bank][:, :], lhsT=wv[:, kc, c0:c0 + 128], rhs=nT[:, kc, tg * 512:(tg + 1) * 512],
                            start=(kc == 0), stop=(kc == NKC - 1)))(kc, tg, bank),
                            reads=wkeys + [("n", kc, tg)], writes=[PSK(bank)])
                        if kc % 2 == 1:
                            yield
                    add("dve", (lambda tg, bank: lambda e: e.tensor_copy(
                        out=qbuf[:, qi, tg * 512:(tg + 1) * 512], in_=PS[bank][:, :]))(tg, bank),
                        writes=[PSK(bank), ("q", qi, tg)])
                    yield

            def sb_step(c):
                wv, wkeys = wstep(c)
                qi = c % 2
                vb = c % 2
                if c == 0:
                    proj_first(wv, wkeys, qi, vb)
                else:
                    proj_fm(wv, wkeys, 128, evac_k)
                if c + 2 < 6:
                    dma_win(c + 2)
                tiles = []
                zring = [2, 3, 4, 5]
                for g in range(4):
                    for par in range(2):
                        u = len([1 for t in tiles if t["first"]])
                        nt = 4 * g + 4
                        for j, b in enumerate(range(nt - 1, -1, -1)):
                            k = b - 4 * g
                            i = len(tiles)
                            tiles.append(dict(b=b, g=g, par=par, cs=max(k, 0) * 128, diag=(k >= 0), i=i, j=j,
                                              first=(j == 0), last=(j == nt - 1), sset=(u % 2) * 3,
                                              zb=zring[i % 4], es=i % 3, sps=i % 4, ws=i % 4, ob=g % 2))

                def s0(t):
                    b, cs, par, zb, es_ = t["b"], t["cs"], t["par"], t["zb"], t["es"]
                    q0 = t["g"] * 512
                    add("pe", lambda e: e.matmul(PS[zb][:, cs:512], lhsT=kz[:, par, b * 128:(b + 1) * 128],
                                                 rhs=qbuf[:, qi, q0 + cs:q0 + 512], start=True, stop=False),
                        reads=[("kz", par, b // 4), ("q", qi, t["g"])], writes=[PSK(zb)])
                    add("act", lambda e: e.activation(out=ebuf[:, es_, cs:512], in_=PS[zb][:, cs:512], func=AF.Exp),
                        writes=[PSK(zb), ("e", es_)])
                    if t["first"]:
                        ss0 = t["sset"]
                        add("dve", lambda e: e.memset(ssum[:, ss0:ss0 + 3, :], 0.0),
                            writes=[("ss", ss0), ("ss", ss0 + 1), ("ss", ss0 + 2)])
                        if par == 0:
                            ob = t["ob"]
                            add("pe", lambda e: e.matmul(PS[ob][:, :], lhsT=zeros, rhs=qbuf[:, qi, q0:q0 + 512],
                                                         start=True, stop=False),
                                reads=["cst", ("q", qi, t["g"])], writes=[PSK(ob)])

                def s1(t):
                    cs, es_, sps = t["cs"], t["es"], t["sps"]
                    add("act", lambda e: e.activation(out=spr[:, sps, cs:512], in_=ebuf[:, es_, cs:512], func=AF.Ln, bias=1.0),
                        reads=[("e", es_)], writes=[("spr", sps)])
                    if t["diag"]:
                        add("dve", lambda e: e.tensor_tensor(out=spr[:, sps, cs:cs + 128], in0=spr[:, sps, cs:cs + 128],
                                                             in1=spmask, op=ALU.mult),
                            reads=["cst"], writes=[("spr", sps)])

                def s2(t):
                    cs, zb, sps, ws, j = t["cs"], t["zb"], t["sps"], t["ws"], t["j"]
                    scur = t["sset"] + (j % 3)
                    snxt = t["sset"] + ((j + 1) % 3)
                    add("pe", lambda e: e.matmul(PS[zb][:, cs:512], lhsT=negtri, rhs=spr[:, sps, cs:512],
                                                 start=False, stop=(t["first"] and not t["diag"])),
                        reads=[("spr", sps), "cst"], writes=[PSK(zb)])
                    if not t["first"]:
                        add("pe", lambda e: e.matmul(PS[zb][:, cs:512], lhsT=negones, rhs=ssum[:, scur, cs:512],
                                                     start=False, stop=(not t["diag"])),
                            reads=[("ss", scur), "cst"], writes=[PSK(zb)])
                    if t["diag"]:
                        add("pe", lambda e: e.matmul(PS[zb][:, cs:cs + 128], lhsT=ident, rhs=negmask,
                                                     start=False, stop=True),
                            reads=["cst"], writes=[PSK(zb)])
                    add("act", lambda e: e.activation(out=wbuf[:, ws, cs:512], in_=PS[zb][:, cs:512], func=AF.Exp),
                        writes=[PSK(zb), ("w", ws)])
                    if not t["last"]:
                        add("dve", lambda e: e.tensor_tensor(out=ssum[:, snxt, cs:512], in0=ssum[:, scur, cs:512],
                                                             in1=spr[:, sps, cs:512], op=ALU.add),
                            reads=[("ss", scur), ("spr", sps)], writes=[("ss", snxt)])

                def s3(t):
                    b, cs, par, ws, ob, g = t["b"], t["cs"], t["par"], t["ws"], t["ob"], t["g"]
                    fin = (par == 1 and t["last"])
                    add("pe", lambda e: e.matmul(PS[ob][:, cs:512], lhsT=vpads[vb][:, b, par, :], rhs=wbuf[:, ws, cs:512],
                                                 start=False, stop=fin),
                        reads=[("w", ws), ("vp", vb, par, b // 4)], writes=[PSK(ob)])
                    if fin:
                        add("dve", lambda e: e.tensor_copy(out=oT[:, c, g * 512:(g + 1) * 512], in_=PS[ob][:, :]),
                            writes=[PSK(ob), ("o", c, g)])

                pipe = _pipeline(tiles, [s0, s1, s2, s3], [0, 1, 2, 4])
                if c + 1 < 4:
                    wv2, wkeys2 = wstep(c + 1)

                    def nxt():
                        yield from gen_proj_q(wv2, wkeys2, 0, (c + 1) % 2, [6, 7])
                        yield from gen_proj_v(wv2, wkeys2, 256, 128, (c + 1) % 2, [6, 7])

                    _interleave(pipe, nxt())
                else:
                    _interleave(pipe)

            def sw_step(j):
                step = 4 + j
                wv, wkeys = wstep(step)
                proj_fm(wv, wkeys, 0, evac_q(0))
                proj_fm(wv, wkeys, 128, evac_q(1))
                proj_fm(wv, wkeys, 256, evac_k)
                proj_v(wv, wkeys, 384, 64)
                if step + 2 < 6:
                    dma_win(step + 2)
                else:
                    if step == 5:
                        for hh in range(2):
                            add("pool", (lambda hh: lambda e: e.dma_start(
                                out=WA[:, hh * 4096:(hh + 1) * 4096], in_=wout_d[l][:, hh * 4096:(hh + 1) * 4096],
                                max_dma_last_dim=4096))(hh),
                                writes=[("WA", 2 * hh), ("WA", 2 * hh + 1)], tag="WA%d" % (2 * hh))

                def gen_score(ci, quad, u):
                    n0 = quad * 4
                    for par in range(2):
                        h = 2 * (2 * j + ci) + par
                        for bi, b in enumerate(range(n0 - 1, n0 + 4)):
                            if b < 0:
                                continue
                            if b == n0 - 1:
                                qlo, ncol, bmo = n0 * 128, 128, 128
                            elif b == n0 + 3:
                                qlo, ncol, bmo = b * 128, 128, 0
                            else:
                                qlo, ncol, bmo = b * 128, 256, 0
                            sbk = 3 + ((par * 5 + bi) % 3)
                            add("pe", (lambda b, qlo, ncol, sbk, par: lambda e: e.matmul(
                                PS[sbk][:, 0:ncol], lhsT=kz[:, par, b * 128:(b + 1) * 128],
                                rhs=qbuf[:, ci, qlo:qlo + ncol], start=True, stop=False))(b, qlo, ncol, sbk, par),
                                reads=[("kz", par, b // 4), ("q", ci, qlo // 512), ("q", ci, (qlo + ncol - 1) // 512)],
                                writes=[PSK(sbk)])
                            add("pe", (lambda ncol, sbk, bmo, h: lambda e: e.matmul(
                                PS[sbk][:, 0:ncol], lhsT=ident, rhs=bm[:, h, bmo:bmo + ncol],
                                start=False, stop=True))(ncol, sbk, bmo, h),
                                reads=["cst", "bm"], writes=[PSK(sbk)])
                            add("act", (lambda ncol, sbk, bmo, par, bi: lambda e: e.activation(
                                out=pbuf[:, u % 2, par, bi, bmo:bmo + ncol], in_=PS[sbk][:, 0:ncol], func=AF.Exp))(ncol, sbk, bmo, par, bi),
                                writes=[PSK(sbk), ("p", u % 2, par, bi)])
                            yield

                def gen_pv(ci, quad, u):
                    n0 = quad * 4
                    cidx = 2 * j + ci
                    ob = 1 + (u % 2)
                    db = 6 + (u % 2)
                    for which in range(2):
                        bank = ob if which == 0 else db
                        for qi in range(4):
                            n = n0 + qi
                            mms = []
                            for par in range(2):
                                if n >= 1:
                                    mms.append((par, n - 1, qi, 128))
                                mms.append((par, n, qi + 1, 0))
                            for mi, (par, kb, bi, po) in enumerate(mms):
                                lhs = vpad[:, kb, par, :] if which == 0 else onespad[par]
                                add("pe", (lambda lhs, par, bi, po, qi, mi, bank, nm: lambda e: e.matmul(
                                    PS[bank][:, qi * 128:(qi + 1) * 128], lhsT=lhs, rhs=pbuf[:, u % 2, par, bi, po:po + 128],
                                    start=(mi == 0), stop=(mi == nm - 1)))(lhs, par, bi, po, qi, mi, bank, len(mms)),
                                    reads=[("p", u % 2, par, bi), ("vp", 0, par, kb // 4), "cst"], writes=[PSK(bank)])
                            yield
                    col = 4 * l + cidx
                    add("act", lambda e: e.activation(out=lnden[:, :], in_=PS[db][:, :], func=AF.Ln,
                                                      bias=sinkexp[:, col:col + 1]),
                        reads=["sinkexp%d" % l], writes=[PSK(db), "lnden"])
                    add("act", lambda e: e.activation(out=rden[:, :], in_=lnden[:, :], func=AF.Exp, scale=-1.0),
                        reads=["lnden"], writes=["rden"])
                    add("dve", lambda e: e.tensor_tensor(out=oT[:, 4 + cidx, n0 * 128:n0 * 128 + 512], in0=PS[ob][:, :],
                                                         in1=rden[:, :], op=ALU.mult),
                        reads=["rden"], writes=[PSK(ob), ("o", 4 + cidx, quad)])
                    yield

                units = [(ci, quad) for ci in range(2) for quad in range(4)]
                prev = None
                for ui, (ci, quad) in enumerate(units):
                    u = j * 8 + ui
                    gs = gen_score(ci, quad, u)
                    if prev is None:
                        _interleave(gs)
                    else:
                        _interleave(prev, gs)
                    prev = gen_pv(ci, quad, u)
                _interleave(prev)

            dma_win(0)
            dma_win(1)
            for c in range(4):
                sb_step(c)
            def o_src(c0):
                return lambda kc, tg: (oT[:, c0 + kc, tg * 512:(tg + 1) * 512], ("o", c0 + kc, tg))

            Sx.barrier()
            sw_step(0)
            rmsnorm(o_src(0), 4, gb + G_OSB, 512, o_src(0))
            sw_step(1)
            Sx.barrier()
            rmsnorm(o_src(4), 4, gb + G_OSW, 512, o_src(4))
            wo = WA[:, :].rearrange("p (k c) -> p k c", k=8)
            for dc in range(NKC):
                for tg in range(NTG):
                    bank = nextbank()
                    for kc in range(NKC):
                        add("pe", (lambda dc, tg, kc, bank: lambda e: e.matmul(
                            PS[bank][:, :], lhsT=wo[:, kc, dc * 128:(dc + 1) * 128], rhs=oT[:, kc, tg * 512:(tg + 1) * 512],
                            start=(kc == 0), stop=(kc == NKC - 1)))(dc, tg, kc, bank),
                            reads=[("WA", kc // 2), ("o", kc, tg)], writes=[PSK(bank)])
                    add("dve", (lambda dc, tg, bank: lambda e: e.tensor_tensor(
                        out=hT[:, dc, tg * 512:(tg + 1) * 512], in0=PS[bank][:, :], in1=hT[:, dc, tg * 512:(tg + 1) * 512],
                        op=ALU.add))(dc, tg, bank),
                        writes=[PSK(bank), ("h", dc, tg)])
            Sx.barrier()

        for l in range(nlayers):
            gb = l * G_PER_L
            if "ffn1" in stages:
                ffn(l, 1, gb + G_FFN1)
            if "mix" in stages:
                mixer(l)
            if "ffn2" in stages:
                ffn(l, 2, gb + G_FFN2)
        Sx.barrier()
        if final_norm:
            rmsnorm(h_src, NKC, G_FINAL, D, h_src)
        for kc in range(NKC):
            add("sp", (lambda kc: lambda e: e.dma_start(out=outT_d[:, kc, :], in_=hT[:, kc, :]))(kc),
                reads=[("h", kc, tg) for tg in range(NTG)], tag="out%d" % kc)
        Sx.emit(nc)
    return nc


def _t5_bucket(dist):
    max_exact = 16
    d = np.maximum(dist, 1).astype(np.float32)
    large = max_exact + (np.log(d / np.float32(max_exact)) / np.float32(np.log(128 / max_exact))
                         * np.float32(32 - max_exact)).astype(np.int32)
    large = np.minimum(large, 31)
    return np.where(dist < max_exact, dist, large)


def _consts():
    c = np.zeros((128, NCST), np.float32)
    p = np.arange(128)[:, None]
    f = np.arange(128)[None, :]
    c[:, C_ONES:C_ONES + 128] = 1.0
    c[:, C_IDENT:C_IDENT + 128] = (p == f)
    c[:, C_NEGTRI:C_NEGTRI + 128] = -1.0 * (p >= f)
    c[:, C_NEGMASK:C_NEGMASK + 128] = NEG * (p >= f)
    c[:, C_SPMASK:C_SPMASK + 128] = 1.0 * (p < f)
    c[:, C_NEGONES:C_NEGONES + 128] = -1.0
    c[:, C_ONESPAD:C_ONESPAD + 64] = 1.0
    c[:, C_ONESPAD + 128 + 64:C_ONESPAD + 256] = 1.0
    return c


def _fm(v):
    return np.ascontiguousarray(v.reshape(-1, 128).T)


def _prep_shared(inp):
    sh = {}
    g = np.zeros((128, NGC), np.float32)
    for l in range(NL):
        b = l * G_PER_L
        g[:, b + G_FFN1:b + G_FFN1 + 8] = _fm(inp["norm_ffn1"][l])
        g[:, b + G_MIX:b + G_MIX + 8] = _fm(inp["norm_mix"][l])
        g[:, b + G_OSB:b + G_OSB + 4] = _fm(inp["norm_out_sb"][l])
        g[:, b + G_OSW:b + G_OSW + 4] = _fm(inp["norm_out_swa"][l])
        g[:, b + G_FFN2:b + G_FFN2 + 8] = _fm(inp["norm_ffn2"][l])
        g[:, b + G_SINK:b + G_SINK + 4] = np.repeat(inp["sinks"][l].reshape(4, 2).T, 64, axis=0)
    g[:, G_FINAL:G_FINAL + 8] = _fm(inp["norm_final"])
    sh["gains"] = g
    sh["cst"] = _consts()
    s_idx = np.arange(128)[:, None]
    a_idx = np.arange(128)[None, :]
    d_cur = a_idx - s_idx
    d_prev = 128 + a_idx - s_idx
    rb = inp["rel_bias"].astype(np.float32)
    bmm = np.zeros((128, 8, 256), np.float32)
    for h in range(8):
        cur = np.where(d_cur >= 0, rb[_t5_bucket(np.maximum(d_cur, 0)), h], np.float32(NEG))
        prev = np.where(d_prev < 128, rb[_t5_bucket(np.minimum(d_prev, 127)), h], np.float32(NEG))
        bmm[:, h, 0:128] = cur
        bmm[:, h, 128:256] = prev
    sh["bm"] = bmm.reshape(128, 8 * 256)
    for l in range(NL):
        for f, (kgu, kd) in ((1, ("w_ffn1_gu", "w_ffn1_down")), (2, ("w_ffn2_gu", "w_ffn2_down"))):
            W = inp[kgu][l]
            gu = np.stack([W[:, :DFF], W[:, DFF:]], axis=1)
            gu = gu.reshape(8, 128, 2, NFC, 128).transpose(1, 3, 0, 2, 4)
            sh["wgu%d_%d" % (f, l)] = np.ascontiguousarray(gu).reshape(128, NFC * 2048)
            Wd = inp[kd][l]
            sh["wd%d_%d" % (f, l)] = np.ascontiguousarray(Wd.reshape(NFC, 128, 1024).transpose(1, 0, 2)).reshape(128, NFC * 1024)
        Wi = inp["w_in"][l]
        parts = []
        for c in range(4):
            cols = np.concatenate([np.arange(c * 128, c * 128 + 128), 512 + np.arange(c * 128, c * 128 + 128),
                                   1024 + np.arange(c * 128, c * 128 + 128)])
            parts.append(Wi[:, cols].reshape(8, 128, 384).transpose(1, 0, 2).reshape(128, WIN_SB))
        for j in range(2):
            kc_ = 2048 + j * 64 + np.arange(64)
            cols = np.concatenate([1536 + j * 256 + np.arange(256), kc_, kc_, 2176 + j * 64 + np.arange(64)])
            parts.append(Wi[:, cols].reshape(8, 128, 448).transpose(1, 0, 2).reshape(128, WIN_SW))
        sh["win_%d" % l] = np.ascontiguousarray(np.concatenate(parts, axis=1))
        Wo = inp["w_out"][l]
        sh["wout_%d" % l] = np.ascontiguousarray(Wo.reshape(8, 128, 1024).transpose(1, 0, 2)).reshape(128, 8 * 1024)
    return sh


_CFG = dict(layers=NL, stages=("ffn1", "mix", "ffn2"), final_norm=True)


def run_cfg(inputs, cfg):
    inp = {k: np.asarray(v, dtype=np.float32) for k, v in inputs.items()}
    sh = _prep_shared(inp)
    x = inp["x"]
    B = x.shape[0]
    in_maps = []
    for b in range(B):
        m = dict(sh)
        m["xT"] = np.ascontiguousarray(x[b].T.reshape(NKC, 128, S).transpose(1, 0, 2))
        in_maps.append(m)
    nc = build_nc(cfg)
    res = run_bass_kernel_spmd(nc, in_maps, core_ids=list(range(B)))
    outs = []
    for b in range(B):
        oT_ = np.asarray(res.results[b]["outT"])
        outs.append(oT_.transpose(1, 0, 2).reshape(D, S).T)
    return np.ascontiguousarray(np.stack(outs, axis=0)).astype(np.float32)


def kernel(**inputs):
    return run_cfg(inputs, _CFG)
```

```python
import numpy as np
from contextlib import ExitStack

import concourse.bass as bass
import concourse.mybir as mybir
from concourse.bass_utils import run_bass_kernel_spmd

F32 = mybir.dt.float32
BF16 = mybir.dt.bfloat16
AF = mybir.ActivationFunctionType
ALU = mybir.AluOpType

D = 1024
S = 2048
DFF = 2816
NFC = DFF // 128
NKC = D // 128
NTG = S // 512
NL = 2
EPS = 1e-6
NEG = -30000.0
FGROUPS = [(0, 4), (4, 4), (8, 4), (12, 4), (16, 3), (19, 3)]

G_PER_L = 36
G_FFN1, G_MIX, G_OSB, G_OSW, G_FFN2, G_SINK = 0, 8, 16, 20, 24, 32
G_FINAL = NL * G_PER_L
NGC = G_FINAL + 8

C_ONES, C_IDENT, C_NEGTRI, C_NEGMASK, C_SPMASK, C_NEGONES, C_ZEROS, C_ONESPAD = 0, 128, 256, 384, 512, 640, 768, 896
NCST = 1152

WIN_SB = 8 * 384
WIN_SW = 8 * 448
WIN_TOT = 4 * WIN_SB + 2 * WIN_SW


class Sched:
    ENGS = ["pe", "act", "dve", "pool", "sp"]

    def __init__(self):
        self.ops = []
        self.last_w = {}
        self.readers = {}
        self.dma_count = {}
        self.bar = set()
        self.last_eng = {}
        self.last_tag = {}

    def add(self, eng, fn, reads=(), writes=(), tag=None):
        i = len(self.ops)
        deps = set(self.bar)
        for r in reads:
            if r in self.last_w:
                deps.add(self.last_w[r])
        for w in writes:
            if w in self.last_w:
                deps.add(self.last_w[w])
            deps.update(self.readers.get(w, ()))
        op = dict(eng=eng, fn=fn, deps=deps, tag=tag, cnt=None)
        if tag is not None:
            self.dma_count[tag] = self.dma_count.get(tag, 0) + 1
            op["dma_n"] = self.dma_count[tag]
            self.last_tag[tag] = i
        else:
            self.last_eng[eng] = i
        self.ops.append(op)
        for r in reads:
            self.readers.setdefault(r, []).append(i)
        for w in writes:
            self.last_w[w] = i
            self.readers[w] = []
        return i

    def barrier(self):
        self.bar = set(self.last_eng.values()) | set(self.last_tag.values())

    def emit(self, nc):
        ops = self.ops
        for op in ops:
            op["deps"] = {
                d for d in op["deps"]
                if not (ops[d]["eng"] == "pe" and op["eng"] == "pe"
                        and ops[d]["tag"] is None and op["tag"] is None)
            }
        needed = set()
        for op in ops:
            for d in op["deps"]:
                if ops[d]["tag"] is None:
                    needed.add(d)
        cnt = {e: 0 for e in self.ENGS}
        for i, op in enumerate(ops):
            if op["tag"] is None and i in needed:
                cnt[op["eng"]] += 1
                op["cnt"] = cnt[op["eng"]]
        tags = sorted(self.dma_count.keys())
        with ExitStack() as es:
            esem = {e: es.enter_context(nc.semaphore("s_" + e)) for e in self.ENGS}
            tsem = {t: es.enter_context(nc.semaphore("d_" + str(t))) for t in tags}
            block = es.enter_context(nc.Block())

            def run(eng_name, eng):
                seen = {}
                for op in ops:
                    if op["eng"] != eng_name:
                        continue
                    waits = {}
                    for d in op["deps"]:
                        dop = ops[d]
                        if dop["tag"] is not None:
                            key, val, sem = ("t", dop["tag"]), 16 * dop["dma_n"], tsem[dop["tag"]]
                        else:
                            key, val, sem = ("e", dop["eng"]), dop["cnt"], esem[dop["eng"]]
                        if seen.get(key, 0) < val and waits.get(key, (0, None))[0] < val:
                            waits[key] = (val, sem)
                    for key, (val, sem) in waits.items():
                        eng.wait_ge(sem, val)
                        seen[key] = val
                    ins = op["fn"](eng)
                    if op["tag"] is not None:
                        ins.then_inc(tsem[op["tag"]], 16)
                    elif op["cnt"] is not None:
                        ins.then_inc(esem[eng_name], 1)
                last = {}
                for op in ops:
                    if op["eng"] == eng_name and op["tag"] is not None:
                        last[op["tag"]] = max(last.get(op["tag"], 0), 16 * op["dma_n"])
                for t, v in last.items():
                    if seen.get(("t", t), 0) < v:
                        eng.wait_ge(tsem[t], v)

            @block.tensor
            def _(e):
                run("pe", e)

            @block.scalar
            def _(e):
                run("act", e)

            @block.vector
            def _(e):
                run("dve", e)

            @block.gpsimd
            def _(e):
                run("pool", e)

            @block.sync
            def _(e):
                run("sp", e)


def _interleave(*gens):
    gens = list(gens)
    while gens:
        for g in list(gens):
            try:
                next(g)
            except StopIteration:
                gens.remove(g)


def _pipeline(tiles, stages, lags):
    n = len(tiles)
    mx = max(lags)
    for s in range(n + mx):
        for st, lag in zip(stages, lags):
            i = s - lag
            if 0 <= i < n:
                st(tiles[i])
        yield


def build_nc(cfg):
    nlayers = cfg.get("layers", NL)
    stages = cfg.get("stages", ("ffn1", "mix", "ffn2"))
    final_norm = cfg.get("final_norm", True)

    nc = bass.Bass("TRN2", target_bir_lowering=False)
    xT_d = nc.dram_tensor("xT", [128, NKC, S], F32, kind="ExternalInput").ap()
    outT_d = nc.dram_tensor("outT", [128, NKC, S], F32, kind="ExternalOutput").ap()
    gains_d = nc.dram_tensor("gains", [128, NGC], F32, kind="ExternalInput").ap()
    cst_d = nc.dram_tensor("cst", [128, NCST], F32, kind="ExternalInput").ap()
    bm_d = nc.dram_tensor("bm", [128, 8 * 256], F32, kind="ExternalInput").ap()
    wgu_d, wd_d, win_d, wout_d = {}, {}, {}, {}
    for l in range(NL):
        for f in (1, 2):
            wgu_d[(l, f)] = nc.dram_tensor("wgu%d_%d" % (f, l), [128, NFC * 2048], F32, kind="ExternalInput").ap()
            wd_d[(l, f)] = nc.dram_tensor("wd%d_%d" % (f, l), [128, NFC * 1024], F32, kind="ExternalInput").ap()
        win_d[l] = nc.dram_tensor("win_%d" % l, [128, WIN_TOT], F32, kind="ExternalInput").ap()
        wout_d[l] = nc.dram_tensor("wout_%d" % l, [128, 8 * 1024], F32, kind="ExternalInput").ap()

    es = ExitStack()
    with es:
        hT = es.enter_context(nc.sbuf_tensor("hT", [128, NKC, S], F32))
        nT = es.enter_context(nc.sbuf_tensor("nT", [128, NKC, S], BF16))
        WA = es.enter_context(nc.sbuf_tensor("WA", [128, 8192], BF16))
        cst = es.enter_context(nc.sbuf_tensor("cst_sb", [128, NCST], BF16))
        bm = es.enter_context(nc.sbuf_tensor("bm_sb", [128, 8, 256], BF16))
        gains = es.enter_context(nc.sbuf_tensor("gains_sb", [128, NGC], F32))
        sinkexp = es.enter_context(nc.sbuf_tensor("sinkexp", [128, NL * 4], F32))
        SCR = es.enter_context(nc.sbuf_tensor("SCR", [128, 45568], BF16))
        PS = [es.enter_context(nc.psum_tensor("ps%d" % i, [128, 512], F32)) for i in range(8)]

        def scr(off, n):
            return SCR[:, off:off + n]

        oT = scr(0, 16384).rearrange("p (c t) -> p c t", c=8)
        qbuf = scr(16384, 4096).rearrange("p (c t) -> p c t", c=2)
        kz = scr(20480, 4096).rearrange("p (c t) -> p c t", c=2)
        vpadA = scr(24576, 4096).rearrange("p (t v c) -> p t v c", t=16, v=2)
        vpadB = scr(41472, 4096).rearrange("p (t v c) -> p t v c", t=16, v=2)
        vpads = [vpadA, vpadB]
        vpad = vpadA
        spr = scr(28672, 2048).rearrange("p (s t) -> p s t", s=4)
        ssum = scr(30720, 3072).rearrange("p (s t) -> p s t", s=6)
        ebuf = scr(33792, 1536).rearrange("p (s t) -> p s t", s=3)
        wbuf = scr(35328, 2048).rearrange("p (s t) -> p s t", s=4)
        pbuf = scr(28672, 5120).rearrange("p (u h b t) -> p u h b t", u=2, h=2, b=5)
        lnden = scr(33792, 1024).bitcast(F32)
        rden = scr(34816, 1024).bitcast(F32)
        actb = scr(0, 16384).rearrange("p (g f t) -> p g f t", g=2, f=4)
        sgb = scr(16384, 4096).bitcast(F32).rearrange("p (s t) -> p s t", s=4)
        WB = scr(28672, 4096).rearrange("p (s t) -> p s t", s=4)
        sqb = scr(37376, 2048).rearrange("p (s t) -> p s t", s=4)
        lnv = scr(39424, 2048).bitcast(F32).rearrange("p (s t) -> p s t", s=2)

        ones = cst[:, C_ONES:C_ONES + 128]
        ident = cst[:, C_IDENT:C_IDENT + 128]
        negtri = cst[:, C_NEGTRI:C_NEGTRI + 128]
        negmask = cst[:, C_NEGMASK:C_NEGMASK + 128]
        spmask = cst[:, C_SPMASK:C_SPMASK + 128]
        onespad = [cst[:, C_ONESPAD + 128 * i:C_ONESPAD + 128 * (i + 1)] for i in range(2)]

        negones = cst[:, C_NEGONES:C_NEGONES + 128]
        zeros = cst[:, C_ZEROS:C_ZEROS + 128]

        Sx = Sched()
        add = Sx.add
        st = dict(sq=0, rs=0, pb=0)

        def PSK(b):
            return ("ps", b)

        add("sp", lambda e: e.dma_start(out=gains[:, :], in_=gains_d[:, :]), writes=["gains"], tag="gains")
        add("pool", lambda e: e.dma_start(out=cst[:, :], in_=cst_d[:, :], max_dma_last_dim=4096), writes=["cst"], tag="cst")
        add("pool", lambda e: e.dma_start(out=bm[:, :, :], in_=bm_d.rearrange("p (h t) -> p h t", h=8), max_dma_last_dim=4096),
            writes=["bm"], tag="bm")
        for kc in range(NKC):
            add("sp", (lambda kc: lambda e: e.dma_start(out=hT[:, kc, :], in_=xT_d[:, kc, :]))(kc),
                writes=[("h", kc, tg) for tg in range(NTG)], tag="h%d" % kc)
        for l in range(NL):
            add("act", (lambda l: lambda e: e.activation(
                out=sinkexp[:, 4 * l:4 * l + 4], in_=gains[:, l * G_PER_L + G_SINK:l * G_PER_L + G_SINK + 4], func=AF.Exp))(l),
                reads=["gains"], writes=["sinkexp%d" % l])

        add("pool", lambda e: e.memset(kz[:, :, :], 0.0), writes=["kzz"])
        add("pool", lambda e: e.memset(vpadA[:, :, :, :], 0.0), writes=["vpz"])
        add("pool", lambda e: e.memset(vpadB[:, :, :, :], 0.0), writes=["vpz2"])

        def rmsnorm(src, nk, gcol, dn, dst, order="kc"):
            for tg in range(NTG):
                bank = 4 + tg
                for kc in range(nk):
                    sl = st["sq"] % 4
                    st["sq"] += 1
                    sap, skey = src(kc, tg)
                    if kc % 8 not in (1, 4, 6):
                        add("act", (lambda sap, sl: lambda e: e.activation(out=sqb[:, sl, :], in_=sap, func=AF.Square))(sap, sl),
                            reads=[skey], writes=[("sq", sl)])
                    else:
                        add("pool", (lambda sap, sl: lambda e: e.tensor_tensor(out=sqb[:, sl, :], in0=sap, in1=sap,
                                                                              op=ALU.mult))(sap, sl),
                            reads=[skey], writes=[("sq", sl)])
                    add("pe", (lambda sl, kc, bank: lambda e: e.matmul(PS[bank][:, :], lhsT=ones, rhs=sqb[:, sl, :],
                                                                         start=(kc == 0), stop=(kc == nk - 1)))(sl, kc, bank),
                        reads=[("sq", sl), "cst"], writes=[PSK(bank)])
                add("act", (lambda bank, tg: lambda e: e.activation(out=lnv[:, tg % 2, :], in_=PS[bank][:, :], func=AF.Ln,
                                                                    scale=1.0 / dn, bias=EPS))(bank, tg),
                    writes=[PSK(bank), ("lnv", tg % 2)])
                add("act", (lambda bank, tg: lambda e: e.activation(out=PS[bank][:, :], in_=lnv[:, tg % 2, :], func=AF.Exp,
                                                                    scale=-0.5))(bank, tg),
                    reads=[("lnv", tg % 2)], writes=[PSK(bank)])
            pairs = [(kc, tg) for kc in range(nk) for tg in range(NTG)] if order == "kc" else \
                [(kc, tg) for tg in range(NTG) for kc in range(nk)]
            for kc, tg in pairs:
                if True:
                    sap, skey = src(kc, tg)
                    dap, dkey = dst(kc, tg)
                    add("dve", (lambda sap, dap, kc, tg: lambda e: e.scalar_tensor_tensor(
                        out=dap, in0=sap, scalar=gains[:, gcol + kc:gcol + kc + 1], in1=PS[4 + tg][:, :],
                        op0=ALU.mult, op1=ALU.mult))(sap, dap, kc, tg),
                        reads=[skey, "gains"], writes=[PSK(4 + tg), dkey])

        def h_src(kc, tg):
            return hT[:, kc, tg * 512:(tg + 1) * 512], ("h", kc, tg)

        def n_dst(kc, tg):
            return nT[:, kc, tg * 512:(tg + 1) * 512], ("n", kc, tg)

        def ffn(l, f, gcol):
            wgu = wgu_d[(l, f)]
            wd = wd_d[(l, f)]
            Sx.barrier()
            rmsnorm(h_src, NKC, gcol, D, n_dst)

            def wa_view(slot):
                return WA[:, slot * 2048:(slot + 1) * 2048].rearrange("p (k h c) -> p k h c", k=8, h=2)

            def dma_wgu(fc):
                slot = fc % 4
                add("pool", lambda e: e.dma_start(out=WA[:, slot * 2048:(slot + 1) * 2048],
                                                  in_=wgu[:, fc * 2048:(fc + 1) * 2048], max_dma_last_dim=4096),
                    writes=[("WA", slot)], tag="WA%d" % slot)

            def dma_wd(fc):
                slot = fc % 4
                add("pool", lambda e: e.dma_start(out=WB[:, slot, :], in_=wd[:, fc * 1024:(fc + 1) * 1024],
                                                  max_dma_last_dim=4096),
                    writes=[("WB", slot)], tag="WB%d" % slot)

            def gu_chunk(g, fi, fc):
                slot = fc % 4
                wv = wa_view(slot)
                for half in range(2):
                    for kc in range(NKC):
                        for tg in range(NTG):
                            bank = half * 4 + tg
                            add("pe", (lambda kc, tg, bank, half: lambda e: e.matmul(
                                PS[bank][:, :], lhsT=wv[:, kc, half, :], rhs=nT[:, kc, tg * 512:(tg + 1) * 512],
                                start=(kc == 0), stop=(kc == NKC - 1)))(kc, tg, bank, half),
                                reads=[("WA", slot), ("n", kc, tg)], writes=[PSK(bank)])
                    if half == 0:
                        for tg in range(NTG):
                            add("act", (lambda tg: lambda e: e.activation(out=sgb[:, tg, :], in_=PS[tg][:, :], func=AF.Silu))(tg),
                                writes=[PSK(tg), ("sg", tg)])
                    else:
                        for tg in range(NTG):
                            add("dve", (lambda tg: lambda e: e.tensor_tensor(
                                out=actb[:, g % 2, fi, tg * 512:(tg + 1) * 512], in0=PS[4 + tg][:, :], in1=sgb[:, tg, :],
                                op=ALU.mult))(tg),
                                reads=[("sg", tg)], writes=[PSK(4 + tg), ("act", g % 2, fi, tg)])

            def down(g, nf, f0):
                for dc in range(NKC):
                    for fi in range(nf):
                        for tg in range(NTG):
                            bank = (dc % 2) * 4 + tg
                            add("pe", (lambda dc, fi, tg, bank: lambda e: e.matmul(
                                PS[bank][:, :], lhsT=WB[:, (f0 + fi) % 4, dc * 128:(dc + 1) * 128],
                                rhs=actb[:, g % 2, fi, tg * 512:(tg + 1) * 512],
                                start=(fi == 0), stop=(fi == nf - 1)))(dc, fi, tg, bank),
                                reads=[("WB", (f0 + fi) % 4), ("act", g % 2, fi, tg)], writes=[PSK(bank)])
                    for tg in range(NTG):
                        bank = (dc % 2) * 4 + tg
                        add("dve", (lambda dc, tg, bank: lambda e: e.scalar_tensor_tensor(
                            out=hT[:, dc, tg * 512:(tg + 1) * 512], in0=PS[bank][:, :], scalar=0.5,
                            in1=hT[:, dc, tg * 512:(tg + 1) * 512], op0=ALU.mult, op1=ALU.add))(dc, tg, bank),
                            reads=[], writes=[PSK(bank), ("h", dc, tg)])

            for fc in range(4):
                dma_wgu(fc)
            for fc in range(4):
                dma_wd(fc)
            for g, (f0, nf) in enumerate(FGROUPS):
                for fi in range(nf):
                    fc = f0 + fi
                    gu_chunk(g, fi, fc)
                    if fc + 4 < NFC:
                        dma_wgu(fc + 4)
                    if fi == 0 and g > 0:
                        pf0, pnf = FGROUPS[g - 1]
                        down(g - 1, pnf, pf0)
                        for k in range(nf):
                            dma_wd(f0 + k)
            lf0, lnf = FGROUPS[-1]
            down(len(FGROUPS) - 1, lnf, lf0)
            Sx.barrier()

        def mixer(l):
            gb = l * G_PER_L
            win = win_d[l]
            Sx.barrier()
            rmsnorm(h_src, NKC, gb + G_MIX, D, n_dst, order="tg")

            def nextbank():
                b = st["pb"] % 8
                st["pb"] += 1
                return b

            def dma_win(step):
                slot = step % 2
                if step < 4:
                    off, n = step * WIN_SB, WIN_SB
                else:
                    off, n = 4 * WIN_SB + (step - 4) * WIN_SW, WIN_SW
                add("pool", lambda e: e.dma_start(out=WA[:, slot * 4096:slot * 4096 + n], in_=win[:, off:off + n],
                                                  max_dma_last_dim=4096),
                    writes=[("WA", 2 * slot), ("WA", 2 * slot + 1)], tag="WA%d" % (2 * slot))

            def wstep(step):
                slot = step % 2
                ncol = 384 if step < 4 else 448
                return WA[:, slot * 4096:slot * 4096 + 8 * ncol].rearrange("p (k c) -> p k c", k=8), \
                    [("WA", 2 * slot), ("WA", 2 * slot + 1)]

            def proj_fm(wv, wkeys, c0, evac):
                base = 4 * (st["pb"] % 2)
                st["pb"] += 1
                for kc in range(NKC):
                    for tg in range(NTG):
                        bank = base + tg
                        add("pe", (lambda kc, tg, bank: lambda e: e.matmul(
                            PS[bank][:, :], lhsT=wv[:, kc, c0:c0 + 128], rhs=nT[:, kc, tg * 512:(tg + 1) * 512],
                            start=(kc == 0), stop=(kc == NKC - 1)))(kc, tg, bank),
                            reads=wkeys + [("n", kc, tg)], writes=[PSK(bank)])
                for tg in range(NTG):
                    evac(tg, base + tg)

            def evac_q(ci):
                def f(tg, bank):
                    add("dve", lambda e: e.tensor_copy(out=qbuf[:, ci, tg * 512:(tg + 1) * 512], in_=PS[bank][:, :]),
                        writes=[PSK(bank), ("q", ci, tg)])
                return f

            def evac_k(tg, bank):
                add("dve", lambda e: e.tensor_scalar(out=kz[0:64, 0, tg * 512:(tg + 1) * 512], in0=PS[bank][0:64, :],
                                                     scalar1=0.125, scalar2=None, op0=ALU.mult),
                    reads=["kzz"], writes=[PSK(bank), ("kz", 0, tg)])
                add("act", lambda e: e.activation(out=kz[64:128, 1, tg * 512:(tg + 1) * 512], in_=PS[bank][64:128, :],
                                                  func=AF.Copy, scale=0.125),
                    reads=["kzz"], writes=[PSK(bank), ("kz", 1, tg)])

            def gen_proj_v(wv, wkeys, c0, ncols, vb, banks):
                vdst = vpads[vb]
                for t4 in range(4):
                    bank = banks[t4 % len(banks)]
                    for ti in range(4):
                        tt = t4 * 4 + ti
                        for kc in range(NKC):
                            add("pe", (lambda kc, tt, ti, bank: lambda e: e.matmul(
                                PS[bank][:, ti * 128:ti * 128 + ncols], lhsT=nT[:, kc, tt * 128:(tt + 1) * 128],
                                rhs=wv[:, kc, c0:c0 + ncols], start=(kc == 0), stop=(kc == NKC - 1)))(kc, tt, ti, bank),
                                reads=wkeys + [("n", kc, tt // 4)], writes=[PSK(bank)])
                            if kc % 4 == 3:
                                yield
                    psv = PS[bank][:, :].rearrange("p (t c) -> p t c", t=4)
                    src1 = psv[:, :, 64:128] if ncols == 128 else psv[:, :, 0:64]
                    add("dve", (lambda t4, psv: lambda e: e.tensor_copy(
                        out=vdst[:, t4 * 4:(t4 + 1) * 4, 0, 0:64], in_=psv[:, :, 0:64]))(t4, psv),
                        reads=["vpz", "vpz2"], writes=[PSK(bank), ("vp", vb, 0, t4)])
                    add("dve", (lambda t4, src1: lambda e: e.tensor_copy(
                        out=vdst[:, t4 * 4:(t4 + 1) * 4, 1, 64:128], in_=src1))(t4, src1),
                        reads=["vpz", "vpz2"], writes=[PSK(bank), ("vp", vb, 1, t4)])
                    yield

            def proj_v(wv, wkeys, c0, ncols, vb=0):
                banks = [nextbank(), nextbank()]
                for _ in gen_proj_v(wv, wkeys, c0, ncols, vb, banks):
                    pass

            def proj_first(wv, wkeys, qi, vb):
                vdst = vpads[vb]
                cnt = [0]

                def nb():
                    b_ = cnt[0] % 4
                    cnt[0] += 1
                    return b_

                for tg in range(NTG):
                    for c0, ev in ((0, evac_q(qi)), (128, evac_k)):
                        bank = nb()
                        for kc in range(NKC):
                            add("pe", (lambda kc, tg, bank, c0: lambda e: e.matmul(
                                PS[bank][:, :], lhsT=wv[:, kc, c0:c0 + 128], rhs=nT[:, kc, tg * 512:(tg + 1) * 512],
                                start=(kc == 0), stop=(kc == NKC - 1)))(kc, tg, bank, c0),
                                reads=wkeys + [("n", kc, tg)], writes=[PSK(bank)])
                        ev(tg, bank)
                    bank = nb()
                    for ti in range(4):
                        tt = tg * 4 + ti
                        for kc in range(NKC):
                            add("pe", (lambda kc, tt, ti, bank: lambda e: e.matmul(
                                PS[bank][:, ti * 128:ti * 128 + 128], lhsT=nT[:, kc, tt * 128:(tt + 1) * 128],
                                rhs=wv[:, kc, 256:384], start=(kc == 0), stop=(kc == NKC - 1)))(kc, tt, ti, bank),
                                reads=wkeys + [("n", kc, tg)], writes=[PSK(bank)])
                    psv = PS[bank][:, :].rearrange("p (t c) -> p t c", t=4)
                    add("dve", (lambda tg, psv: lambda e: e.tensor_copy(
                        out=vdst[:, tg * 4:(tg + 1) * 4, 0, 0:64], in_=psv[:, :, 0:64]))(tg, psv),
                        reads=["vpz", "vpz2"], writes=[PSK(bank), ("vp", vb, 0, tg)])
                    add("dve", (lambda tg, psv: lambda e: e.tensor_copy(
                        out=vdst[:, tg * 4:(tg + 1) * 4, 1, 64:128], in_=psv[:, :, 64:128]))(tg, psv),
                        reads=["vpz", "vpz2"], writes=[PSK(bank), ("vp", vb, 1, tg)])

            def gen_proj_q(wv, wkeys, c0, qi, banks):
                for tg in range(NTG):
                    bank = banks[tg % len(banks)]
                    for kc in range(NKC):
                        add("pe", (lambda kc, tg, bank: lambda e: e.matmul(
                            PS[bank][:, :], lhsT=wv[:, kc, c0:c0 + 128], rhs=nT[:, kc, tg * 512:(tg + 1) * 512],
                            start=(kc == 0), stop=(kc == NKC - 1)))(kc, tg, bank),
                            reads=wkeys + [("n", kc, tg)], writes=[PSK(bank)])
                        if kc % 2 == 1:
                            yield
                    add("dve", (lambda tg, bank: lambda e: e.tensor_copy(
                        out=qbuf[:, qi, tg * 512:(tg + 1) * 512], in_=PS[bank][:, :]))(tg, bank),
                        writes=[PSK(bank), ("q", qi, tg)])
                    yield

            def sb_step(c):
                wv, wkeys = wstep(c)
                qi = c % 2
                vb = c % 2
                if c == 0:
                    proj_first(wv, wkeys, qi, vb)
                else:
                    proj_fm(wv, wkeys, 128, evac_k)
                if c + 2 < 6:
                    dma_win(c + 2)
                tiles = []
                zring = [2, 3, 4, 5]
                for g in range(4):
                    for par in range(2):
                        u = len([1 for t in tiles if t["first"]])
                        nt = 4 * g + 4
                        for j, b in enumerate(range(nt - 1, -1, -1)):
                            k = b - 4 * g
                            i = len(tiles)
                            tiles.append(dict(b=b, g=g, par=par, cs=max(k, 0) * 128, diag=(k >= 0), i=i, j=j,
                                              first=(j == 0), last=(j == nt - 1), sset=(u % 2) * 3,
                                              zb=zring[i % 4], es=i % 3, sps=i % 4, ws=i % 4, ob=g % 2))

                def s0(t):
                    b, cs, par, zb, es_ = t["b"], t["cs"], t["par"], t["zb"], t["es"]
                    q0 = t["g"] * 512
                    add("pe", lambda e: e.matmul(PS[zb][:, cs:512], lhsT=kz[:, par, b * 128:(b + 1) * 128],
                                                 rhs=qbuf[:, qi, q0 + cs:q0 + 512], start=True, stop=False),
                        reads=[("kz", par, b // 4), ("q", qi, t["g"])], writes=[PSK(zb)])
                    add("act", lambda e: e.activation(out=ebuf[:, es_, cs:512], in_=PS[zb][:, cs:512], func=AF.Exp),
                        writes=[PSK(zb), ("e", es_)])
                    if t["first"]:
                        ss0 = t["sset"]
                        add("dve", lambda e: e.memset(ssum[:, ss0:ss0 + 3, :], 0.0),
                            writes=[("ss", ss0), ("ss", ss0 + 1), ("ss", ss0 + 2)])
                        if par == 0:
                            ob = t["ob"]
                            add("pe", lambda e: e.matmul(PS[ob][:, :], lhsT=zeros, rhs=qbuf[:, qi, q0:q0 + 512],
                                                         start=True, stop=False),
                                reads=["cst", ("q", qi, t["g"])], writes=[PSK(ob)])

                def s1(t):
                    cs, es_, sps = t["cs"], t["es"], t["sps"]
                    add("act", lambda e: e.activation(out=spr[:, sps, cs:512], in_=ebuf[:, es_, cs:512], func=AF.Ln, bias=1.0),
                        reads=[("e", es_)], writes=[("spr", sps)])
                    if t["diag"]:
                        add("dve", lambda e: e.tensor_tensor(out=spr[:, sps, cs:cs + 128], in0=spr[:, sps, cs:cs + 128],
                                                             in1=spmask, op=ALU.mult),
                            reads=["cst"], writes=[("spr", sps)])

                def s2(t):
                    cs, zb, sps, ws, j = t["cs"], t["zb"], t["sps"], t["ws"], t["j"]
                    scur = t["sset"] + (j % 3)
                    snxt = t["sset"] + ((j + 1) % 3)
                    add("pe", lambda e: e.matmul(PS[zb][:, cs:512], lhsT=negtri, rhs=spr[:, sps, cs:512],
                                                 start=False, stop=(t["first"] and not t["diag"])),
                        reads=[("spr", sps), "cst"], writes=[PSK(zb)])
                    if not t["first"]:
                        add("pe", lambda e: e.matmul(PS[zb][:, cs:512], lhsT=negones, rhs=ssum[:, scur, cs:512],
                                                     start=False, stop=(not t["diag"])),
                            reads=[("ss", scur), "cst"], writes=[PSK(zb)])
                    if t["diag"]:
                        add("pe", lambda e: e.matmul(PS[zb][:, cs:cs + 128], lhsT=ident, rhs=negmask,
                                                     start=False, stop=True),
                            reads=["cst"], writes=[PSK(zb)])
                    add("act", lambda e: e.activation(out=wbuf[:, ws, cs:512], in_=PS[zb][:, cs:512], func=AF.Exp),
                        writes=[PSK(zb), ("w", ws)])
                    if not t["last"]:
                        add("dve", lambda e: e.tensor_tensor(out=ssum[:, snxt, cs:512], in0=ssum[:, scur, cs:512],
                                                             in1=spr[:, sps, cs:512], op=ALU.add),
                            reads=[("ss", scur), ("spr", sps)], writes=[("ss", snxt)])

                def s3(t):
                    b, cs, par, ws, ob, g = t["b"], t["cs"], t["par"], t["ws"], t["ob"], t["g"]
                    fin = (par == 1 and t["last"])
                    add("pe", lambda e: e.matmul(PS[ob][:, cs:512], lhsT=vpads[vb][:, b, par, :], rhs=wbuf[:, ws, cs:512],
                                                 start=False, stop=fin),
                        reads=[("w", ws), ("vp", vb, par, b // 4)], writes=[PSK(ob)])
                    if fin:
                        add("dve", lambda e: e.tensor_copy(out=oT[:, c, g * 512:(g + 1) * 512], in_=PS[ob][:, :]),
                            writes=[PSK(ob), ("o", c, g)])

                pipe = _pipeline(tiles, [s0, s1, s2, s3], [0, 1, 2, 4])
                if c + 1 < 4:
                    wv2, wkeys2 = wstep(c + 1)

                    def nxt():
                        yield from gen_proj_q(wv2, wkeys2, 0, (c + 1) % 2, [6, 7])
                        yield from gen_proj_v(wv2, wkeys2, 256, 128, (c + 1) % 2, [6, 7])

                    _interleave(pipe, nxt())
                else:
                    wv2, wkeys2 = wstep(4)

                    def nxt4():
                        yield from gen_proj_q(wv2, wkeys2, 0, 0, [6, 7])
                        yield from gen_proj_v(wv2, wkeys2, 384, 64, 0, [6, 7])

                    _interleave(pipe, nxt4())

            def sw_step(j):
                step = 4 + j
                wv, wkeys = wstep(step)
                if j > 0:
                    proj_fm(wv, wkeys, 0, evac_q(0))
                proj_fm(wv, wkeys, 128, evac_q(1))
                proj_fm(wv, wkeys, 256, evac_k)
                if j > 0:
                    proj_v(wv, wkeys, 384, 64)
                if step + 2 < 6:
                    dma_win(step + 2)
                else:
                    if step == 5:
                        for hh in range(2):
                            add("pool", (lambda hh: lambda e: e.dma_start(
                                out=WA[:, hh * 4096:(hh + 1) * 4096], in_=wout_d[l][:, hh * 4096:(hh + 1) * 4096],
                                max_dma_last_dim=4096))(hh),
                                writes=[("WA", 2 * hh), ("WA", 2 * hh + 1)], tag="WA%d" % (2 * hh))

                def gen_score(ci, quad, u):
                    n0 = quad * 4
                    for par in range(2):
                        h = 2 * (2 * j + ci) + par
                        for bi, b in enumerate(range(n0 - 1, n0 + 4)):
                            if b < 0:
                                continue
                            if b == n0 - 1:
                                qlo, ncol, bmo = n0 * 128, 128, 128
                            elif b == n0 + 3:
                                qlo, ncol, bmo = b * 128, 128, 0
                            else:
                                qlo, ncol, bmo = b * 128, 256, 0
                            sbk = 3 + ((par * 5 + bi) % 3)
                            add("pe", (lambda b, qlo, ncol, sbk, par: lambda e: e.matmul(
                                PS[sbk][:, 0:ncol], lhsT=kz[:, par, b * 128:(b + 1) * 128],
                                rhs=qbuf[:, ci, qlo:qlo + ncol], start=True, stop=False))(b, qlo, ncol, sbk, par),
                                reads=[("kz", par, b // 4), ("q", ci, qlo // 512), ("q", ci, (qlo + ncol - 1) // 512)],
                                writes=[PSK(sbk)])
                            add("pe", (lambda ncol, sbk, bmo, h: lambda e: e.matmul(
                                PS[sbk][:, 0:ncol], lhsT=ident, rhs=bm[:, h, bmo:bmo + ncol],
                                start=False, stop=True))(ncol, sbk, bmo, h),
                                reads=["cst", "bm"], writes=[PSK(sbk)])
                            add("act", (lambda ncol, sbk, bmo, par, bi: lambda e: e.activation(
                                out=pbuf[:, u % 2, par, bi, bmo:bmo + ncol], in_=PS[sbk][:, 0:ncol], func=AF.Exp))(ncol, sbk, bmo, par, bi),
                                writes=[PSK(sbk), ("p", u % 2, par, bi)])
                            yield

                def gen_pv(ci, quad, u):
                    n0 = quad * 4
                    cidx = 2 * j + ci
                    ob = 1 + (u % 2)
                    db = 6 + (u % 2)
                    for which in range(2):
                        bank = ob if which == 0 else db
                        for qi in range(4):
                            n = n0 + qi
                            mms = []
                            for par in range(2):
                                if n >= 1:
                                    mms.append((par, n - 1, qi, 128))
                                mms.append((par, n, qi + 1, 0))
                            for mi, (par, kb, bi, po) in enumerate(mms):
                                lhs = vpad[:, kb, par, :] if which == 0 else onespad[par]
                                add("pe", (lambda lhs, par, bi, po, qi, mi, bank, nm: lambda e: e.matmul(
                                    PS[bank][:, qi * 128:(qi + 1) * 128], lhsT=lhs, rhs=pbuf[:, u % 2, par, bi, po:po + 128],
                                    start=(mi == 0), stop=(mi == nm - 1)))(lhs, par, bi, po, qi, mi, bank, len(mms)),
                                    reads=[("p", u % 2, par, bi), ("vp", 0, par, kb // 4), "cst"], writes=[PSK(bank)])
                            yield
                    col = 4 * l + cidx
                    add("act", lambda e: e.activation(out=lnden[:, :], in_=PS[db][:, :], func=AF.Ln,
                                                      bias=sinkexp[:, col:col + 1]),
                        reads=["sinkexp%d" % l], writes=[PSK(db), "lnden"])
                    add("act", lambda e: e.activation(out=rden[:, :], in_=lnden[:, :], func=AF.Exp, scale=-1.0),
                        reads=["lnden"], writes=["rden"])
                    add("dve", lambda e: e.tensor_tensor(out=oT[:, 4 + cidx, n0 * 128:n0 * 128 + 512], in0=PS[ob][:, :],
                                                         in1=rden[:, :], op=ALU.mult),
                        reads=["rden"], writes=[PSK(ob), ("o", 4 + cidx, quad)])
                    yield

                units = [(ci, quad) for ci in range(2) for quad in range(4)]
                prev = None
                for ui, (ci, quad) in enumerate(units):
                    u = j * 8 + ui
                    gs = gen_score(ci, quad, u)
                    if prev is None:
                        _interleave(gs)
                    else:
                        _interleave(prev, gs)
                    prev = gen_pv(ci, quad, u)
                _interleave(prev)

            dma_win(0)
            dma_win(1)
            for c in range(4):
                sb_step(c)
            def o_src(c0):
                return lambda kc, tg: (oT[:, c0 + kc, tg * 512:(tg + 1) * 512], ("o", c0 + kc, tg))

            Sx.barrier()
            sw_step(0)
            rmsnorm(o_src(0), 4, gb + G_OSB, 512, o_src(0))
            sw_step(1)
            Sx.barrier()
            rmsnorm(o_src(4), 4, gb + G_OSW, 512, o_src(4))
            wo = WA[:, :].rearrange("p (k c) -> p k c", k=8)
            for dc in range(NKC):
                for tg in range(NTG):
                    bank = nextbank()
                    for kc in range(NKC):
                        add("pe", (lambda dc, tg, kc, bank: lambda e: e.matmul(
                            PS[bank][:, :], lhsT=wo[:, kc, dc * 128:(dc + 1) * 128], rhs=oT[:, kc, tg * 512:(tg + 1) * 512],
                            start=(kc == 0), stop=(kc == NKC - 1)))(dc, tg, kc, bank),
                            reads=[("WA", kc // 2), ("o", kc, tg)], writes=[PSK(bank)])
                    add("dve", (lambda dc, tg, bank: lambda e: e.tensor_tensor(
                        out=hT[:, dc, tg * 512:(tg + 1) * 512], in0=PS[bank][:, :], in1=hT[:, dc, tg * 512:(tg + 1) * 512],
                        op=ALU.add))(dc, tg, bank),
                        writes=[PSK(bank), ("h", dc, tg)])
            Sx.barrier()

        for l in range(nlayers):
            gb = l * G_PER_L
            if "ffn1" in stages:
                ffn(l, 1, gb + G_FFN1)
            if "mix" in stages:
                mixer(l)
            if "ffn2" in stages:
                ffn(l, 2, gb + G_FFN2)
        Sx.barrier()
        if final_norm:
            rmsnorm(h_src, NKC, G_FINAL, D, h_src)
        for kc in range(NKC):
            add("sp", (lambda kc: lambda e: e.dma_start(out=outT_d[:, kc, :], in_=hT[:, kc, :]))(kc),
                reads=[("h", kc, tg) for tg in range(NTG)], tag="out%d" % kc)
        Sx.emit(nc)
    return nc


def _t5_bucket(dist):
    max_exact = 16
    d = np.maximum(dist, 1).astype(np.float32)
    large = max_exact + (np.log(d / np.float32(max_exact)) / np.float32(np.log(128 / max_exact))
                         * np.float32(32 - max_exact)).astype(np.int32)
    large = np.minimum(large, 31)
    return np.where(dist < max_exact, dist, large)


def _consts():
    c = np.zeros((128, NCST), np.float32)
    p = np.arange(128)[:, None]
    f = np.arange(128)[None, :]
    c[:, C_ONES:C_ONES + 128] = 1.0
    c[:, C_IDENT:C_IDENT + 128] = (p == f)
    c[:, C_NEGTRI:C_NEGTRI + 128] = -1.0 * (p >= f)
    c[:, C_NEGMASK:C_NEGMASK + 128] = NEG * (p >= f)
    c[:, C_SPMASK:C_SPMASK + 128] = 1.0 * (p < f)
    c[:, C_NEGONES:C_NEGONES + 128] = -1.0
    c[:, C_ONESPAD:C_ONESPAD + 64] = 1.0
    c[:, C_ONESPAD + 128 + 64:C_ONESPAD + 256] = 1.0
    return c


def _fm(v):
    return np.ascontiguousarray(v.reshape(-1, 128).T)


def _prep_shared(inp):
    sh = {}
    g = np.zeros((128, NGC), np.float32)
    for l in range(NL):
        b = l * G_PER_L
        g[:, b + G_FFN1:b + G_FFN1 + 8] = _fm(inp["norm_ffn1"][l])
        g[:, b + G_MIX:b + G_MIX + 8] = _fm(inp["norm_mix"][l])
        g[:, b + G_OSB:b + G_OSB + 4] = _fm(inp["norm_out_sb"][l])
        g[:, b + G_OSW:b + G_OSW + 4] = _fm(inp["norm_out_swa"][l])
        g[:, b + G_FFN2:b + G_FFN2 + 8] = _fm(inp["norm_ffn2"][l])
        g[:, b + G_SINK:b + G_SINK + 4] = np.repeat(inp["sinks"][l].reshape(4, 2).T, 64, axis=0)
    g[:, G_FINAL:G_FINAL + 8] = _fm(inp["norm_final"])
    sh["gains"] = g
    sh["cst"] = _consts()
    s_idx = np.arange(128)[:, None]
    a_idx = np.arange(128)[None, :]
    d_cur = a_idx - s_idx
    d_prev = 128 + a_idx - s_idx
    rb = inp["rel_bias"].astype(np.float32)
    bmm = np.zeros((128, 8, 256), np.float32)
    for h in range(8):
        cur = np.where(d_cur >= 0, rb[_t5_bucket(np.maximum(d_cur, 0)), h], np.float32(NEG))
        prev = np.where(d_prev < 128, rb[_t5_bucket(np.minimum(d_prev, 127)), h], np.float32(NEG))
        bmm[:, h, 0:128] = cur
        bmm[:, h, 128:256] = prev
    sh["bm"] = bmm.reshape(128, 8 * 256)
    for l in range(NL):
        for f, (kgu, kd) in ((1, ("w_ffn1_gu", "w_ffn1_down")), (2, ("w_ffn2_gu", "w_ffn2_down"))):
            W = inp[kgu][l]
            gu = np.stack([W[:, :DFF], W[:, DFF:]], axis=1)
            gu = gu.reshape(8, 128, 2, NFC, 128).transpose(1, 3, 0, 2, 4)
            sh["wgu%d_%d" % (f, l)] = np.ascontiguousarray(gu).reshape(128, NFC * 2048)
            Wd = inp[kd][l]
            sh["wd%d_%d" % (f, l)] = np.ascontiguousarray(Wd.reshape(NFC, 128, 1024).transpose(1, 0, 2)).reshape(128, NFC * 1024)
        Wi = inp["w_in"][l]
        parts = []
        for c in range(4):
            cols = np.concatenate([np.arange(c * 128, c * 128 + 128), 512 + np.arange(c * 128, c * 128 + 128),
                                   1024 + np.arange(c * 128, c * 128 + 128)])
            parts.append(Wi[:, cols].reshape(8, 128, 384).transpose(1, 0, 2).reshape(128, WIN_SB))
        for j in range(2):
            kc_ = 2048 + j * 64 + np.arange(64)
            cols = np.concatenate([1536 + j * 256 + np.arange(256), kc_, kc_, 2176 + j * 64 + np.arange(64)])
            parts.append(Wi[:, cols].reshape(8, 128, 448).transpose(1, 0, 2).reshape(128, WIN_SW))
        sh["win_%d" % l] = np.ascontiguousarray(np.concatenate(parts, axis=1))
        Wo = inp["w_out"][l]
        sh["wout_%d" % l] = np.ascontiguousarray(Wo.reshape(8, 128, 1024).transpose(1, 0, 2)).reshape(128, 8 * 1024)
    return sh


_CFG = dict(layers=NL, stages=("ffn1", "mix", "ffn2"), final_norm=True)


def run_cfg(inputs, cfg):
    inp = {k: np.asarray(v, dtype=np.float32) for k, v in inputs.items()}
    sh = _prep_shared(inp)
    x = inp["x"]
    B = x.shape[0]
    in_maps = []
    for b in range(B):
        m = dict(sh)
        m["xT"] = np.ascontiguousarray(x[b].T.reshape(NKC, 128, S).transpose(1, 0, 2))
        in_maps.append(m)
    nc = build_nc(cfg)
    res = run_bass_kernel_spmd(nc, in_maps, core_ids=list(range(B)))
    outs = []
    for b in range(B):
        oT_ = np.asarray(res.results[b]["outT"])
        outs.append(oT_.transpose(1, 0, 2).reshape(D, S).T)
    return np.ascontiguousarray(np.stack(outs, axis=0)).astype(np.float32)


def kernel(**inputs):
    return run_cfg(inputs, _CFG)
```

```python
import numpy as np
from contextlib import ExitStack

import concourse.bass as bass
import concourse.mybir as mybir
from concourse.bass_utils import run_bass_kernel_spmd

F32 = mybir.dt.float32
BF16 = mybir.dt.bfloat16
AF = mybir.ActivationFunctionType
ALU = mybir.AluOpType

D = 1024
S = 2048
DFF = 2816
NFC = DFF // 128
NKC = D // 128
NTG = S // 512
NL = 2
EPS = 1e-6
NEG = -30000.0
FGROUPS = [(0, 4), (4, 4), (8, 4), (12, 4), (16, 3), (19, 3)]

G_PER_L = 36
G_FFN1, G_MIX, G_OSB, G_OSW, G_FFN2, G_SINK = 0, 8, 16, 20, 24, 32
G_FINAL = NL * G_PER_L
NGC = G_FINAL + 8

C_ONES, C_IDENT, C_NEGTRI, C_NEGMASK, C_SPMASK, C_NEGONES, C_ZEROS, C_ONESPAD = 0, 128, 256, 384, 512, 640, 768, 896
NCST = 1152

WIN_SB = 8 * 384
WIN_SW = 8 * 448
WIN_TOT = 4 * WIN_SB + 2 * WIN_SW


class Sched:
    ENGS = ["pe", "act", "dve", "pool", "sp"]

    def __init__(self):
        self.ops = []
        self.last_w = {}
        self.readers = {}
        self.dma_count = {}
        self.bar = set()
        self.last_eng = {}
        self.last_tag = {}

    def add(self, eng, fn, reads=(), writes=(), tag=None):
        i = len(self.ops)
        deps = set(self.bar)
        for r in reads:
            if r in self.last_w:
                deps.add(self.last_w[r])
        for w in writes:
            if w in self.last_w:
                deps.add(self.last_w[w])
            deps.update(self.readers.get(w, ()))
        op = dict(eng=eng, fn=fn, deps=deps, tag=tag, cnt=None)
        if tag is not None:
            self.dma_count[tag] = self.dma_count.get(tag, 0) + 1
            op["dma_n"] = self.dma_count[tag]
            self.last_tag[tag] = i
        else:
            self.last_eng[eng] = i
        self.ops.append(op)
        for r in reads:
            self.readers.setdefault(r, []).append(i)
        for w in writes:
            self.last_w[w] = i
            self.readers[w] = []
        return i

    def barrier(self):
        self.bar = set(self.last_eng.values()) | set(self.last_tag.values())

    def emit(self, nc):
        ops = self.ops
        for op in ops:
            op["deps"] = {
                d for d in op["deps"]
                if not (ops[d]["eng"] == "pe" and op["eng"] == "pe"
                        and ops[d]["tag"] is None and op["tag"] is None)
            }
        needed = set()
        for op in ops:
            for d in op["deps"]:
                if ops[d]["tag"] is None:
                    needed.add(d)
        cnt = {e: 0 for e in self.ENGS}
        for i, op in enumerate(ops):
            if op["tag"] is None and i in needed:
                cnt[op["eng"]] += 1
                op["cnt"] = cnt[op["eng"]]
        tags = sorted(self.dma_count.keys())
        with ExitStack() as es:
            esem = {e: es.enter_context(nc.semaphore("s_" + e)) for e in self.ENGS}
            tsem = {t: es.enter_context(nc.semaphore("d_" + str(t))) for t in tags}
            block = es.enter_context(nc.Block())

            def run(eng_name, eng):
                seen = {}
                for op in ops:
                    if op["eng"] != eng_name:
                        continue
                    waits = {}
                    for d in op["deps"]:
                        dop = ops[d]
                        if dop["tag"] is not None:
                            key, val, sem = ("t", dop["tag"]), 16 * dop["dma_n"], tsem[dop["tag"]]
                        else:
                            key, val, sem = ("e", dop["eng"]), dop["cnt"], esem[dop["eng"]]
                        if seen.get(key, 0) < val and waits.get(key, (0, None))[0] < val:
                            waits[key] = (val, sem)
                    for key, (val, sem) in waits.items():
                        eng.wait_ge(sem, val)
                        seen[key] = val
                    ins = op["fn"](eng)
                    if op["tag"] is not None:
                        ins.then_inc(tsem[op["tag"]], 16)
                    elif op["cnt"] is not None:
                        ins.then_inc(esem[eng_name], 1)
                last = {}
                for op in ops:
                    if op["eng"] == eng_name and op["tag"] is not None:
                        last[op["tag"]] = max(last.get(op["tag"], 0), 16 * op["dma_n"])
                for t, v in last.items():
                    if seen.get(("t", t), 0) < v:
                        eng.wait_ge(tsem[t], v)

            @block.tensor
            def _(e):
                run("pe", e)

            @block.scalar
            def _(e):
                run("act", e)

            @block.vector
            def _(e):
                run("dve", e)

            @block.gpsimd
            def _(e):
                run("pool", e)

            @block.sync
            def _(e):
                run("sp", e)


def _interleave(*gens):
    gens = list(gens)
    while gens:
        for g in list(gens):
            try:
                next(g)
            except StopIteration:
                gens.remove(g)


def _pipeline(tiles, stages, lags):
    n = len(tiles)
    mx = max(lags)
    for s in range(n + mx):
        for st, lag in zip(stages, lags):
            i = s - lag
            if 0 <= i < n:
                st(tiles[i])
        yield


def build_nc(cfg):
    nlayers = cfg.get("layers", NL)
    stages = cfg.get("stages", ("ffn1", "mix", "ffn2"))
    final_norm = cfg.get("final_norm", True)

    nc = bass.Bass("TRN2", target_bir_lowering=False)
    xT_d = nc.dram_tensor("xT", [128, NKC, S], F32, kind="ExternalInput").ap()
    outT_d = nc.dram_tensor("outT", [128, NKC, S], F32, kind="ExternalOutput").ap()
    gains_d = nc.dram_tensor("gains", [128, NGC], F32, kind="ExternalInput").ap()
    cst_d = nc.dram_tensor("cst", [128, NCST], F32, kind="ExternalInput").ap()
    bm_d = nc.dram_tensor("bm", [128, 8 * 256], F32, kind="ExternalInput").ap()
    wgu_d, wd_d, win_d, wout_d = {}, {}, {}, {}
    for l in range(NL):
        for f in (1, 2):
            wgu_d[(l, f)] = nc.dram_tensor("wgu%d_%d" % (f, l), [128, NFC * 2048], F32, kind="ExternalInput").ap()
            wd_d[(l, f)] = nc.dram_tensor("wd%d_%d" % (f, l), [128, NFC * 1024], F32, kind="ExternalInput").ap()
        win_d[l] = nc.dram_tensor("win_%d" % l, [128, WIN_TOT], F32, kind="ExternalInput").ap()
        wout_d[l] = nc.dram_tensor("wout_%d" % l, [128, 8 * 1024], F32, kind="ExternalInput").ap()

    es = ExitStack()
    with es:
        hT = es.enter_context(nc.sbuf_tensor("hT", [128, NKC, S], F32))
        nT = es.enter_context(nc.sbuf_tensor("nT", [128, NKC, S], BF16))
        WA = es.enter_context(nc.sbuf_tensor("WA", [128, 8192], BF16))
        cst = es.enter_context(nc.sbuf_tensor("cst_sb", [128, NCST], BF16))
        bm = es.enter_context(nc.sbuf_tensor("bm_sb", [128, 8, 256], BF16))
        gains = es.enter_context(nc.sbuf_tensor("gains_sb", [128, NGC], F32))
        sinkexp = es.enter_context(nc.sbuf_tensor("sinkexp", [128, NL * 4], F32))
        SCR = es.enter_context(nc.sbuf_tensor("SCR", [128, 44544], BF16))
        PS = [es.enter_context(nc.psum_tensor("ps%d" % i, [128, 512], F32)) for i in range(8)]

        def scr(off, n):
            return SCR[:, off:off + n]

        oT = scr(0, 16384).rearrange("p (c t) -> p c t", c=8)
        qbuf = scr(16384, 4096).rearrange("p (c t) -> p c t", c=2)
        kz = scr(20480, 4096).rearrange("p (c t) -> p c t", c=2)
        vpadA = scr(24576, 4096).rearrange("p (t v c) -> p t v c", t=16, v=2)
        vpadB = scr(40448, 4096).rearrange("p (t v c) -> p t v c", t=16, v=2)
        vpads = [vpadA, vpadB]
        vpad = vpadA
        spr = scr(28672, 2048).rearrange("p (s t) -> p s t", s=4)
        ssum = scr(30720, 3072).rearrange("p (s t) -> p s t", s=6)
        ebuf = scr(33792, 1536).rearrange("p (s t) -> p s t", s=3)
        wbuf = scr(35328, 2048).rearrange("p (s t) -> p s t", s=4)
        pbuf = scr(28672, 5120).rearrange("p (u h b t) -> p u h b t", u=2, h=2, b=5)
        lnden = scr(33792, 1024).bitcast(F32)
        rden = scr(34816, 1024).bitcast(F32)
        actb = scr(0, 16384).rearrange("p (g f t) -> p g f t", g=2, f=4)
        sgb = scr(16384, 4096).bitcast(F32).rearrange("p (s t) -> p s t", s=4)
        WB = scr(28672, 4096).rearrange("p (s t) -> p s t", s=4)
        sqb = scr(37376, 2048).rearrange("p (s t) -> p s t", s=4)
        lnv = scr(39424, 1024).bitcast(F32).rearrange("p (s t) -> p s t", s=1)

        ones = cst[:, C_ONES:C_ONES + 128]
        ident = cst[:, C_IDENT:C_IDENT + 128]
        negtri = cst[:, C_NEGTRI:C_NEGTRI + 128]
        negmask = cst[:, C_NEGMASK:C_NEGMASK + 128]
        spmask = cst[:, C_SPMASK:C_SPMASK + 128]
        onespad = [cst[:, C_ONESPAD + 128 * i:C_ONESPAD + 128 * (i + 1)] for i in range(2)]

        negones = cst[:, C_NEGONES:C_NEGONES + 128]
        zeros = cst[:, C_ZEROS:C_ZEROS + 128]

        Sx = Sched()
        add = Sx.add
        st = dict(sq=0, rs=0, pb=0)

        def PSK(b):
            return ("ps", b)

        add("sp", lambda e: e.dma_start(out=gains[:, :], in_=gains_d[:, :]), writes=["gains"], tag="gains")
        add("pool", lambda e: e.dma_start(out=cst[:, :], in_=cst_d[:, :], max_dma_last_dim=4096), writes=["cst"], tag="cst")
        add("pool", lambda e: e.dma_start(out=bm[:, :, :], in_=bm_d.rearrange("p (h t) -> p h t", h=8), max_dma_last_dim=4096),
            writes=["bm"], tag="bm")
        for kc in range(NKC):
            add("sp", (lambda kc: lambda e: e.dma_start(out=hT[:, kc, :], in_=xT_d[:, kc, :]))(kc),
                writes=[("h", kc, tg) for tg in range(NTG)], tag="h%d" % kc)
        for l in range(NL):
            add("act", (lambda l: lambda e: e.activation(
                out=sinkexp[:, 4 * l:4 * l + 4], in_=gains[:, l * G_PER_L + G_SINK:l * G_PER_L + G_SINK + 4], func=AF.Exp))(l),
                reads=["gains"], writes=["sinkexp%d" % l])

        add("pool", lambda e: e.memset(kz[:, :, :], 0.0), writes=["kzz"])
        add("pool", lambda e: e.memset(vpadA[:, :, :, :], 0.0), writes=["vpz"])
        add("pool", lambda e: e.memset(vpadB[:, :, :, :], 0.0), writes=["vpz2"])

        def rmsnorm(src, nk, gcol, dn, dst, order="kc"):
            for tg in range(NTG):
                bank = 4 + tg
                for kc in range(nk):
                    sl = st["sq"] % 4
                    st["sq"] += 1
                    sap, skey = src(kc, tg)
                    if kc % 8 not in (1, 4, 6):
                        add("act", (lambda sap, sl: lambda e: e.activation(out=sqb[:, sl, :], in_=sap, func=AF.Square))(sap, sl),
                            reads=[skey], writes=[("sq", sl)])
                    else:
                        add("pool", (lambda sap, sl: lambda e: e.tensor_tensor(out=sqb[:, sl, :], in0=sap, in1=sap,
                                                                              op=ALU.mult))(sap, sl),
                            reads=[skey], writes=[("sq", sl)])
                    add("pe", (lambda sl, kc, bank: lambda e: e.matmul(PS[bank][:, :], lhsT=ones, rhs=sqb[:, sl, :],
                                                                         start=(kc == 0), stop=(kc == nk - 1)))(sl, kc, bank),
                        reads=[("sq", sl), "cst"], writes=[PSK(bank)])
                add("act", (lambda bank, tg: lambda e: e.activation(out=lnv[:, 0, :], in_=PS[bank][:, :], func=AF.Ln,
                                                                    scale=1.0 / dn, bias=EPS))(bank, tg),
                    writes=[PSK(bank), ("lnv", 0)])
                add("act", (lambda bank, tg: lambda e: e.activation(out=PS[bank][:, :], in_=lnv[:, 0, :], func=AF.Exp,
                                                                    scale=-0.5))(bank, tg),
                    reads=[("lnv", 0)], writes=[PSK(bank)])
            pairs = [(kc, tg) for kc in range(nk) for tg in range(NTG)] if order == "kc" else \
                [(kc, tg) for tg in range(NTG) for kc in range(nk)]
            for kc, tg in pairs:
                if True:
                    sap, skey = src(kc, tg)
                    dap, dkey = dst(kc, tg)
                    add("dve", (lambda sap, dap, kc, tg: lambda e: e.scalar_tensor_tensor(
                        out=dap, in0=sap, scalar=gains[:, gcol + kc:gcol + kc + 1], in1=PS[4 + tg][:, :],
                        op0=ALU.mult, op1=ALU.mult))(sap, dap, kc, tg),
                        reads=[skey, "gains"], writes=[PSK(4 + tg), dkey])

        def h_src(kc, tg):
            return hT[:, kc, tg * 512:(tg + 1) * 512], ("h", kc, tg)

        def n_dst(kc, tg):
            return nT[:, kc, tg * 512:(tg + 1) * 512], ("n", kc, tg)

        def ffn(l, f, gcol):
            wgu = wgu_d[(l, f)]
            wd = wd_d[(l, f)]
            Sx.barrier()
            rmsnorm(h_src, NKC, gcol, D, n_dst)

            def wa_view(slot):
                return WA[:, slot * 2048:(slot + 1) * 2048].rearrange("p (k h c) -> p k h c", k=8, h=2)

            def dma_wgu(fc):
                slot = fc % 4
                add("pool", lambda e: e.dma_start(out=WA[:, slot * 2048:(slot + 1) * 2048],
                                                  in_=wgu[:, fc * 2048:(fc + 1) * 2048], max_dma_last_dim=4096),
                    writes=[("WA", slot)], tag="WA%d" % slot)

            def dma_wd(fc):
                slot = fc % 4
                add("pool", lambda e: e.dma_start(out=WB[:, slot, :], in_=wd[:, fc * 1024:(fc + 1) * 1024],
                                                  max_dma_last_dim=4096),
                    writes=[("WB", slot)], tag="WB%d" % slot)

            def gu_chunk(g, fi, fc):
                slot = fc % 4
                wv = wa_view(slot)
                for half in range(2):
                    for kc in range(NKC):
                        for tg in range(NTG):
                            bank = half * 4 + tg
                            add("pe", (lambda kc, tg, bank, half: lambda e: e.matmul(
                                PS[bank][:, :], lhsT=wv[:, kc, half, :], rhs=nT[:, kc, tg * 512:(tg + 1) * 512],
                                start=(kc == 0), stop=(kc == NKC - 1)))(kc, tg, bank, half),
                                reads=[("WA", slot), ("n", kc, tg)], writes=[PSK(bank)])
                    if half == 0:
                        for tg in range(NTG):
                            add("act", (lambda tg: lambda e: e.activation(out=sgb[:, tg, :], in_=PS[tg][:, :], func=AF.Silu))(tg),
                                writes=[PSK(tg), ("sg", tg)])
                    else:
                        for tg in range(NTG):
                            add("dve", (lambda tg: lambda e: e.tensor_tensor(
                                out=actb[:, g % 2, fi, tg * 512:(tg + 1) * 512], in0=PS[4 + tg][:, :], in1=sgb[:, tg, :],
                                op=ALU.mult))(tg),
                                reads=[("sg", tg)], writes=[PSK(4 + tg), ("act", g % 2, fi, tg)])

            def down(g, nf, f0):
                for dc in range(NKC):
                    for fi in range(nf):
                        for tg in range(NTG):
                            bank = (dc % 2) * 4 + tg
                            add("pe", (lambda dc, fi, tg, bank: lambda e: e.matmul(
                                PS[bank][:, :], lhsT=WB[:, (f0 + fi) % 4, dc * 128:(dc + 1) * 128],
                                rhs=actb[:, g % 2, fi, tg * 512:(tg + 1) * 512],
                                start=(fi == 0), stop=(fi == nf - 1)))(dc, fi, tg, bank),
                                reads=[("WB", (f0 + fi) % 4), ("act", g % 2, fi, tg)], writes=[PSK(bank)])
                    for tg in range(NTG):
                        bank = (dc % 2) * 4 + tg
                        add("dve", (lambda dc, tg, bank: lambda e: e.scalar_tensor_tensor(
                            out=hT[:, dc, tg * 512:(tg + 1) * 512], in0=PS[bank][:, :], scalar=0.5,
                            in1=hT[:, dc, tg * 512:(tg + 1) * 512], op0=ALU.mult, op1=ALU.add))(dc, tg, bank),
                            reads=[], writes=[PSK(bank), ("h", dc, tg)])

            for fc in range(4):
                dma_wgu(fc)
            for fc in range(4):
                dma_wd(fc)
            for g, (f0, nf) in enumerate(FGROUPS):
                for fi in range(nf):
                    fc = f0 + fi
                    gu_chunk(g, fi, fc)
                    if fc + 4 < NFC:
                        dma_wgu(fc + 4)
                    if fi == 0 and g > 0:
                        pf0, pnf = FGROUPS[g - 1]
                        down(g - 1, pnf, pf0)
                        for k in range(nf):
                            dma_wd(f0 + k)
            lf0, lnf = FGROUPS[-1]
            down(len(FGROUPS) - 1, lnf, lf0)
            Sx.barrier()

        def mixer(l):
            gb = l * G_PER_L
            win = win_d[l]
            Sx.barrier()
            rmsnorm(h_src, NKC, gb + G_MIX, D, n_dst, order="tg")

            def nextbank():
                b = st["pb"] % 8
                st["pb"] += 1
                return b

            def dma_win(step):
                slot = step % 2
                if step < 4:
                    off, n = step * WIN_SB, WIN_SB
                else:
                    off, n = 4 * WIN_SB + (step - 4) * WIN_SW, WIN_SW
                add("pool", lambda e: e.dma_start(out=WA[:, slot * 4096:slot * 4096 + n], in_=win[:, off:off + n],
                                                  max_dma_last_dim=4096),
                    writes=[("WA", 2 * slot), ("WA", 2 * slot + 1)], tag="WA%d" % (2 * slot))

            def wstep(step):
                slot = step % 2
                ncol = 384 if step < 4 else 448
                return WA[:, slot * 4096:slot * 4096 + 8 * ncol].rearrange("p (k c) -> p k c", k=8), \
                    [("WA", 2 * slot), ("WA", 2 * slot + 1)]

            def proj_fm(wv, wkeys, c0, evac):
                base = 4 * (st["pb"] % 2)
                st["pb"] += 1
                for kc in range(NKC):
                    for tg in range(NTG):
                        bank = base + tg
                        add("pe", (lambda kc, tg, bank: lambda e: e.matmul(
                            PS[bank][:, :], lhsT=wv[:, kc, c0:c0 + 128], rhs=nT[:, kc, tg * 512:(tg + 1) * 512],
                            start=(kc == 0), stop=(kc == NKC - 1)))(kc, tg, bank),
                            reads=wkeys + [("n", kc, tg)], writes=[PSK(bank)])
                for tg in range(NTG):
                    evac(tg, base + tg)

            def evac_q(ci):
                def f(tg, bank):
                    add("dve", lambda e: e.tensor_copy(out=qbuf[:, ci, tg * 512:(tg + 1) * 512], in_=PS[bank][:, :]),
                        writes=[PSK(bank), ("q", ci, tg)])
                return f

            def evac_k(tg, bank):
                add("dve", lambda e: e.tensor_scalar(out=kz[0:64, 0, tg * 512:(tg + 1) * 512], in0=PS[bank][0:64, :],
                                                     scalar1=0.125, scalar2=None, op0=ALU.mult),
                    reads=["kzz"], writes=[PSK(bank), ("kz", 0, tg)])
                add("act", lambda e: e.activation(out=kz[64:128, 1, tg * 512:(tg + 1) * 512], in_=PS[bank][64:128, :],
                                                  func=AF.Copy, scale=0.125),
                    reads=["kzz"], writes=[PSK(bank), ("kz", 1, tg)])

            def gen_proj_v(wv, wkeys, c0, ncols, vb, banks):
                vdst = vpads[vb]
                for t4 in range(4):
                    bank = banks[t4 % len(banks)]
                    for ti in range(4):
                        tt = t4 * 4 + ti
                        for kc in range(NKC):
                            add("pe", (lambda kc, tt, ti, bank: lambda e: e.matmul(
                                PS[bank][:, ti * 128:ti * 128 + ncols], lhsT=nT[:, kc, tt * 128:(tt + 1) * 128],
                                rhs=wv[:, kc, c0:c0 + ncols], start=(kc == 0), stop=(kc == NKC - 1)))(kc, tt, ti, bank),
                                reads=wkeys + [("n", kc, tt // 4)], writes=[PSK(bank)])
                            if kc % 4 == 3:
                                yield
                    psv = PS[bank][:, :].rearrange("p (t c) -> p t c", t=4)
                    src1 = psv[:, :, 64:128] if ncols == 128 else psv[:, :, 0:64]
                    add("dve", (lambda t4, psv: lambda e: e.tensor_copy(
                        out=vdst[:, t4 * 4:(t4 + 1) * 4, 0, 0:64], in_=psv[:, :, 0:64]))(t4, psv),
                        reads=["vpz", "vpz2"], writes=[PSK(bank), ("vp", vb, 0, t4)])
                    add("dve", (lambda t4, src1: lambda e: e.tensor_copy(
                        out=vdst[:, t4 * 4:(t4 + 1) * 4, 1, 64:128], in_=src1))(t4, src1),
                        reads=["vpz", "vpz2"], writes=[PSK(bank), ("vp", vb, 1, t4)])
                    yield

            def proj_v(wv, wkeys, c0, ncols, vb=0):
                banks = [nextbank(), nextbank()]
                for _ in gen_proj_v(wv, wkeys, c0, ncols, vb, banks):
                    pass

            def proj_first(wv, wkeys, qi, vb):
                vdst = vpads[vb]
                cnt = [0]

                def nb():
                    b_ = cnt[0] % 4
                    cnt[0] += 1
                    return b_

                for tg in range(NTG):
                    for c0, ev in ((0, evac_q(qi)), (128, evac_k)):
                        bank = nb()
                        for kc in range(NKC):
                            add("pe", (lambda kc, tg, bank, c0: lambda e: e.matmul(
                                PS[bank][:, :], lhsT=wv[:, kc, c0:c0 + 128], rhs=nT[:, kc, tg * 512:(tg + 1) * 512],
                                start=(kc == 0), stop=(kc == NKC - 1)))(kc, tg, bank, c0),
                                reads=wkeys + [("n", kc, tg)], writes=[PSK(bank)])
                        ev(tg, bank)
                    bank = nb()
                    for ti in range(4):
                        tt = tg * 4 + ti
                        for kc in range(NKC):
                            add("pe", (lambda kc, tt, ti, bank: lambda e: e.matmul(
                                PS[bank][:, ti * 128:ti * 128 + 128], lhsT=nT[:, kc, tt * 128:(tt + 1) * 128],
                                rhs=wv[:, kc, 256:384], start=(kc == 0), stop=(kc == NKC - 1)))(kc, tt, ti, bank),
                                reads=wkeys + [("n", kc, tg)], writes=[PSK(bank)])
                    psv = PS[bank][:, :].rearrange("p (t c) -> p t c", t=4)
                    add("dve", (lambda tg, psv: lambda e: e.tensor_copy(
                        out=vdst[:, tg * 4:(tg + 1) * 4, 0, 0:64], in_=psv[:, :, 0:64]))(tg, psv),
                        reads=["vpz", "vpz2"], writes=[PSK(bank), ("vp", vb, 0, tg)])
                    add("dve", (lambda tg, psv: lambda e: e.tensor_copy(
                        out=vdst[:, tg * 4:(tg + 1) * 4, 1, 64:128], in_=psv[:, :, 64:128]))(tg, psv),
                        reads=["vpz", "vpz2"], writes=[PSK(bank), ("vp", vb, 1, tg)])

            def gen_proj_k(wv, wkeys, c0, banks):
                for tg in range(NTG):
                    bank = banks[tg % len(banks)]
                    for kc in range(NKC):
                        add("pe", (lambda kc, tg, bank: lambda e: e.matmul(
                            PS[bank][:, :], lhsT=wv[:, kc, c0:c0 + 128], rhs=nT[:, kc, tg * 512:(tg + 1) * 512],
                            start=(kc == 0), stop=(kc == NKC - 1)))(kc, tg, bank),
                            reads=wkeys + [("n", kc, tg)], writes=[PSK(bank)])
                    evac_k(tg, bank)
                    yield

            def gen_proj_q(wv, wkeys, c0, qi, banks):
                for tg in range(NTG):
                    bank = banks[tg % len(banks)]
                    for kc in range(NKC):
                        add("pe", (lambda kc, tg, bank: lambda e: e.matmul(
                            PS[bank][:, :], lhsT=wv[:, kc, c0:c0 + 128], rhs=nT[:, kc, tg * 512:(tg + 1) * 512],
                            start=(kc == 0), stop=(kc == NKC - 1)))(kc, tg, bank),
                            reads=wkeys + [("n", kc, tg)], writes=[PSK(bank)])
                        if kc % 2 == 1:
                            yield
                    add("dve", (lambda tg, bank: lambda e: e.tensor_copy(
                        out=qbuf[:, qi, tg * 512:(tg + 1) * 512], in_=PS[bank][:, :]))(tg, bank),
                        writes=[PSK(bank), ("q", qi, tg)])
                    yield

            def sb_step(c):
                wv, wkeys = wstep(c)
                qi = c % 2
                vb = c % 2
                gk = None
                if c == 0:
                    proj_first(wv, wkeys, qi, vb)
                    if c + 2 < 6:
                        dma_win(c + 2)
                else:
                    gk = gen_proj_k(wv, wkeys, 128, [6, 7])
                    next(gk)
                tiles = []
                zring = [2, 3, 4, 5]
                for g in range(4):
                    for par in range(2):
                        u = len([1 for t in tiles if t["first"]])
                        nt = 4 * g + 4
                        for j, b in enumerate(range(nt - 1, -1, -1)):
                            k = b - 4 * g
                            i = len(tiles)
                            tiles.append(dict(b=b, g=g, par=par, cs=max(k, 0) * 128, diag=(k >= 0), i=i, j=j,
                                              first=(j == 0), last=(j == nt - 1), sset=(u % 2) * 3,
                                              zb=zring[i % 4], es=i % 3, sps=i % 4, ws=i % 4, ob=g % 2))

                def s0(t):
                    b, cs, par, zb, es_ = t["b"], t["cs"], t["par"], t["zb"], t["es"]
                    q0 = t["g"] * 512
                    add("pe", lambda e: e.matmul(PS[zb][:, cs:512], lhsT=kz[:, par, b * 128:(b + 1) * 128],
                                                 rhs=qbuf[:, qi, q0 + cs:q0 + 512], start=True, stop=False),
                        reads=[("kz", par, b // 4), ("q", qi, t["g"])], writes=[PSK(zb)])
                    add("act", lambda e: e.activation(out=ebuf[:, es_, cs:512], in_=PS[zb][:, cs:512], func=AF.Exp),
                        writes=[PSK(zb), ("e", es_)])
                    if t["first"]:
                        ss0 = t["sset"]
                        add("dve", lambda e: e.memset(ssum[:, ss0:ss0 + 3, :], 0.0),
                            writes=[("ss", ss0), ("ss", ss0 + 1), ("ss", ss0 + 2)])
                        if par == 0:
                            ob = t["ob"]
                            add("pe", lambda e: e.matmul(PS[ob][:, :], lhsT=zeros, rhs=qbuf[:, qi, q0:q0 + 512],
                                                         start=True, stop=False),
                                reads=["cst", ("q", qi, t["g"])], writes=[PSK(ob)])

                def s1(t):
                    cs, es_, sps = t["cs"], t["es"], t["sps"]
                    add("act", lambda e: e.activation(out=spr[:, sps, cs:512], in_=ebuf[:, es_, cs:512], func=AF.Ln, bias=1.0),
                        reads=[("e", es_)], writes=[("spr", sps)])
                    if t["diag"]:
                        add("dve", lambda e: e.tensor_tensor(out=spr[:, sps, cs:cs + 128], in0=spr[:, sps, cs:cs + 128],
                                                             in1=spmask, op=ALU.mult),
                            reads=["cst"], writes=[("spr", sps)])

                def s2(t):
                    cs, zb, sps, ws, j = t["cs"], t["zb"], t["sps"], t["ws"], t["j"]
                    scur = t["sset"] + (j % 3)
                    snxt = t["sset"] + ((j + 1) % 3)
                    add("pe", lambda e: e.matmul(PS[zb][:, cs:512], lhsT=negtri, rhs=spr[:, sps, cs:512],
                                                 start=False, stop=(t["first"] and not t["diag"])),
                        reads=[("spr", sps), "cst"], writes=[PSK(zb)])
                    if not t["first"]:
                        add("pe", lambda e: e.matmul(PS[zb][:, cs:512], lhsT=negones, rhs=ssum[:, scur, cs:512],
                                                     start=False, stop=(not t["diag"])),
                            reads=[("ss", scur), "cst"], writes=[PSK(zb)])
                    if t["diag"]:
                        add("pe", lambda e: e.matmul(PS[zb][:, cs:cs + 128], lhsT=ident, rhs=negmask,
                                                     start=False, stop=True),
                            reads=["cst"], writes=[PSK(zb)])
                    add("act", lambda e: e.activation(out=wbuf[:, ws, cs:512], in_=PS[zb][:, cs:512], func=AF.Exp),
                        writes=[PSK(zb), ("w", ws)])
                    if not t["last"]:
                        add("dve", lambda e: e.tensor_tensor(out=ssum[:, snxt, cs:512], in0=ssum[:, scur, cs:512],
                                                             in1=spr[:, sps, cs:512], op=ALU.add),
                            reads=[("ss", scur), ("spr", sps)], writes=[("ss", snxt)])

                def s3(t):
                    b, cs, par, ws, ob, g = t["b"], t["cs"], t["par"], t["ws"], t["ob"], t["g"]
                    fin = (par == 1 and t["last"])
                    add("pe", lambda e: e.matmul(PS[ob][:, cs:512], lhsT=vpads[vb][:, b, par, :], rhs=wbuf[:, ws, cs:512],
                                                 start=False, stop=fin),
                        reads=[("w", ws), ("vp", vb, par, b // 4)], writes=[PSK(ob)])
                    if fin:
                        add("dve", lambda e: e.tensor_copy(out=oT[:, c, g * 512:(g + 1) * 512], in_=PS[ob][:, :]),
                            writes=[PSK(ob), ("o", c, g)])

                pipe = _pipeline(tiles, [s0, s1, s2, s3], [0, 1, 2, 4])

                def side():
                    if gk is not None:
                        yield from gk
                        if c + 2 < 6:
                            dma_win(c + 2)
                    if c + 1 < 4:
                        wv2, wkeys2 = wstep(c + 1)
                        yield from gen_proj_q(wv2, wkeys2, 0, (c + 1) % 2, [6, 7])
                        yield from gen_proj_v(wv2, wkeys2, 256, 128, (c + 1) % 2, [6, 7])
                    else:
                        wv2, wkeys2 = wstep(4)
                        yield from gen_proj_q(wv2, wkeys2, 0, 0, [6, 7])
                        yield from gen_proj_v(wv2, wkeys2, 384, 64, 0, [6, 7])

                _interleave(pipe, side())

            def sw_step(j):
                step = 4 + j
                wv, wkeys = wstep(step)
                if j > 0:
                    proj_fm(wv, wkeys, 0, evac_q(0))
                proj_fm(wv, wkeys, 128, evac_q(1))
                proj_fm(wv, wkeys, 256, evac_k)
                if j > 0:
                    proj_v(wv, wkeys, 384, 64)
                if step + 2 < 6:
                    dma_win(step + 2)
                else:
                    if step == 5:
                        for hh in range(2):
                            add("pool", (lambda hh: lambda e: e.dma_start(
                                out=WA[:, hh * 4096:(hh + 1) * 4096], in_=wout_d[l][:, hh * 4096:(hh + 1) * 4096],
                                max_dma_last_dim=4096))(hh),
                                writes=[("WA", 2 * hh), ("WA", 2 * hh + 1)], tag="WA%d" % (2 * hh))

                def gen_score(ci, quad, u):
                    n0 = quad * 4
                    for par in range(2):
                        h = 2 * (2 * j + ci) + par
                        for bi, b in enumerate(range(n0 - 1, n0 + 4)):
                            if b < 0:
                                continue
                            if b == n0 - 1:
                                qlo, ncol, bmo = n0 * 128, 128, 128
                            elif b == n0 + 3:
                                qlo, ncol, bmo = b * 128, 128, 0
                            else:
                                qlo, ncol, bmo = b * 128, 256, 0
                            sbk = 3 + ((par * 5 + bi) % 3)
                            add("pe", (lambda b, qlo, ncol, sbk, par: lambda e: e.matmul(
                                PS[sbk][:, 0:ncol], lhsT=kz[:, par, b * 128:(b + 1) * 128],
                                rhs=qbuf[:, ci, qlo:qlo + ncol], start=True, stop=False))(b, qlo, ncol, sbk, par),
                                reads=[("kz", par, b // 4), ("q", ci, qlo // 512), ("q", ci, (qlo + ncol - 1) // 512)],
                                writes=[PSK(sbk)])
                            add("pe", (lambda ncol, sbk, bmo, h: lambda e: e.matmul(
                                PS[sbk][:, 0:ncol], lhsT=ident, rhs=bm[:, h, bmo:bmo + ncol],
                                start=False, stop=True))(ncol, sbk, bmo, h),
                                reads=["cst", "bm"], writes=[PSK(sbk)])
                            add("act", (lambda ncol, sbk, bmo, par, bi: lambda e: e.activation(
                                out=pbuf[:, u % 2, par, bi, bmo:bmo + ncol], in_=PS[sbk][:, 0:ncol], func=AF.Exp))(ncol, sbk, bmo, par, bi),
                                writes=[PSK(sbk), ("p", u % 2, par, bi)])
                            yield

                def gen_pv(ci, quad, u):
                    n0 = quad * 4
                    cidx = 2 * j + ci
                    ob = 1 + (u % 2)
                    db = 6 + (u % 2)
                    for which in range(2):
                        bank = ob if which == 0 else db
                        for qi in range(4):
                            n = n0 + qi
                            mms = []
                            for par in range(2):
                                if n >= 1:
                                    mms.append((par, n - 1, qi, 128))
                                mms.append((par, n, qi + 1, 0))
                            for mi, (par, kb, bi, po) in enumerate(mms):
                                lhs = vpad[:, kb, par, :] if which == 0 else onespad[par]
                                add("pe", (lambda lhs, par, bi, po, qi, mi, bank, nm: lambda e: e.matmul(
                                    PS[bank][:, qi * 128:(qi + 1) * 128], lhsT=lhs, rhs=pbuf[:, u % 2, par, bi, po:po + 128],
                                    start=(mi == 0), stop=(mi == nm - 1)))(lhs, par, bi, po, qi, mi, bank, len(mms)),
                                    reads=[("p", u % 2, par, bi), ("vp", 0, par, kb // 4), "cst"], writes=[PSK(bank)])
                            yield
                    col = 4 * l + cidx
                    add("act", lambda e: e.activation(out=lnden[:, :], in_=PS[db][:, :], func=AF.Ln,
                                                      bias=sinkexp[:, col:col + 1]),
                        reads=["sinkexp%d" % l], writes=[PSK(db), "lnden"])
                    add("act", lambda e: e.activation(out=rden[:, :], in_=lnden[:, :], func=AF.Exp, scale=-1.0),
                        reads=["lnden"], writes=["rden"])
                    add("dve", lambda e: e.tensor_tensor(out=oT[:, 4 + cidx, n0 * 128:n0 * 128 + 512], in0=PS[ob][:, :],
                                                         in1=rden[:, :], op=ALU.mult),
                        reads=["rden"], writes=[PSK(ob), ("o", 4 + cidx, quad)])
                    yield

                units = [(ci, quad) for ci in range(2) for quad in range(4)]
                prev = None
                for ui, (ci, quad) in enumerate(units):
                    u = j * 8 + ui
                    gs = gen_score(ci, quad, u)
                    if prev is None:
                        _interleave(gs)
                    else:
                        _interleave(prev, gs)
                    prev = gen_pv(ci, quad, u)
                _interleave(prev)

            dma_win(0)
            dma_win(1)
            for c in range(4):
                sb_step(c)
            def o_src(c0):
                return lambda kc, tg: (oT[:, c0 + kc, tg * 512:(tg + 1) * 512], ("o", c0 + kc, tg))

            Sx.barrier()
            sw_step(0)
            rmsnorm(o_src(0), 4, gb + G_OSB, 512, o_src(0))
            sw_step(1)
            Sx.barrier()
            rmsnorm(o_src(4), 4, gb + G_OSW, 512, o_src(4))
            wo = WA[:, :].rearrange("p (k c) -> p k c", k=8)
            for dc in range(NKC):
                for tg in range(NTG):
                    bank = nextbank()
                    for kc in range(NKC):
                        add("pe", (lambda dc, tg, kc, bank: lambda e: e.matmul(
                            PS[bank][:, :], lhsT=wo[:, kc, dc * 128:(dc + 1) * 128], rhs=oT[:, kc, tg * 512:(tg + 1) * 512],
                            start=(kc == 0), stop=(kc == NKC - 1)))(dc, tg, kc, bank),
                            reads=[("WA", kc // 2), ("o", kc, tg)], writes=[PSK(bank)])
                    add("dve", (lambda dc, tg, bank: lambda e: e.tensor_tensor(
                        out=hT[:, dc, tg * 512:(tg + 1) * 512], in0=PS[bank][:, :], in1=hT[:, dc, tg * 512:(tg + 1) * 512],
                        op=ALU.add))(dc, tg, bank),
                        writes=[PSK(bank), ("h", dc, tg)])
            Sx.barrier()

        for l in range(nlayers):
            gb = l * G_PER_L
            if "ffn1" in stages:
                ffn(l, 1, gb + G_FFN1)
            if "mix" in stages:
                mixer(l)
            if "ffn2" in stages:
                ffn(l, 2, gb + G_FFN2)
        Sx.barrier()
        if final_norm:
            rmsnorm(h_src, NKC, G_FINAL, D, h_src)
        for kc in range(NKC):
            add("sp", (lambda kc: lambda e: e.dma_start(out=outT_d[:, kc, :], in_=hT[:, kc, :]))(kc),
                reads=[("h", kc, tg) for tg in range(NTG)], tag="out%d" % kc)
        Sx.emit(nc)
    return nc


def _t5_bucket(dist):
    max_exact = 16
    d = np.maximum(dist, 1).astype(np.float32)
    large = max_exact + (np.log(d / np.float32(max_exact)) / np.float32(np.log(128 / max_exact))
                         * np.float32(32 - max_exact)).astype(np.int32)
    large = np.minimum(large, 31)
    return np.where(dist < max_exact, dist, large)


def _consts():
    c = np.zeros((128, NCST), np.float32)
    p = np.arange(128)[:, None]
    f = np.arange(128)[None, :]
    c[:, C_ONES:C_ONES + 128] = 1.0
    c[:, C_IDENT:C_IDENT + 128] = (p == f)
    c[:, C_NEGTRI:C_NEGTRI + 128] = -1.0 * (p >= f)
    c[:, C_NEGMASK:C_NEGMASK + 128] = NEG * (p >= f)
    c[:, C_SPMASK:C_SPMASK + 128] = 1.0 * (p < f)
    c[:, C_NEGONES:C_NEGONES + 128] = -1.0
    c[:, C_ONESPAD:C_ONESPAD + 64] = 1.0
    c[:, C_ONESPAD + 128 + 64:C_ONESPAD + 256] = 1.0
    return c


def _fm(v):
    return np.ascontiguousarray(v.reshape(-1, 128).T)


def _prep_shared(inp):
    sh = {}
    g = np.zeros((128, NGC), np.float32)
    for l in range(NL):
        b = l * G_PER_L
        g[:, b + G_FFN1:b + G_FFN1 + 8] = _fm(inp["norm_ffn1"][l])
        g[:, b + G_MIX:b + G_MIX + 8] = _fm(inp["norm_mix"][l])
        g[:, b + G_OSB:b + G_OSB + 4] = _fm(inp["norm_out_sb"][l])
        g[:, b + G_OSW:b + G_OSW + 4] = _fm(inp["norm_out_swa"][l])
        g[:, b + G_FFN2:b + G_FFN2 + 8] = _fm(inp["norm_ffn2"][l])
        g[:, b + G_SINK:b + G_SINK + 4] = np.repeat(inp["sinks"][l].reshape(4, 2).T, 64, axis=0)
    g[:, G_FINAL:G_FINAL + 8] = _fm(inp["norm_final"])
    sh["gains"] = g
    sh["cst"] = _consts()
    s_idx = np.arange(128)[:, None]
    a_idx = np.arange(128)[None, :]
    d_cur = a_idx - s_idx
    d_prev = 128 + a_idx - s_idx
    rb = inp["rel_bias"].astype(np.float32)
    bmm = np.zeros((128, 8, 256), np.float32)
    for h in range(8):
        cur = np.where(d_cur >= 0, rb[_t5_bucket(np.maximum(d_cur, 0)), h], np.float32(NEG))
        prev = np.where(d_prev < 128, rb[_t5_bucket(np.minimum(d_prev, 127)), h], np.float32(NEG))
        bmm[:, h, 0:128] = cur
        bmm[:, h, 128:256] = prev
    sh["bm"] = bmm.reshape(128, 8 * 256)
    for l in range(NL):
        for f, (kgu, kd) in ((1, ("w_ffn1_gu", "w_ffn1_down")), (2, ("w_ffn2_gu", "w_ffn2_down"))):
            W = inp[kgu][l]
            gu = np.stack([W[:, :DFF], W[:, DFF:]], axis=1)
            gu = gu.reshape(8, 128, 2, NFC, 128).transpose(1, 3, 0, 2, 4)
            sh["wgu%d_%d" % (f, l)] = np.ascontiguousarray(gu).reshape(128, NFC * 2048)
            Wd = inp[kd][l]
            sh["wd%d_%d" % (f, l)] = np.ascontiguousarray(Wd.reshape(NFC, 128, 1024).transpose(1, 0, 2)).reshape(128, NFC * 1024)
        Wi = inp["w_in"][l]
        parts = []
        for c in range(4):
            cols = np.concatenate([np.arange(c * 128, c * 128 + 128), 512 + np.arange(c * 128, c * 128 + 128),
                                   1024 + np.arange(c * 128, c * 128 + 128)])
            parts.append(Wi[:, cols].reshape(8, 128, 384).transpose(1, 0, 2).reshape(128, WIN_SB))
        for j in range(2):
            kc_ = 2048 + j * 64 + np.arange(64)
            cols = np.concatenate([1536 + j * 256 + np.arange(256), kc_, kc_, 2176 + j * 64 + np.arange(64)])
            parts.append(Wi[:, cols].reshape(8, 128, 448).transpose(1, 0, 2).reshape(128, WIN_SW))
        sh["win_%d" % l] = np.ascontiguousarray(np.concatenate(parts, axis=1))
        Wo = inp["w_out"][l]
        sh["wout_%d" % l] = np.ascontiguousarray(Wo.reshape(8, 128, 1024).transpose(1, 0, 2)).reshape(128, 8 * 1024)
    return sh


_CFG = dict(layers=NL, stages=("ffn1", "mix", "ffn2"), final_norm=True)


def run_cfg(inputs, cfg):
    inp = {k: np.asarray(v, dtype=np.float32) for k, v in inputs.items()}
    sh = _prep_shared(inp)
    x = inp["x"]
    B = x.shape[0]
    in_maps = []
    for b in range(B):
        m = dict(sh)
        m["xT"] = np.ascontiguousarray(x[b].T.reshape(NKC, 128, S).transpose(1, 0, 2))
        in_maps.append(m)
    nc = build_nc(cfg)
    res = run_bass_kernel_spmd(nc, in_maps, core_ids=list(range(B)))
    outs = []
    for b in range(B):
        oT_ = np.asarray(res.results[b]["outT"])
        outs.append(oT_.transpose(1, 0, 2).reshape(D, S).T)
    return np.ascontiguousarray(np.stack(outs, axis=0)).astype(np.float32)


def kernel(**inputs):
    return run_cfg(inputs, _CFG)
```
